# Optimizing a Trainium2 kernel written in Bass

```python
import math
import jax, jax.numpy as jnp
from jax import lax
import numpy as np

D_MODEL = 1024
BATCH = 8
SEQ = 4096
DEPTH = 1
DEC_BATCH = 32
DEC_SEQ = 1
PAST_LEN = 16384
PAGE_SIZE = 128

DILATED_PATTERNS = ((128, 1), (512, 4), (2048, 16))
N_GROUPS = 3
HEADS_PER_GROUP = 4
HEAD_DIM = 64
ATTN_WIDTH = N_GROUPS * HEADS_PER_GROUP * HEAD_DIM
ATTN_OUT_WIDTH = HEADS_PER_GROUP * HEAD_DIM
ATTN_BLOCK = 128
ATTN_SCALE = HEAD_DIM ** -0.5
ROPE_DIM = HEAD_DIM // 4
ROPE_THETA = 500000.0
RET_HEADS = 4
RET_QK_DIM = 128
RET_V_DIM = 256
RET_QK_WIDTH = RET_HEADS * RET_QK_DIM
RET_V_WIDTH = RET_HEADS * RET_V_DIM
RET_CHUNK = 128
RET_ROPE_THETA = 10000.0
IN_WIDTH = 3 * ATTN_WIDTH + 2 * RET_QK_WIDTH + 2 * RET_V_WIDTH + 2 * D_MODEL
D_FF = ((-(-8 * D_MODEL // 3)) + 255) // 256 * 256
DEEPNORM_ALPHA = (2 * DEPTH) ** 0.25
DEEPNORM_BETA = (8 * DEPTH) ** -0.25
LN_EPS = 1e-5
GN_EPS = 1e-6

kernel_name = 'hybrid_dilated_retention_decoder_step'


def _in_splits():
    widths = (ATTN_WIDTH,) * 3 + (RET_QK_WIDTH,) * 2 + (RET_V_WIDTH, RET_V_WIDTH, D_MODEL, D_MODEL)
    return [int(v) for v in np.cumsum(widths)[:-1]]


def _layer_norm(x, g, b):
    xf = x.astype(jnp.float32)
    mu = jnp.mean(xf, -1, keepdims=True)
    var = jnp.mean(jnp.square(xf - mu), -1, keepdims=True)
    y = (xf - mu) * lax.rsqrt(var + LN_EPS) * g.astype(jnp.float32) + b.astype(jnp.float32)
    return y.astype(x.dtype)


def _group_norm(y):
    mu = jnp.mean(y, -1, keepdims=True)
    var = jnp.mean(jnp.square(y - mu), -1, keepdims=True)
    return (y - mu) * lax.rsqrt(var + GN_EPS)


def _rope(x, pos, rot_dim, theta):
    half = rot_dim // 2
    inv = jnp.exp(-math.log(theta) * jnp.arange(half, dtype=jnp.float32) * (2.0 / rot_dim))
    ang = pos.astype(jnp.float32)[:, None] * inv[None, :]
    cos = jnp.cos(ang)[:, None, :]
    sin = jnp.sin(ang)[:, None, :]
    xf = x.astype(jnp.float32)
    x1 = xf[..., :half]
    x2 = xf[..., half:rot_dim]
    out = jnp.concatenate([x1 * cos - x2 * sin, x1 * sin + x2 * cos, xf[..., rot_dim:]], axis=-1)
    return out.astype(x.dtype)


def _attend(s, mask, v, eq):
    s = jnp.where(mask, s.astype(jnp.float32), -jnp.inf)
    m = jnp.max(s, -1, keepdims=True)
    p = jnp.exp(s - m)
    l = jnp.sum(p, -1, keepdims=True)
    o = jnp.einsum(eq, (p / l).astype(v.dtype), v)
    return o, (m + jnp.log(l))[..., 0]


def _dilated_group_prompt(q, k, v, window, dil):
    B, S, H, dh = q.shape
    n = window // dil
    L = S // dil
    bq = min(ATTN_BLOCK, L)
    nblk = -(-L // bq)
    Lp = nblk * bq

    def sub(t, front):
        t = t.reshape(B, L, dil, H, dh).transpose(0, 2, 1, 3, 4)
        return jnp.pad(t, ((0, 0), (0, 0), (front, Lp - L), (0, 0), (0, 0)))

    qs = sub(q, 0).reshape(B, dil, nblk, bq, H, dh)
    win = (jnp.arange(nblk) * bq)[:, None] + jnp.arange(n + bq)[None, :]
    kb = sub(k, n)[:, :, win]
    vb = sub(v, n)[:, :, win]
    s = jnp.einsum('brnqhd,brnkhd->brnhqk', qs, kb) * ATTN_SCALE
    i = jnp.arange(bq)[:, None]
    j = jnp.arange(n + bq)[None, :]
    dist = i + n - j
    key_pos = (jnp.arange(nblk) * bq)[:, None, None] + j[None] - n
    mask = (dist >= 0) & (dist <= n) & (key_pos >= 0)
    o, lse = _attend(s, mask[None, None, :, None], vb, 'brnhqk,brnkhd->brnqhd')
    o = o.reshape(B, dil, Lp, H, dh)[:, :, :L].transpose(0, 2, 1, 3, 4).reshape(B, S, H, dh)
    lse = lse.transpose(0, 1, 2, 4, 3).reshape(B, dil, Lp, H)[:, :, :L].transpose(0, 2, 1, 3).reshape(B, S, H)
    return o, lse


def _dilated_group_cached(q, k, v, k_buf, v_buf, window, dil):
    T = q.shape[1]
    Lg = k_buf.shape[1]
    n = window // dil
    kc = jnp.concatenate([k_buf, k], axis=1)
    vc = jnp.concatenate([v_buf, v], axis=1)
    idx = Lg + jnp.arange(T)[:, None] - dil * jnp.arange(n + 1)[None, :]
    valid = idx >= 0
    idx = jnp.maximum(idx, 0)
    kg = kc[:, idx]
    vg = vc[:, idx]
    s = jnp.einsum('bthd,btkhd->bthk', q, kg) * ATTN_SCALE
    o, lse = _attend(s, valid[None, :, None, :], vg, 'bthk,btkhd->bthd')
    return o, lse, kc[:, -Lg:], vc[:, -Lg:]


def _dilated_mixture(q, k, v, bufs):
    outs, lses, new_bufs = [], [], []
    for g, (window, dil) in enumerate(DILATED_PATTERNS):
        qg, kg, vg = q[:, :, g], k[:, :, g], v[:, :, g]
        if bufs is None:
            o, lse = _dilated_group_prompt(qg, kg, vg, window, dil)
            keep = min(window, qg.shape[1])
            new_bufs += [kg[:, -keep:], vg[:, -keep:]]
        else:
            o, lse, kn, vn = _dilated_group_cached(qg, kg, vg, bufs[2 * g], bufs[2 * g + 1], window, dil)
            new_bufs += [kn, vn]
        outs.append(o.astype(jnp.float32))
        lses.append(lse)
    w = jax.nn.softmax(jnp.stack(lses), axis=0)
    o = jnp.einsum('gbsh,gbshd->bshd', w, jnp.stack(outs)).astype(q.dtype)
    return o, new_bufs


def _retention(q, k, v, state0):
    B, S, H, dk = q.shape
    dv = v.shape[-1]
    C = math.gcd(S, RET_CHUNK)
    nc = S // C
    lg = jnp.log(1.0 - 2.0 ** (-5.0 - jnp.arange(H, dtype=jnp.float32)))
    idx = jnp.arange(C, dtype=jnp.float32)
    diff = idx[:, None] - idx[None, :]
    decay_in = jnp.where(diff >= 0, jnp.exp(lg[:, None, None] * jnp.maximum(diff, 0.0)), 0.0)
    decay_q = jnp.exp(lg[None, :] * (idx[:, None] + 1.0))
    decay_k = jnp.exp(lg[None, :] * (C - 1.0 - idx[:, None]))
    decay_c = jnp.exp(lg * C)

    def chunks(t):
        return jnp.moveaxis(t.astype(jnp.float32).reshape(B, nc, C, H, t.shape[-1]), 1, 0)

    def step(R, qkv):
        qc, kc, vc = qkv
        s = jnp.einsum('bihd,bjhd->bhij', qc, kc) * decay_in
        o = (jnp.einsum('bhij,bjhe->bihe', s, vc)
             + jnp.einsum('bihd,bhde->bihe', qc, R) * decay_q[None, :, :, None])
        R = R * decay_c[None, :, None, None] + jnp.einsum('bjhd,bjhe->bhde', kc * decay_k[None, :, :, None], vc)
        return R, o

    R, o = lax.scan(step, state0.astype(jnp.float32), (chunks(q), chunks(k), chunks(v)))
    return jnp.moveaxis(o, 0, 1).reshape(B, S, H, dv), R


def _layer(x, c, pos, att_bufs, ret_state, w_ada, b_ada, w_in, w_att_out, w_ret_out, w_o,
           ln1_g, ln1_b, w_ffn_in, w_ffn_out, ln2_g, ln2_b):
    B, S, _ = x.shape
    mod = (jax.nn.silu(c) @ w_ada + b_ada)[:, None, :]
    shift1, scale1, gate1, shift2, scale2, gate2 = jnp.split(mod, 6, axis=-1)
    h = x * (1 + scale1) + shift1
    qa, ka, va, rq, rk, rv, rg, ga, gb = jnp.split(h @ w_in, _in_splits(), axis=-1)
    n_att = N_GROUPS * HEADS_PER_GROUP
    qa = _rope(qa.reshape(B, S, n_att, HEAD_DIM), pos, ROPE_DIM, ROPE_THETA).reshape(B, S, N_GROUPS, HEADS_PER_GROUP, HEAD_DIM)
    ka = _rope(ka.reshape(B, S, n_att, HEAD_DIM), pos, ROPE_DIM, ROPE_THETA).reshape(B, S, N_GROUPS, HEADS_PER_GROUP, HEAD_DIM)
    va = va.reshape(B, S, N_GROUPS, HEADS_PER_GROUP, HEAD_DIM)
    o_att, att_state = _dilated_mixture(qa, ka, va, att_bufs)
    y_att = o_att.reshape(B, S, ATTN_OUT_WIDTH) @ w_att_out
    rq = _rope(rq.reshape(B, S, RET_HEADS, RET_QK_DIM), pos, RET_QK_DIM, RET_ROPE_THETA)
    rk = _rope(rk.reshape(B, S, RET_HEADS, RET_QK_DIM), pos, RET_QK_DIM, RET_ROPE_THETA) * (RET_QK_DIM ** -0.5)
    rv = rv.reshape(B, S, RET_HEADS, RET_V_DIM)
    o_ret, ret_state = _retention(rq, rk, rv, ret_state)
    o_ret = _group_norm(o_ret).astype(x.dtype).reshape(B, S, RET_V_WIDTH)
    y_ret = (jax.nn.silu(rg) * o_ret) @ w_ret_out
    mixed = (jax.nn.sigmoid(ga) * y_att + jax.nn.sigmoid(gb) * y_ret) @ w_o
    x = _layer_norm(DEEPNORM_ALPHA * x + gate1 * mixed, ln1_g, ln1_b)
    h2 = x * (1 + scale2) + shift2
    f_gate, f_up = jnp.split(h2 @ w_ffn_in, 2, axis=-1)
    ff = (jax.nn.silu(f_gate) * f_up) @ w_ffn_out
    x = _layer_norm(DEEPNORM_ALPHA * x + gate2 * ff, ln2_g, ln2_b)
    return x, att_state, ret_state


def setup_inputs(seed: int = 0) -> dict:
    key = jax.random.key(seed)
    ks = iter(jax.random.split(key, 40))
    D = D_MODEL
    fan = D ** -0.5
    beta = DEEPNORM_BETA

    def nrm(shape, scale=1.0):
        return jax.random.normal(next(ks), shape, jnp.float32) * scale

    inputs = {}
    inputs['x_prompt'] = nrm((BATCH, SEQ, D))
    inputs['x_sample'] = nrm((DEC_BATCH, DEC_SEQ, D))
    for window, _ in DILATED_PATTERNS:
        L = min(window, PAST_LEN)
        inputs['cache_k_w%d' % window] = nrm((DEPTH, DEC_BATCH, L, HEADS_PER_GROUP, HEAD_DIM))
        inputs['cache_v_w%d' % window] = nrm((DEPTH, DEC_BATCH, L, HEADS_PER_GROUP, HEAD_DIM))
    inputs['state_retention'] = nrm((DEPTH, DEC_BATCH, RET_HEADS, RET_QK_DIM, RET_V_DIM), 0.1)
    inputs['c_prompt'] = nrm((BATCH, D))
    inputs['c_sample'] = nrm((DEC_BATCH, D))
    inputs['w_ada'] = nrm((DEPTH, D, 6 * D), 0.5 * fan)
    inputs['b_ada'] = nrm((DEPTH, 6 * D), 0.02)
    inputs['w_in'] = jnp.concatenate([
        nrm((DEPTH, D, ATTN_WIDTH), fan),
        nrm((DEPTH, D, ATTN_WIDTH), fan),
        nrm((DEPTH, D, ATTN_WIDTH), beta * fan),
        nrm((DEPTH, D, RET_QK_WIDTH), fan),
        nrm((DEPTH, D, RET_QK_WIDTH), fan),
        nrm((DEPTH, D, RET_V_WIDTH), beta * fan),
        nrm((DEPTH, D, RET_V_WIDTH), fan),
        nrm((DEPTH, D, 2 * D), fan),
    ], axis=-1)
    inputs['w_att_out'] = nrm((DEPTH, ATTN_OUT_WIDTH, D), beta * ATTN_OUT_WIDTH ** -0.5)
    inputs['w_ret_out'] = nrm((DEPTH, RET_V_WIDTH, D), beta * RET_V_WIDTH ** -0.5)
    inputs['w_o'] = nrm((DEPTH, D, D), beta * fan)
    inputs['ln1_g'] = 1.0 + nrm((DEPTH, D), 0.02)
    inputs['ln1_b'] = nrm((DEPTH, D), 0.02)
    inputs['w_ffn_in'] = nrm((DEPTH, D, 2 * D_FF), beta * fan)
    inputs['w_ffn_out'] = nrm((DEPTH, D_FF, D), beta * D_FF ** -0.5)
    inputs['ln2_g'] = 1.0 + nrm((DEPTH, D), 0.02)
    inputs['ln2_b'] = nrm((DEPTH, D), 0.02)
    return inputs


def reference(x_prompt, x_sample, cache_k_w128, cache_v_w128, cache_k_w512, cache_v_w512,
              cache_k_w2048, cache_v_w2048, state_retention, c_prompt, c_sample,
              w_ada, b_ada, w_in, w_att_out, w_ret_out, w_o, ln1_g, ln1_b,
              w_ffn_in, w_ffn_out, ln2_g, ln2_b):
    pos_p = jnp.arange(x_prompt.shape[1], dtype=jnp.int32)
    pos_s = PAST_LEN + jnp.arange(x_sample.shape[1], dtype=jnp.int32)
    sample_caches = (cache_k_w128, cache_v_w128, cache_k_w512, cache_v_w512, cache_k_w2048, cache_v_w2048)
    x_p, x_s = x_prompt, x_sample
    p_new = [[] for _ in range(7)]
    s_new = [[] for _ in range(7)]
    for layer in range(DEPTH):
        lw = (w_ada[layer], b_ada[layer], w_in[layer], w_att_out[layer], w_ret_out[layer], w_o[layer],
              ln1_g[layer], ln1_b[layer], w_ffn_in[layer], w_ffn_out[layer], ln2_g[layer], ln2_b[layer])
        zero_state = jnp.zeros((x_p.shape[0], RET_HEADS, RET_QK_DIM, RET_V_DIM), jnp.float32)
        x_p, att_p, ret_p = _layer(x_p, c_prompt, pos_p, None, zero_state, *lw)
        x_s, att_s, ret_s = _layer(x_s, c_sample, pos_s, [cc[layer] for cc in sample_caches],
                                   state_retention[layer], *lw)
        for i, t in enumerate(att_p + [ret_p]):
            p_new[i].append(t)
        for i, t in enumerate(att_s + [ret_s]):
            s_new[i].append(t)
    p_new = [jnp.stack(t) for t in p_new]
    s_new = [jnp.stack(t) for t in s_new]
    return (x_p, x_s,
            p_new[0], p_new[1], p_new[2], p_new[3], p_new[4], p_new[5], p_new[6],
            s_new[0], s_new[1], s_new[2], s_new[3], s_new[4], s_new[5], s_new[6])
```

```python
import contextlib
import math
import numpy as np
import concourse.bass as bass
import concourse.mybir as mybir
from concourse.bass_utils import run_bass_kernel_spmd

F32 = mybir.dt.float32
BF16 = mybir.dt.bfloat16
AF = mybir.ActivationFunctionType
ALU = mybir.AluOpType

D = 1024
T = 4096
NS = 4
DFF = 2816
PAST = 16384
DILS = (1, 4, 16)
WINS = (128, 512, 2048)
ALPHA = 2.0 ** 0.25
STRICT = True

DEBUG = False
STAGE = 99
GSEL = (0, 1, 2)
EPI = 9
ATT = 9


class Ctx:
    ENG = ("pe", "act", "dve", "pool", "sp")

    def __init__(self, nc, es):
        self.nc, self.es = nc, es
        self.e = dict(pe=nc.tensor, act=nc.scalar, dve=nc.vector, pool=nc.gpsimd, sp=nc.sync)
        self.semh = {}
        self.cnt = {}
        for k in self.ENG:
            self.semh["E:" + k] = es.enter_context(nc.semaphore("sem_" + k))
            self.cnt["E:" + k] = 0
        self.seen = {k: {} for k in self.ENG}
        self.lastw = {}
        self.readers = {}
        self.psum_keys = set()
        self.nins = 0

    def _wait(self, e, dep):
        sk, v = dep
        if sk == "E:" + e:
            if e == "pe" or e == "sp" or not STRICT:
                return
        if self.seen[e].get(sk, 0) >= v:
            return
        assert v <= self.cnt[sk], ("dependency on unsignaled instruction", e, dep)
        self.e[e].wait_ge(self.semh[sk], v)
        self.seen[e][sk] = v

    def _deps(self, e, reads, writes):
        deps = set()
        for k in reads:
            if k in self.lastw:
                deps.add(self.lastw[k])
            if k in self.psum_keys:
                for sk, v in self.readers.get(k, {}).items():
                    if sk != "E:" + e:
                        deps.add((sk, v))
        for k in writes:
            if k in self.lastw:
                deps.add(self.lastw[k])
            for sk, v in self.readers.get(k, {}).items():
                deps.add((sk, v))
        for d in sorted(deps, key=lambda t: (t[0], t[1])):
            self._wait(e, d)

    def _record(self, tag, reads, writes):
        sk, v = tag
        for k in reads:
            r = self.readers.setdefault(k, {})
            if r.get(sk, 0) < v:
                r[sk] = v
        for k in writes:
            self.lastw[k] = tag
            self.readers[k] = {}

    def op(self, e, fn, reads=(), writes=(), sig=True):
        self._deps(e, reads, writes)
        ins = fn(self.e[e])
        self.nins += 1
        sk = "E:" + e
        if sig:
            self.cnt[sk] += 1
            ins.then_inc(self.semh[sk], 1)
            tag = (sk, self.cnt[sk])
        else:
            tag = (sk, self.cnt[sk] + 1)
        self._record(tag, reads, writes)
        return ins

    def dma(self, q, out, in_, reads=(), writes=(), skey=None):
        self._deps(q, reads, writes)
        grp = getattr(self, "_grp", None)
        if grp is not None:
            skey = grp[0] + "_" + q
        sk = "D:" + skey
        if sk not in self.semh:
            self.semh[sk] = self.es.enter_context(self.nc.semaphore("dsem_%d" % len(self.semh)))
            self.cnt[sk] = 0
        self.cnt[sk] += 16
        self.e[q].dma_start(out=out, in_=in_).then_inc(self.semh[sk], 16)
        self.nins += 1
        if grp is not None:
            grp[1].append((sk, tuple(reads), tuple(writes)))
        else:
            self._record((sk, self.cnt[sk]), reads, writes)

    @contextlib.contextmanager
    def group(self, skey):
        self._grp = (skey, [])
        try:
            yield
        finally:
            g = self._grp
            self._grp = None
            for sk, reads, writes in g[1]:
                self._record((sk, self.cnt[sk]), reads, writes)

    def barrier(self):
        for e in self.ENG:
            for sk, h in self.semh.items():
                v = self.cnt[sk]
                if v > 0 and sk != "E:" + e and self.seen[e].get(sk, 0) < v:
                    self.e[e].wait_ge(h, v)
                    self.seen[e][sk] = v

    def finish(self):
        for sk, h in self.semh.items():
            if sk.startswith("D:") and self.cnt[sk] > 0:
                self.e["sp"].wait_ge(h, self.cnt[sk])
        for k in ("pe", "act", "dve", "pool"):
            sk = "E:" + k
            if self.cnt[sk] > 0:
                self.e["sp"].wait_ge(self.semh[sk], self.cnt[sk])


class Rot:
    def __init__(self, n):
        self.n, self.i = n, -1

    def next(self):
        self.i = (self.i + 1) % self.n
        return self.i


def _consts():
    c = {}
    c["ident"] = np.eye(128, dtype=np.float32)
    k = np.arange(128)[:, None]
    q = np.arange(128)[None, :]
    prev = (q <= k).astype(np.float32)
    same = (q >= k).astype(np.float32)
    am = np.zeros((128, 2, 2, 128), np.float32)
    am[:, :, 0, :] = prev[:, None, :]
    am[:, :, 1, :] = same[:, None, :]
    c["amask"] = am.reshape(128, 512)
    c["cmask"] = (k <= q).astype(np.float32)
    def _inv32(theta, half, rot):
        t = (np.float32(-math.log(theta)) * np.arange(half, dtype=np.float32)).astype(np.float32)
        return np.exp((t * np.float32(2.0 / rot)).astype(np.float32)).astype(np.float32)

    def _ang32(pos, inv):
        return (np.asarray(pos, dtype=np.float32)[:, None] * inv[None, :]).astype(np.float32).astype(np.float64)

    inv = _inv32(500000.0, 8, 16)
    ra = np.zeros((3, 128, 32, 32), np.float32)
    for g, dil in enumerate(DILS):
        nb = 32 // dil
        for r in range(dil):
            for cb in range(nb):
                sidx = r * nb + cb
                pos = (128 * cb + np.arange(128)) * dil + r
                ang = _ang32(pos, inv)
                cs, sn = np.cos(ang), np.sin(ang)
                ra[g, :, sidx, 0:8] = cs
                ra[g, :, sidx, 8:16] = cs
                ra[g, :, sidx, 16:24] = -sn
                ra[g, :, sidx, 24:32] = sn
    c["ropeA"] = ra
    inv64 = _inv32(10000.0, 64, 128)
    pos = np.arange(T, dtype=np.float64)
    ang = _ang32(pos, inv64)
    rr = np.zeros((T, 256), np.float32)
    rr[:, 0:64] = np.cos(ang)
    rr[:, 64:128] = np.cos(ang)
    rr[:, 128:192] = -np.sin(ang)
    rr[:, 192:256] = np.sin(ang)
    c["ropeR"] = rr
    gam = 1.0 - 2.0 ** (-5.0 - np.arange(4, dtype=np.float64))
    i = np.arange(128, dtype=np.float64)[:, None]
    dec = np.zeros((128, 12), np.float64)
    dec[:, 0:4] = gam[None, :] ** (i + 1.0)
    dec[:, 4:8] = gam[None, :] ** (-(i + 1.0)) * (128.0 ** -0.5)
    dec[:, 8:12] = gam[None, :] ** 128.0
    c["dec"] = dec.astype(np.float32)
    sel = np.zeros((5, 128), np.float32)
    sel[4, :] = 1.0
    c["sel5"] = sel
    a = _ang32([PAST], inv)[0]
    ta = np.concatenate([np.cos(a), np.cos(a), -np.sin(a), np.sin(a)]).astype(np.float32)
    c["tabAs"] = np.repeat(ta[None, :], 4, axis=0)
    a = _ang32([PAST], inv64)[0]
    tr = np.concatenate([np.cos(a), np.cos(a), -np.sin(a), np.sin(a)]).astype(np.float32)
    c["tabRs"] = np.repeat(tr[None, :], 4, axis=0)
    oh = np.zeros((128, 4, 4, 4), np.float32)
    for s_ in range(4):
        oh[:, s_, :, s_] = 1.0
    c["oh"] = oh
    ohE = np.zeros((12, 4, 4), np.float32)
    for r_ in range(12):
        ohE[r_, :, r_ % 4] = 1.0
    c["ohE"] = ohE
    selS = np.zeros((4, 4, 128), np.float32)
    for s_ in range(4):
        selS[s_, s_, :] = 1.0
    c["selS"] = selS
    c["oh4"] = np.eye(4, dtype=np.float32)
    return c


SCONST_SHAPES = dict(tabAs=[4, 32], tabRs=[4, 256], oh=[128, 4, 4, 4], ohE=[12, 4, 4], selS=[4, 4, 128], oh4=[4, 4])


CONST_SHAPES = dict(ident=[128, 128], amask=[128, 512], cmask=[128, 128], ropeA=[3, 128, 32, 32],
                    ropeR=[T, 256], dec=[128, 12], sel5=[5, 128])


def build(debug=False):
    nc = bass.Bass("TRN2", target_bir_lowering=False)

    def din(name, shape, dt=F32):
        return nc.dram_tensor(name, list(shape), dt, kind="ExternalInput").ap()

    def dout(name, shape, dt=F32):
        return nc.dram_tensor(name, list(shape), dt, kind="ExternalOutput").ap()

    x = din("x", [T, D])
    xs = din("xs", [NS, D])
    cT = din("cT", [128, 8, 5])
    bT5 = din("bT5", [128, 48, 5])
    b5 = din("b5", [5, 6 * D])
    w_ada = din("w_ada", [D, 6 * D])
    w_in = din("w_in", [D, 7424])
    w_att_out = din("w_att_out", [256, D])
    w_ret_out = din("w_ret_out", [D, D])
    w_o = din("w_o", [D, D])
    w_ffn_in = din("w_ffn_in", [D, 2 * DFF])
    w_ffn_out = din("w_ffn_out", [DFF, D])
    lnrep = {n: din(n, [128, D]) for n in ("lng1", "lnb1", "lng2", "lnb2")}
    lng1T = din("lng1T", [128, 8])
    lnb1T = din("lnb1T", [128, 8])
    cin = {n: din("c_" + n, s) for n, s in CONST_SHAPES.items()}

    cin_s = {}
    for w in WINS:
        cin_s["ck%d" % w] = din("ck%d" % w, [NS, w, 256])
        cin_s["cv%d" % w] = din("cv%d" % w, [NS, w, 256])
    cin_s["state"] = din("state", [NS, 4, 128, 256])
    for n, shp in SCONST_SHAPES.items():
        cin_s[n] = din("c_" + n, shp)
    y_p = dout("y_p", [T, D])
    y_s = dout("y_s", [NS, D])
    sk = [dout("sk%d" % w, [NS, w, 256]) for w in WINS]
    sv = [dout("sv%d" % w, [NS, w, 256]) for w in WINS]
    s_state = dout("s_state", [NS, 4, 128, 256])
    pk = [dout("pk%d" % w, [w, 256]) for w in WINS]
    pv = [dout("pv%d" % w, [w, 256]) for w in WINS]
    p_state = dout("p_state", [4, 128, 256])
    dbg = {}
    if debug:
        dbg["oT"] = dout("dbg_oT", [128, 2, T], BF16)
        dbg["modT"] = dout("dbg_modT", [128, 48, 5])
        dbg["G"] = dout("dbg_G", [128, 2, D])
        dbg["gT"] = dout("dbg_gT", [128, 8, 512], BF16)
        dbg["mT"] = dout("dbg_mT", [128, 8, 512], BF16)

    x1_scr = nc.dram_tensor("x1_scr", [T, D], F32, kind="Internal").ap()
    h2T_scr = nc.dram_tensor("h2T_scr", [128, 8, T], BF16, kind="Internal").ap()
    oT_scr = nc.dram_tensor("oT_scr", [128, 2, T], BF16, kind="Internal").ap()
    modtm_scr = nc.dram_tensor("modtm_scr", [5, 6 * D], F32, kind="Internal").ap()
    hT_scr = nc.dram_tensor("hT_scr", [128, 8, T], BF16, kind="Internal").ap()
    WIN_bf = nc.dram_tensor("WIN_bf", [128, 8, 7424], BF16, kind="Internal").ap()
    WRO_bf = nc.dram_tensor("WRO_bf", [128, 8, D], BF16, kind="Internal").ap()
    WO_bf = nc.dram_tensor("WO_bf", [128, 8, D], BF16, kind="Internal").ap()
    WFI_bf = nc.dram_tensor("WFI_bf", [128, 8, 2 * DFF], BF16, kind="Internal").ap()
    WFO_bf = nc.dram_tensor("WFO_bf", [128, 22, D], BF16, kind="Internal").ap()

    w_in_r = w_in.rearrange("(k p) n -> p k n", p=128)
    w_ada_r = w_ada.rearrange("(k p) n -> p k n", p=128)

    es = contextlib.ExitStack()
    with es:
        cx = Ctx(nc, es)

        def sb(name, shape, dt=F32, st=None):
            return (st or es).enter_context(nc.sbuf_tensor(name, list(shape), dt))

        def ps(name, shape, dt=F32, st=None):
            cx.psum_keys.add(name)
            return (st or es).enter_context(nc.psum_tensor(name, list(shape), dt))

        ident = sb("ident", [128, 128])
        ident_b = sb("ident_b", [128, 128], BF16)
        ones_b = sb("ones_b", [128, 64], BF16)
        dec = sb("dec", [128, 12])
        modT = sb("modT", [128, 48, 5])
        ops1T = sb("ops1T", [128, 8, 5])
        a2T = sb("a2T", [128, 8])
        b2T = sb("b2T", [128, 8])
        G1 = sb("G1", [128, D])
        G2 = sb("G2", [128, D])
        with cx.group("consts"):
            cx.dma("sp", ident[:], cin["ident"], writes=["ident"])
            cx.dma("pool", ident_b[:], cin["ident"], writes=["ident_b"])
            cx.dma("sp", dec[:], cin["dec"], writes=["dec"])
        cx.op("dve", lambda e: e.memset(ones_b[:], 1.0), writes=["ones_b"])

        GAM = [1.0 - 2.0 ** (-5.0 - h) for h in range(4)]
        GC = [g_ ** 128.0 for g_ in GAM]
        LN_EPS = 1e-5
        GN_EPS = 1e-6

        with contextlib.ExitStack() as s0:
            cT_sb = sb("cT_sb", [128, 8, 5], F32, s0)
            scT = sb("scT", [128, 8, 5], F32, s0)
            bT5_sb = sb("bT5_sb", [128, 48, 5], F32, s0)
            modtm = sb("modtm", [5, 6 * D], F32, s0)
            sel5 = sb("sel5", [5, 128], F32, s0)
            g1T = sb("g1T", [128, 8], F32, s0)
            wsl = [sb("wsl0_%d" % i, [128, 8, 512], F32, s0) for i in range(3)]
            pm = ps("pm", [128, 512], F32, s0)
            pg = [ps("pg%d" % i, [128, 512], F32, s0) for i in range(2)]
            with cx.group("p0c"):
                cx.dma("sp", cT_sb[:], cT, writes=["cT_sb"])
                cx.dma("sp", bT5_sb[:], bT5, writes=["bT5"])
                cx.dma("sp", modtm[:], b5, writes=["modtm"])
                cx.dma("sp", sel5[:], cin["sel5"], writes=["sel5"])
                cx.dma("sp", g1T[:], lng1T, writes=["g1T"])
                cx.dma("sp", b2T[:], lnb1T, writes=["b2T"])
            cx.op("act", lambda e: e.activation(out=scT[:], in_=cT_sb[:], func=AF.Silu),
                  reads=["cT_sb"], writes=["scT"])
            rg = Rot(2)
            for s in range(12):
                w = wsl[s % 3]
                wk = "wsl0_%d" % (s % 3)
                cx.dma("sp", w[:], w_ada_r[:, :, s * 512:(s + 1) * 512], writes=[wk], skey=wk)
                if True:
                    bi = rg.next()
                    for k in range(8):
                        cx.op("pe", lambda e, k=k, w=w, bi=bi: e.matmul(
                            pg[bi][0:5, :], lhsT=scT[:, k, :], rhs=w[:, k, :], start=(k == 0), stop=(k == 7)),
                            reads=[wk, "scT"], writes=["pg%d" % bi], sig=(k == 7))
                    cx.op("dve", lambda e, bi=bi, s=s: e.tensor_tensor(
                        out=modtm[:, s * 512:(s + 1) * 512], in0=pg[bi][0:5, :],
                        in1=modtm[:, s * 512:(s + 1) * 512], op=ALU.add),
                        reads=["pg%d" % bi, "modtm"], writes=["modtm"])
            for c in range(48):
                cx.op("pe", lambda e, c=c: e.transpose(pm[:, c * 5:(c + 1) * 5], modtm[0:5, c * 128:(c + 1) * 128], ident[0:5, 0:5]),
                      reads=["modtm", "ident"], writes=["pm"], sig=(c == 47))
            cx.op("dve", lambda e: e.tensor_copy(out=modT[:], in_=pm[:, 0:240].rearrange("p (a b) -> p a b", b=5)),
                  reads=["pm"], writes=["modT"])
            cx.op("dve", lambda e: e.tensor_scalar(out=ops1T[:], in0=modT[:, 8:16, :], scalar1=1.0, scalar2=None,
                                                   op0=ALU.add), reads=["modT"], writes=["ops1T"])
            tmp8 = sb("tmp8", [128, 8], F32, s0)
            cx.op("dve", lambda e: e.tensor_scalar(out=tmp8[:], in0=modT[:, 32:40, 4], scalar1=1.0, scalar2=None,
                                                   op0=ALU.add), reads=["modT"], writes=["tmp8"])
            cx.op("dve", lambda e: e.tensor_tensor(out=a2T[:], in0=g1T[:], in1=tmp8[:], op=ALU.mult),
                  reads=["tmp8", "g1T"], writes=["a2T"])
            cx.op("dve", lambda e: e.tensor_tensor(out=b2T[:], in0=b2T[:], in1=tmp8[:], op=ALU.mult),
                  reads=["tmp8", "b2T"], writes=["b2T"])
            cx.op("dve", lambda e: e.tensor_tensor(out=b2T[:], in0=b2T[:], in1=modT[:, 24:32, 4], op=ALU.add),
                  reads=["modT", "b2T"], writes=["b2T"])
            for gi, Gt in enumerate((G1, G2)):
                for half in range(2):
                    bi = rg.next()
                    cx.op("pe", lambda e, gi=gi, half=half, bi=bi: e.matmul(
                        pg[bi][:, :], lhsT=sel5[:], rhs=modtm[:, (2 + 3 * gi) * D + half * 512:(2 + 3 * gi) * D + (half + 1) * 512],
                        start=True, stop=True),
                        reads=["sel5", "modtm"], writes=["pg%d" % bi])
                    cx.op("act", lambda e, Gt=Gt, half=half, bi=bi: e.activation(
                        out=Gt[:, half * 512:(half + 1) * 512], in_=pg[bi][:, :], func=AF.Copy),
                        reads=["pg%d" % bi], writes=[("G", gi, half)])
            if debug:
                cx.dma("sp", dbg["modT"], modT[:], reads=["modT"], skey="dbg0")
                cx.dma("sp", dbg["G"][:, 0, :], G1[:], reads=[("G", 0, 0), ("G", 0, 1)], skey="dbg1")
                cx.dma("sp", dbg["G"][:, 1, :], G2[:], reads=[("G", 1, 0), ("G", 1, 1)], skey="dbg2")
            cx.dma("sp", modtm_scr, modtm[:], reads=["modtm"], writes=["modtm_scr"], skey="modtm_out")
            cx.barrier()
        GK = [[("G", gi, h) for h in range(2)] for gi in range(2)]

        def make_hT(xt, xkeys, nt, hT, hkey, banks, bankkeys, rot):
            for k in range(8):
                bi = rot.next()
                for t in range(nt):
                    cx.op("pe", lambda e, t=t, k=k, bi=bi: e.transpose(
                        banks[bi][:, t * 128:(t + 1) * 128], xt[:, t, k * 128:(k + 1) * 128], ident[:]),
                        reads=[xkeys[t], "ident"], writes=[bankkeys[bi]], sig=(t == nt - 1))
                if bi == 0:
                    cx.op("act", lambda e, k=k, bi=bi: e.activation(
                        out=hT[:, k, :], in_=banks[bi][:, 0:nt * 128], func=AF.Identity,
                        scale=ops1T[:, k, 4:5], bias=modT[:, k, 4:5]),
                        reads=[bankkeys[bi], "ops1T", "modT"], writes=[hkey])
                else:
                    cx.op("dve", lambda e, k=k, bi=bi: e.tensor_scalar(
                        out=hT[:, k, :], in0=banks[bi][:, 0:nt * 128], scalar1=ops1T[:, k, 4:5],
                        scalar2=modT[:, k, 4:5], op0=ALU.mult, op1=ALU.add),
                        reads=[bankkeys[bi], "ops1T", "modT"], writes=[hkey])

        with contextlib.ExitStack() as s1:
          if STAGE >= 1:
            wA = sb("wA", [128, 8, 9, 256], BF16, s1)
            hTB = sb("hTB", [128, 8, 2048], BF16, s1)
            qT = sb("qT", [128, 3, 2048], BF16, s1)
            kT = sb("kT", [128, 3, T], BF16, s1)
            vS = sb("vS", [128, 3, 32, 128], BF16, s1)
            ropeA = sb("ropeA", [128, 3, 32, 32], F32, s1)
            amask = sb("amask", [128, 512], BF16, s1)
            xt4 = [sb("xt4_%d" % i, [128, 4, D], F32, s1) for i in range(2)]
            qkv = [sb("qkv_%d" % i, [128, 384], F32, s1) for i in range(3)]
            rA = [sb("rA_%d" % i, [128, 4, 16], F32, s1) for i in range(2)]
            rB = [sb("rB_%d" % i, [128, 4, 16], F32, s1) for i in range(2)]
            pT = [sb("pT_%d" % i, [128, 512], BF16, s1) for i in range(4)]
            rLL = sb("rLL", [128, 512], F32, s1)
            Uc = sb("Uc", [128, 512], F32, s1)
            Lc = sb("Lc", [128, 512], F32, s1)
            pending = [None]
            oTw = [sb("oTw%d" % i, [128, 512], BF16, s1) for i in range(2)]
            r_ow = Rot(2)
            PA = ps("PA", [128, 1024], F32, s1)
            PB = ps("PB", [128, 1024], F32, s1)
            PC = ps("PC", [128, 1024], F32, s1)
            pq = [PA[:, 0:512], PA[:, 512:1024]]
            pt = [PB[:, 0:512], PB[:, 512:1024]]
            pss = [PB, PC, PA]
            psk = [["pt0", "pt1"], ["PC0", "PC1"], ["pq0", "pq1"]]
            cx.psum_keys.update(["pq0", "pq1", "pt0", "pt1", "PC0", "PC1"])
            U = ps("U", [128, 512], F32, s1)
            LL = ps("LL", [128, 512], F32, s1)
            with cx.group("p1c"):
                for g in range(3):
                    cx.dma("sp", ropeA[:, g, :, :], cin["ropeA"][g], writes=["ropeA"])
                cx.dma("pool", amask[:], cin["amask"], writes=["amask"])
            with cx.group("s_roll"):
                for g, Lg in enumerate(WINS):
                    for dst_, src_ in ((sk[g], cin_s["ck%d" % Lg]), (sv[g], cin_s["cv%d" % Lg])):
                        for s_ in range(NS):
                            cx.dma("act",
                                   dst_[s_, 0:Lg - 1, :].rearrange("l d -> (l d)").rearrange("(c n) -> c n", c=16),
                                   src_[s_, 1:Lg, :].rearrange("l d -> (l d)").rearrange("(c n) -> c n", c=16))
            r_pq, r_pt, r_ps, r_x, r_qk, r_v, r_r, r_p = Rot(2), Rot(2), Rot(3), Rot(2), Rot(3), Rot(2), Rot(2), Rot(4)
            for g in range(3):
                for t in range(3):
                    c0 = t * 768 + g * 256
                    cx.dma("pool", wA[:, :, g * 3 + t, :], w_in_r[:, :, c0:c0 + 256], writes=["wA"], skey="wA")
            for hp in range(2):
                if hp == 0:
                    with cx.group("wconv"):
                        for k in range(8):
                            cx.dma("pool", WIN_bf[:, k, :], w_in[k * 128:(k + 1) * 128, :], writes=[("WS", "in")])
                        for k in range(8):
                            cx.dma("pool", WRO_bf[:, k, :], w_ret_out[k * 128:(k + 1) * 128, :], writes=[("WS", "ro")])
                            cx.dma("pool", WO_bf[:, k, :], w_o[k * 128:(k + 1) * 128, :], writes=[("WS", "o")])
                        for k in range(8):
                            cx.dma("pool", WFI_bf[:, k, :], w_ffn_in[k * 128:(k + 1) * 128, :], writes=[("WS", "fi")])
                        for k in range(22):
                            cx.dma("pool", WFO_bf[:, k, :], w_ffn_out[k * 128:(k + 1) * 128, :], writes=[("WS", "fo")])
                for B in range(2):
                    for tg in range(4):
                        tk0 = B * 2048 + tg * 512
                        if hp == 0:
                            xi = r_x.next()
                            xk = ["xt4_%d_%d" % (xi, t) for t in range(4)]
                            for t in range(4):
                                tok = tk0 + t * 128
                                cx.dma("sp", xt4[xi][:, t, :], x[tok:tok + 128, :], writes=[xk[t]], skey=xk[t])
                            make_hT(xt4[xi], xk, 4, hTB[:, :, tg * 512:(tg + 1) * 512], ("hTB", tg),
                                    pt, ["pt0", "pt1"], r_pt)
                            cx.dma("act", hT_scr[:, :, tk0:tk0 + 512], hTB[:, :, tg * 512:(tg + 1) * 512], reads=[("hTB", tg)],
                                   writes=[("hT_scr", B * 4 + tg)], skey="hTst%d" % tg)
                        else:
                            cx.dma("sp", hTB[:, :, tg * 512:(tg + 1) * 512], hT_scr[:, :, tk0:tk0 + 512],
                                   reads=[("hT_scr", B * 4 + tg)], writes=[("hTB", tg)], skey="hTld%d" % tg)
                    hkeys = [("hTB", tg) for tg in range(4)]
                    for g, dil in (enumerate(DILS) if STAGE >= 1.5 else []):
                        nbl = 16 // dil
                        nb = 32 // dil
                        for r in range(dil):
                            for cl in range(nbl):
                                cb = B * nbl + cl
                                sidx = r * nb + cb
                                qidx = r * nbl + cl
                                st = r + 128 * cl * dil
                                if g not in GSEL:
                                    continue
                                bi = r_pq.next()
                                for k in range(8):
                                    cx.op("pe", lambda e, k=k, bi=bi, st=st, dil=dil, g=g, hp=hp: e.matmul(
                                        pq[bi][:, 0:384], lhsT=hTB[:, k, st:st + 127 * dil + 1:dil],
                                        rhs=wA[:, k, g * 3:(g + 1) * 3, hp * 128:(hp + 1) * 128],
                                        start=(k == 0), stop=(k == 7)),
                                        reads=hkeys + ["wA"], writes=["pq%d" % bi], sig=(k == 7))
                                ri = r_r.next()
                                qi = r_qk.next()
                                qk = ("qkv", qi)
                                cx.op("act", lambda e, bi=bi, qi=qi: e.activation(
                                    out=qkv[qi][:, :], in_=pq[bi][:, 0:384], func=AF.Copy),
                                    reads=["pq%d" % bi], writes=[qk])
                                pqv = qkv[qi][:, 0:256].rearrange("p (a c) -> p a c", a=4)
                                tabC = ropeA[:, g, sidx:sidx + 1, 0:16].to_broadcast([128, 4, 16])
                                cx.op("dve", lambda e, ri=ri, pqv=pqv, tabC=tabC: e.tensor_tensor(
                                    out=rA[ri][:], in0=pqv[:, :, 0:16], in1=tabC, op=ALU.mult),
                                    reads=[qk, "ropeA"], writes=["rA%d" % ri])
                                for hf in range(2):
                                    tabS = ropeA[:, g, sidx:sidx + 1, 16 + 8 * hf:24 + 8 * hf].to_broadcast([128, 4, 8])
                                    cx.op("dve", lambda e, ri=ri, pqv=pqv, tabS=tabS, hf=hf: e.tensor_tensor(
                                        out=rB[ri][:, :, 8 * hf:8 * hf + 8],
                                        in0=pqv[:, :, 8 - 8 * hf:16 - 8 * hf], in1=tabS, op=ALU.mult),
                                        reads=[qk, "ropeA"], writes=["rB%d" % ri])
                                cx.op("dve", lambda e, ri=ri, pqv=pqv: e.tensor_tensor(
                                    out=pqv[:, :, 0:16], in0=rA[ri][:], in1=rB[ri][:], op=ALU.add),
                                    reads=["rA%d" % ri, "rB%d" % ri], writes=[qk])
                                cx.op("act", lambda e, qi=qi, g=g, sidx=sidx: e.activation(
                                    out=vS[:, g, sidx, :], in_=qkv[qi][:, 256:384], func=AF.Copy),
                                    reads=[qk], writes=[("vS", g, sidx)])
                                inwin = (128 * cb * dil + r) >= T - WINS[g]
                                if inwin:
                                    row0 = 128 * cb * dil + r - (T - WINS[g])
                                    cx.dma("sp", pv[g][row0:row0 + 127 * dil + 1:dil, hp * 128:(hp + 1) * 128],
                                           qkv[qi][:, 256:384], reads=[qk], skey="qkv_%d" % qi)
                                    cx.dma("sp", pk[g][row0:row0 + 127 * dil + 1:dil, hp * 128:(hp + 1) * 128],
                                           qkv[qi][:, 128:256], reads=[qk], skey="qkv_%d" % qi)
                                ti = r_pt.next()
                                for a in range(2):
                                    cx.op("pe", lambda e, a=a, ti=ti, qi=qi: e.transpose(
                                        pt[ti][:, a * 128:(a + 1) * 128], qkv[qi][:, a * 128:(a + 1) * 128], ident[:]),
                                        reads=[qk, "ident"], writes=["pt%d" % ti], sig=(a == 1))
                                if ti == 0:
                                    cx.op("act", lambda e, ti=ti, g=g, qidx=qidx: e.activation(
                                        out=qT[:, g, qidx * 128:(qidx + 1) * 128], in_=pt[ti][:, 0:128], func=AF.Copy),
                                        reads=["pt%d" % ti], writes=[("qT", g, qidx)])
                                    cx.op("act", lambda e, ti=ti, g=g, sidx=sidx: e.activation(
                                        out=kT[:, g, sidx * 128:(sidx + 1) * 128], in_=pt[ti][:, 128:256], func=AF.Copy),
                                        reads=["pt%d" % ti], writes=[("kT", g, sidx)])
                                else:
                                    cx.op("dve", lambda e, ti=ti, g=g, qidx=qidx: e.tensor_copy(
                                        out=qT[:, g, qidx * 128:(qidx + 1) * 128], in_=pt[ti][:, 0:128]),
                                        reads=["pt%d" % ti], writes=[("qT", g, qidx)])
                                    cx.op("dve", lambda e, ti=ti, g=g, sidx=sidx: e.tensor_copy(
                                        out=kT[:, g, sidx * 128:(sidx + 1) * 128], in_=pt[ti][:, 128:256]),
                                        reads=["pt%d" % ti], writes=[("kT", g, sidx)])
                    for Wl in range(4 if STAGE >= 2 else 0):
                        W = 4 * B + Wl
                        cx.op("dve", lambda e: e.memset(U[:], 0.0), writes=["U"])
                        cx.op("dve", lambda e: e.memset(LL[:], 0.0), writes=["LL"])
                        units = []
                        for j in range(4 * Wl, 4 * Wl + 4):
                            units.append((0, 0, j, 0, 128, (j % 4) * 128, 1))
                        for r in range(4):
                            units.append((1, r, Wl, 0, 128, r, 4))
                        for r in range(16):
                            units.append((2, r, 0, Wl * 32, 32, r, 16))
                        def unit_front(u):
                            (g, r, cl, q0, nq, col0, cstep) = u
                            dil = DILS[g]
                            nbl = 16 // dil
                            nb = 32 // dil
                            cb = B * nbl + cl
                            qidx = r * nbl + cl
                            kbs = [(0, cb - 1), (1, cb)] if cb >= 1 else [(1, cb)]
                            si = r_ps.next()
                            last = (kbs[-1][0], 1)
                            for (kbi, kb) in kbs:
                                sidx = r * nb + kb
                                for e2 in range(2):
                                    cx.op("pe", lambda e, g=g, sidx=sidx, e2=e2, qidx=qidx, q0=q0, nq=nq, kbi=kbi, si=si:
                                          e.matmul(pss[si][:, e2 * 512 + kbi * nq:e2 * 512 + (kbi + 1) * nq],
                                                   lhsT=kT[64 * e2:64 * e2 + 64, g, sidx * 128:(sidx + 1) * 128],
                                                   rhs=qT[64 * e2:64 * e2 + 64, g, qidx * 128 + q0:qidx * 128 + q0 + nq],
                                                   start=True, stop=True),
                                          reads=[("kT", g, sidx), ("qT", g, qidx)], writes=psk[si],
                                          sig=((kbi, e2) == last))
                            lo = 0 if len(kbs) == 2 else nq
                            hi = 2 * nq
                            pi = r_p.next()
                            pTv = pT[pi][:, 0:4 * nq].rearrange("p (a b) -> p a b", a=2)
                            cx.op("act", lambda e, si=si, pTv=pTv, lo=lo, hi=hi: e.activation(
                                out=pTv[:, :, lo:hi], in_=pss[si][:, :].rearrange("p (a b) -> p a b", a=2)[:, :, lo:hi],
                                func=AF.Exp, scale=0.125),
                                reads=psk[si], writes=["pT%d" % pi])
                            mk = amask[:, :].rearrange("p (a k b) -> p a k b", a=2, k=2)[:, :, lo // nq:2, q0:q0 + nq]
                            pT4 = pT[pi][:, 0:4 * nq].rearrange("p (a k b) -> p a k b", a=2, k=2)[:, :, lo // nq:2, :]
                            cx.op("dve", lambda e, pT4=pT4, mk=mk: e.tensor_tensor(out=pT4, in0=pT4, in1=mk, op=ALU.mult),
                                  reads=["pT%d" % pi, "amask"], writes=["pT%d" % pi])
                            return (g, r, nb, nq, col0, cstep, kbs, pi)

                        def unit_back(st):
                            (g, r, nb, nq, col0, cstep, kbs, pi) = st
                            cols = slice(col0, col0 + (nq - 1) * cstep + 1, cstep)
                            nmm = len(kbs) * 2
                            cntm = 0
                            for (kbi, kb) in kbs:
                                sidx = r * nb + kb
                                for e2 in range(2):
                                    slot = e2 * 2 + kbi
                                    cntm += 1
                                    cx.op("pe", lambda e, g=g, sidx=sidx, e2=e2, slot=slot, pi=pi, nq=nq, cols=cols: e.matmul(
                                        U[64 * e2:64 * e2 + 64, cols], lhsT=vS[:, g, sidx, 64 * e2:64 * e2 + 64],
                                        rhs=pT[pi][:, slot * nq:(slot + 1) * nq], start=False, stop=False,
                                        skip_group_check=True),
                                        reads=[("vS", g, sidx), "pT%d" % pi], writes=["U"], sig=False)
                                    cx.op("pe", lambda e, e2=e2, slot=slot, pi=pi, nq=nq, cols=cols: e.matmul(
                                        LL[64 * e2:64 * e2 + 64, cols], lhsT=ones_b[:, 0:64],
                                        rhs=pT[pi][:, slot * nq:(slot + 1) * nq], start=False, stop=False,
                                        skip_group_check=True),
                                        reads=["ones_b", "pT%d" % pi], writes=["LL"], sig=(cntm == nmm))

                        inflight = []
                        for ui, u in enumerate(units):
                            inflight.append(unit_front(u))
                            if len(inflight) > 2:
                                unit_back(inflight.pop(0))
                            if ui == 2 and pending[0] is not None:
                                pending[0]()
                                pending[0] = None
                        while inflight:
                            unit_back(inflight.pop(0))
                        cx.op("act", lambda e: e.activation(out=Uc[:], in_=U[:], func=AF.Copy), reads=["U"], writes=["Uc"])
                        cx.op("act", lambda e: e.activation(out=Lc[:], in_=LL[:], func=AF.Copy), reads=["LL"], writes=["Lc"])

                        def finalize(hp=hp, W=W):
                            cx.op("dve", lambda e: e.reciprocal(out=rLL[:], in_=Lc[:]), reads=["Lc"], writes=["rLL"])
                            oi = r_ow.next()
                            cx.op("dve", lambda e, oi=oi: e.tensor_tensor(
                                out=oTw[oi][:], in0=Uc[:], in1=rLL[:], op=ALU.mult),
                                reads=["Uc", "rLL"], writes=["oTw%d" % oi])
                            cx.dma("sp", oT_scr[:, hp, W * 512:(W + 1) * 512], oTw[oi][:], reads=["oTw%d" % oi],
                                   writes=[("oT_scr", W)], skey="oTw%d" % oi)
                            if debug:
                                cx.dma("sp", dbg["oT"][:, hp, W * 512:(W + 1) * 512], oTw[oi][:], reads=["oTw%d" % oi],
                                       skey="oTwd%d" % oi)
                        pending[0] = finalize
                    if pending[0] is not None:
                        pending[0]()
                        pending[0] = None
            cx.barrier()

        def ln_stats(src, stt, mv, rstd, key, eps):
            for hh in range(2):
                cx.op("dve", lambda e, hh=hh: e.bn_stats(out=stt[:, hh * 6:(hh + 1) * 6], in_=src[:, hh * 512:(hh + 1) * 512]),
                      reads=[key], writes=["lnst"])
            cx.op("dve", lambda e: e.bn_aggr(out=mv[:, 0:2], in_=stt[:, 0:12]), reads=["lnst"], writes=["lnmv"])
            cx.op("dve", lambda e: e.tensor_scalar(out=rstd[:, 0:1], in0=mv[:, 1:2], scalar1=eps, scalar2=None, op0=ALU.add),
                  reads=["lnmv"], writes=["lnrs"])
            cx.op("act", lambda e: e.activation(out=rstd[:, 0:1], in_=rstd[:, 0:1], func=AF.Sqrt), reads=["lnrs"], writes=["lnrs"])
            cx.op("dve", lambda e: e.reciprocal(out=rstd[:, 0:1], in_=rstd[:, 0:1]), reads=["lnrs"], writes=["lnrs"])

        with contextlib.ExitStack() as s0:
            if STAGE >= 0.5:
                modtm = sb("modtm_s", [5, 6 * D], F32, s0)
                wsl = [sb("wslS_%d" % i, [128, 8, 512], BF16, s0) for i in range(4)]
                pg = [ps("pgS%d" % i, [128, 512], F32, s0) for i in range(2)]
                rg = Rot(2)
                cx.dma("sp", modtm[:], modtm_scr, reads=["modtm_scr"], writes=["modtm"], skey="s_modtm")
                ck = [cin_s["ck%d" % w] for w in WINS]
                cv = [cin_s["cv%d" % w] for w in WINS]
                wslS = wsl
                r_ws = Rot(4)
                proj = sb("proj", [4, 7424], F32, s0)
                xs_sb = sb("xs_sb", [4, D], F32, s0)
                row = [sb("row%d" % i, [4, D], F32, s0) for i in range(6)]
                big = sb("big", [4, 2 * DFF], F32, s0)
                xT_s = sb("xT_s", [128, 22, 4], BF16, s0)
                tabAs = sb("tabAs", [4, 32], F32, s0)
                tabRs = sb("tabRs", [4, 256], F32, s0)
                oh = sb("oh", [128, 4, 4, 4], F32, s0)
                ohE = sb("ohE", [12, 4, 4], F32, s0)
                selS = sb("selS", [4, 4, 128], F32, s0)
                oh4 = sb("oh4", [4, 4], F32, s0)
                lnr = sb("lnr", [4, 4, D], F32, s0)
                Rst = sb("Rst", [128, 16, 256], F32, s0)
                Ksel = [sb("Ksel%d" % i, [128, 256], F32, s0) for i in range(4)]
                Vsel = [sb("Vsel%d" % i, [128, 256], F32, s0) for i in range(4)]
                ones_f = sb("ones_f", [128, 1], F32, s0)
                KE = sb("KE", [12, 256], F32, s0)
                VE = sb("VE", [12, 4, 65], F32, s0)
                qE = sb("qE", [12, 256], F32, s0)
                prod = [sb("prod%d" % i, [128, 256], F32, s0) for i in range(4)]
                scs = [sb("scs%d" % i, [128, 4], F32, s0) for i in range(4)]
                Pm = [sb("Pm%d" % i, [128, 4, 4], F32, s0) for i in range(4)]
                qTm = sb("qTm", [128, 4, 4, 4], F32, s0)
                qTr = sb("qTr", [128, 4, 4], F32, s0)
                kM = sb("kM", [4, 4, 512], F32, s0)
                sm4 = sb("sm4", [4, 64], F32, s0)
                PT = ps("PT", [128, 512], F32, s0)
                QB = ps("QB", [128, 512], F32, s0)
                UL = ps("UL", [128, 512], F32, s0)
                QR = ps("QR", [128, 1024], F32, s0)
                cx.psum_keys.update(["QR0", "QR1"])
                with cx.group("s_consts"):
                    cx.dma("sp", xs_sb[:], xs, writes=["xs_sb"])
                    cx.dma("sp", tabAs[:], cin_s["tabAs"], writes=["tabAs"])
                    cx.dma("sp", tabRs[:], cin_s["tabRs"], writes=["tabRs"])
                    cx.dma("sp", oh[:], cin_s["oh"], writes=["oh"])
                    cx.dma("sp", ohE[:], cin_s["ohE"], writes=["ohE"])
                    cx.dma("sp", selS[:], cin_s["selS"], writes=["selS"])
                    cx.dma("sp", oh4[:], cin_s["oh4"], writes=["oh4"])
                    for i, nme in enumerate(("lng1", "lnb1", "lng2", "lnb2")):
                        cx.dma("sp", lnr[:, i, :], lnrep[nme][0:4, :], writes=["lnr"])
                    cx.dma("sp", Rst[:], cin_s["state"].rearrange("s h p c -> p (s h) c"), writes=["Rst"])

                def s_transpose(src, n, dst_off):
                    for i in range(n):
                        cx.op("pe", lambda e, i=i: e.transpose(PT[:, i * 4:(i + 1) * 4], src[:, i * 128:(i + 1) * 128], ident[0:4, 0:4]),
                              reads=["srow", "ident"], writes=["PT"], sig=(i == n - 1))
                    cx.op("act", lambda e: e.activation(out=xT_s[:, dst_off:dst_off + n, :].rearrange("p a b -> p (a b)"),
                                                        in_=PT[:, 0:4 * n], func=AF.Copy), reads=["PT"], writes=["xT_s"])

                def s_linear(nk, wr, c0, ncols, xoff=0, wkey=None):
                    bi = rg.next()
                    for k0 in range(0, nk, 8):
                        kk = min(8, nk - k0)
                        wi = r_ws.next()
                        wk = "wslS_%d" % wi
                        if wkey is None:
                            cx.dma("pool", wslS[wi][:, 0:kk, 0:ncols], wr[:, k0:k0 + kk, c0:c0 + ncols], writes=[wk], skey=wk + "p")
                        else:
                            cx.dma("sp", wslS[wi][:, 0:kk, 0:ncols], wr[:, k0:k0 + kk, c0:c0 + ncols], reads=[("WS", wkey)],
                                   writes=[wk], skey=wk)
                        for k in range(kk):
                            cx.op("pe", lambda e, k=k, k0=k0, wi=wi, bi=bi: e.matmul(
                                pg[bi][0:4, 0:ncols], lhsT=xT_s[:, xoff + k0 + k, :], rhs=wslS[wi][:, k, 0:ncols],
                                start=(k0 + k == 0), stop=(k0 + k == nk - 1)),
                                reads=[wk, "xT_s"], writes=["pgS%d" % bi], sig=(k0 + k == nk - 1))
                    return pg[bi][0:4, 0:ncols], "pgS%d" % bi

                def s_ln(src, dst, gi):
                    for hh in range(2):
                        cx.op("dve", lambda e, hh=hh: e.bn_stats(out=sm4[:, hh * 6:(hh + 1) * 6], in_=src[:, hh * 512:(hh + 1) * 512]),
                              reads=["srow"], writes=["sm4"])
                    cx.op("dve", lambda e: e.bn_aggr(out=sm4[:, 12:14], in_=sm4[:, 0:12]), reads=["sm4"], writes=["sm4"])
                    cx.op("dve", lambda e: e.tensor_scalar(out=sm4[:, 14:15], in0=sm4[:, 13:14], scalar1=LN_EPS, scalar2=None, op0=ALU.add),
                          reads=["sm4"], writes=["sm4"])
                    cx.op("act", lambda e: e.activation(out=sm4[:, 14:15], in_=sm4[:, 14:15], func=AF.Sqrt), reads=["sm4"], writes=["sm4"])
                    cx.op("dve", lambda e: e.reciprocal(out=sm4[:, 14:15], in_=sm4[:, 14:15]), reads=["sm4"], writes=["sm4"])
                    cx.op("dve", lambda e: e.tensor_scalar(out=row[5][:], in0=src, scalar1=sm4[:, 12:13], scalar2=sm4[:, 14:15],
                                                           op0=ALU.subtract, op1=ALU.mult), reads=["srow", "sm4"], writes=["srow"])
                    cx.op("dve", lambda e: e.tensor_tensor(out=dst, in0=row[5][:], in1=lnr[:, gi, :], op=ALU.mult),
                          reads=["srow", "lnr"], writes=["srow"])
                    cx.op("dve", lambda e: e.tensor_tensor(out=dst, in0=dst, in1=lnr[:, gi + 1, :], op=ALU.add),
                          reads=["srow", "lnr"], writes=["srow"])

                SR = ["srow"]
                cx.op("dve", lambda e: e.scalar_tensor_tensor(out=row[0][:], in0=modtm[0:4, D:2 * D], scalar=1.0, in1=xs_sb[:],
                                                              op0=ALU.add, op1=ALU.mult), reads=["modtm", "xs_sb"], writes=SR)
                cx.op("dve", lambda e: e.tensor_tensor(out=row[0][:], in0=row[0][:], in1=modtm[0:4, 0:D], op=ALU.add),
                      reads=["modtm"] + SR, writes=SR)
                s_transpose(row[0], 8, 0)
                for c0 in range(0, 7424, 512):
                    ncols = min(512, 7424 - c0)
                    pp, pk_ = s_linear(8, WIN_bf, c0, ncols, wkey="in")
                    cx.op("act", lambda e, pp=pp, c0=c0, ncols=ncols: e.activation(out=proj[:, c0:c0 + ncols], in_=pp, func=AF.Copy),
                          reads=[pk_], writes=["proj"])
                qk24 = proj[:, 0:1536].rearrange("p (h c) -> p h c", c=64)
                rA_s = big[:, 0:384].rearrange("p (h c) -> p h c", c=16)
                rB_s = big[:, 384:768].rearrange("p (h c) -> p h c", c=16)
                cx.op("dve", lambda e: e.tensor_tensor(out=rA_s, in0=qk24[:, :, 0:16], in1=tabAs[:, None, 0:16].to_broadcast([4, 24, 16]),
                                                       op=ALU.mult), reads=["proj", "tabAs"], writes=["big"])
                for hf in range(2):
                    cx.op("dve", lambda e, hf=hf: e.tensor_tensor(
                        out=rB_s[:, :, 8 * hf:8 * hf + 8], in0=qk24[:, :, 8 - 8 * hf:16 - 8 * hf],
                        in1=tabAs[:, None, 16 + 8 * hf:24 + 8 * hf].to_broadcast([4, 24, 8]), op=ALU.mult),
                        reads=["proj", "tabAs"], writes=["big"])
                cx.op("dve", lambda e: e.tensor_tensor(out=qk24[:, :, 0:16], in0=rA_s, in1=rB_s, op=ALU.add), reads=["big"], writes=["proj"])
                rqk8 = proj[:, 2304:3328].rearrange("p (h c) -> p h c", c=128)
                rA_r = big[:, 0:1024].rearrange("p (h c) -> p h c", c=128)
                rB_r = big[:, 1024:2048].rearrange("p (h c) -> p h c", c=128)
                cx.op("dve", lambda e: e.tensor_tensor(out=rA_r, in0=rqk8, in1=tabRs[:, None, 0:128].to_broadcast([4, 8, 128]), op=ALU.mult),
                      reads=["proj", "tabRs"], writes=["big"])
                for hf in range(2):
                    cx.op("dve", lambda e, hf=hf: e.tensor_tensor(
                        out=rB_r[:, :, 64 * hf:64 * hf + 64], in0=rqk8[:, :, 64 - 64 * hf:128 - 64 * hf],
                        in1=tabRs[:, None, 128 + 64 * hf:192 + 64 * hf].to_broadcast([4, 8, 64]), op=ALU.mult),
                        reads=["proj", "tabRs"], writes=["big"])
                cx.op("dve", lambda e: e.tensor_tensor(out=rqk8, in0=rA_r, in1=rB_r, op=ALU.add), reads=["big"], writes=["proj"])
                cx.op("dve", lambda e: e.tensor_scalar(out=proj[:, 2816:3328], in0=proj[:, 2816:3328], scalar1=float(128.0 ** -0.5),
                                                       scalar2=None, op0=ALU.mult), reads=["proj"], writes=["proj"])
                with cx.group("s_new"):
                    for g, Lg in enumerate(WINS):
                        cx.dma("sp", sk[g][:, Lg - 1, :], proj[:, 768 + g * 256:768 + (g + 1) * 256], reads=["proj"])
                        cx.dma("sp", sv[g][:, Lg - 1, :], proj[:, 1536 + g * 256:1536 + (g + 1) * 256], reads=["proj"])
                cx.op("dve", lambda e: e.memset(UL[:], 0.0), writes=["UL"])
                cx.op("dve", lambda e: e.memset(ones_f[:], 1.0), writes=["ones_f"])
                cx.op("dve", lambda e: e.memset(VE[:], 1.0), writes=["VE"])
                r_kv = Rot(4)
                for g, Lg in enumerate(WINS):
                    dil = DILS[g]
                    cx.dma("sp", KE[g * 4:(g + 1) * 4, :], proj[:, 768 + g * 256:768 + (g + 1) * 256], reads=["proj"], writes=["KE"],
                           skey="s_KE")
                    cx.dma("sp", VE[g * 4:(g + 1) * 4, :, 0:64],
                           proj[:, 1536 + g * 256:1536 + (g + 1) * 256].rearrange("s (h d) -> s h d", d=64), reads=["VE", "proj"],
                           writes=["VE"], skey="s_VE")
                    cx.dma("sp", qE[g * 4:(g + 1) * 4, :], proj[:, g * 256:(g + 1) * 256], reads=["proj"], writes=["qE"], skey="s_qE")
                def s_front(s_, g):
                    Lg, dil = WINS[g], DILS[g]
                    ki = r_kv.next()
                    kk_, vk_ = "Ksel%d" % ki, "Vsel%d" % ki
                    cx.dma("sp", Ksel[ki][:, :], ck[g][s_, 0:Lg - dil + 1:dil, :], writes=[kk_], skey=kk_)
                    cx.dma("act", Vsel[ki][:, :], cv[g][s_, 0:Lg - dil + 1:dil, :], writes=[vk_], skey=vk_)
                    cx.op("pe", lambda e, s_=s_, g=g: e.matmul(QB[:, 0:256], lhsT=selS[:, s_, :], rhs=proj[:, g * 256:(g + 1) * 256],
                                                               start=True, stop=True), reads=["selS", "proj"], writes=["QB"])
                    cx.op("dve", lambda e, ki=ki: e.tensor_tensor(out=prod[ki][:], in0=QB[:, 0:256], in1=Ksel[ki][:], op=ALU.mult),
                          reads=["QB", kk_], writes=["prod%d" % ki])
                    cx.op("dve", lambda e, ki=ki: e.tensor_reduce(out=scs[ki][:], in_=prod[ki][:].rearrange("p (h d) -> p h d", d=64),
                                                                  axis=mybir.AxisListType.X, op=ALU.add),
                          reads=["prod%d" % ki], writes=["scs%d" % ki])
                    cx.op("act", lambda e, ki=ki: e.activation(out=scs[ki][:], in_=scs[ki][:], func=AF.Exp, scale=0.125),
                          reads=["scs%d" % ki], writes=["scs%d" % ki])
                    cx.op("dve", lambda e, ki=ki, s_=s_: e.tensor_tensor(
                        out=Pm[ki][:], in0=scs[ki][:, :, None].to_broadcast([128, 4, 4]), in1=oh[:, s_, :, :], op=ALU.mult),
                        reads=["scs%d" % ki, "oh"], writes=["Pm%d" % ki])
                    return ki

                def s_back(ki):
                    for h in range(4):
                        cx.op("pe", lambda e, h=h, ki=ki: e.matmul(
                            UL[0:4, h * 65:h * 65 + 64], lhsT=Pm[ki][:, h, :], rhs=Vsel[ki][:, h * 64:(h + 1) * 64], start=False,
                            stop=False, skip_group_check=True), reads=["Pm%d" % ki, "Vsel%d" % ki], writes=["UL"], sig=False)
                        cx.op("pe", lambda e, h=h, ki=ki: e.matmul(
                            UL[0:4, h * 65 + 64:h * 65 + 65], lhsT=Pm[ki][:, h, :], rhs=ones_f[:, 0:1], start=False,
                            stop=False, skip_group_check=True), reads=["Pm%d" % ki, "ones_f"], writes=["UL"], sig=(h == 3))

                prev_ = None
                for s_ in range(4):
                    for g in range(3):
                        cur_ = s_front(s_, g)
                        if prev_ is not None:
                            s_back(prev_)
                        prev_ = cur_
                s_back(prev_)
                cx.op("dve", lambda e: e.tensor_tensor(out=prod[0][0:12, :], in0=qE[:], in1=KE[:], op=ALU.mult),
                      reads=["qE", "KE"], writes=["prod0"])
                cx.op("dve", lambda e: e.tensor_reduce(out=scs[0][0:12, :], in_=prod[0][0:12, :].rearrange("p (h d) -> p h d", d=64),
                                                       axis=mybir.AxisListType.X, op=ALU.add), reads=["prod0"], writes=["scs0"])
                cx.op("act", lambda e: e.activation(out=scs[0][0:12, :], in_=scs[0][0:12, :], func=AF.Exp, scale=0.125),
                      reads=["scs0"], writes=["scs0"])
                cx.op("dve", lambda e: e.tensor_tensor(out=Pm[0][0:12, :, :], in0=scs[0][0:12, :, None].to_broadcast([12, 4, 4]),
                                                       in1=ohE[:], op=ALU.mult), reads=["scs0", "ohE"], writes=["Pm0"])
                for h in range(4):
                    cx.op("pe", lambda e, h=h: e.matmul(UL[0:4, h * 65:(h + 1) * 65], lhsT=Pm[0][0:12, h, :], rhs=VE[:, h, :],
                                                        start=False, stop=False, skip_group_check=True),
                          reads=["Pm0", "VE"], writes=["UL"], sig=(h == 3))
                ULv = UL[0:4, 0:260].rearrange("p (h c) -> p h c", c=65)
                cx.op("dve", lambda e: e.reciprocal(out=sm4[:, 16:20], in_=ULv[:, :, 64]), reads=["UL"], writes=["sm4"])
                cx.op("dve", lambda e: e.tensor_tensor(out=row[1][:, 0:256].rearrange("p (h d) -> p h d", d=64), in0=ULv[:, :, 0:64],
                                                       in1=sm4[:, 16:20, None].to_broadcast([4, 4, 64]), op=ALU.mult),
                      reads=["UL", "sm4"], writes=SR)
                s_transpose(row[1], 2, 8)
                rqv = proj[:, 2304:2816]
                rkv = proj[:, 2816:3328]
                for i in range(4):
                    cx.op("pe", lambda e, i=i: e.transpose(PT[:, 64 + i * 4:64 + (i + 1) * 4], rqv[:, i * 128:(i + 1) * 128], ident[0:4, 0:4]),
                          reads=["proj", "ident"], writes=["PT"], sig=(i == 3))
                cx.op("act", lambda e: e.activation(out=qTr[:].rearrange("p a b -> p (a b)"), in_=PT[:, 64:80], func=AF.Copy),
                      reads=["PT"], writes=["qTr"])
                cx.op("dve", lambda e: e.tensor_tensor(out=qTm[:], in0=qTr[:, :, None, :].to_broadcast([128, 4, 4, 4]),
                                                       in1=oh[:].rearrange("p s h c -> p h s c"), op=ALU.mult),
                      reads=["qTr", "oh"], writes=["qTm"])
                cx.op("dve", lambda e: e.memset(QR[:], 0.0), writes=["QR0", "QR1"])
                for h in range(4):
                    for s_ in range(4):
                        cx.op("pe", lambda e, h=h, s_=s_: e.matmul(
                            QR[0:4, h * 256:(h + 1) * 256], lhsT=qTm[:, h, s_, :], rhs=Rst[:, s_ * 4 + h, :], start=False, stop=False,
                            skip_group_check=True), reads=["qTm", "Rst"], writes=["QR%d" % (h // 2)], sig=(s_ == 3))
                cx.op("dve", lambda e: e.tensor_tensor(out=big[:, 0:512], in0=rqv, in1=rkv, op=ALU.mult), reads=["proj"], writes=["big"])
                cx.op("dve", lambda e: e.tensor_reduce(out=sm4[:, 20:24], in_=big[:, 0:512].rearrange("p (h d) -> p h d", d=128),
                                                       axis=mybir.AxisListType.X, op=ALU.add), reads=["big"], writes=["sm4"])
                rvv = proj[:, 3328:4352]
                for h in range(4):
                    cx.op("dve", lambda e, h=h: e.tensor_scalar(out=row[2][:, h * 256:(h + 1) * 256], in0=rvv[:, h * 256:(h + 1) * 256],
                                                                scalar1=sm4[:, 20 + h:21 + h], scalar2=None, op0=ALU.mult),
                          reads=["proj", "sm4"], writes=SR)
                    cx.op("dve", lambda e, h=h: e.scalar_tensor_tensor(
                        out=row[2][:, h * 256:(h + 1) * 256], in0=QR[0:4, h * 256:(h + 1) * 256], scalar=float(GAM[h]),
                        in1=row[2][:, h * 256:(h + 1) * 256], op0=ALU.mult, op1=ALU.add),
                        reads=["QR%d" % (h // 2)] + SR, writes=SR)
                for s_ in range(4):
                    cx.op("dve", lambda e, s_=s_: e.tensor_scalar(out=kM[:, s_, :], in0=rkv, scalar1=oh4[:, s_:s_ + 1], scalar2=None,
                                                                  op0=ALU.mult), reads=["proj", "oh4"], writes=["kM"])
                for s_ in range(4):
                    for h in range(4):
                        bk_ = (s_ * 4 + h) % 3
                        pb_, pk__ = ((QB[:, 256:512], "QB"), (pg[0][:, 0:256], "pgS0"), (pg[1][:, 0:256], "pgS1"))[bk_]
                        rk_ = ("Rst", s_ * 4 + h)
                        cx.op("pe", lambda e, s_=s_, h=h, pb_=pb_: e.matmul(pb_, lhsT=kM[:, s_, h * 128:(h + 1) * 128],
                                                                          rhs=rvv[:, h * 256:(h + 1) * 256], start=True, stop=True),
                              reads=["kM", "proj"], writes=[pk__])
                        cx.op("dve", lambda e, s_=s_, h=h, pb_=pb_: e.scalar_tensor_tensor(
                            out=Rst[:, s_ * 4 + h, :], in0=Rst[:, s_ * 4 + h, :], scalar=float(GAM[h]), in1=pb_,
                            op0=ALU.mult, op1=ALU.add), reads=[pk__, "Rst", rk_], writes=[rk_])
                cx.dma("sp", s_state.rearrange("s h p c -> p (s h) c"), Rst[:], reads=["Rst"] + [("Rst", i) for i in range(16)], skey="s_Rout")
                for h in range(4):
                    cx.op("dve", lambda e, h=h: e.bn_stats(out=sm4[:, 24 + h * 6:30 + h * 6], in_=row[2][:, h * 256:(h + 1) * 256]),
                          reads=SR, writes=["sm4"])
                    cx.op("dve", lambda e, h=h: e.bn_aggr(out=sm4[:, 48 + 2 * h:50 + 2 * h], in_=sm4[:, 24 + h * 6:30 + h * 6]),
                          reads=["sm4"], writes=["sm4"])
                    cx.op("dve", lambda e, h=h: e.tensor_scalar(out=sm4[:, 56 + h:57 + h], in0=sm4[:, 49 + 2 * h:50 + 2 * h], scalar1=GN_EPS,
                                                                scalar2=None, op0=ALU.add), reads=["sm4"], writes=["sm4"])
                cx.op("act", lambda e: e.activation(out=sm4[:, 56:60], in_=sm4[:, 56:60], func=AF.Sqrt), reads=["sm4"], writes=["sm4"])
                cx.op("dve", lambda e: e.reciprocal(out=sm4[:, 56:60], in_=sm4[:, 56:60]), reads=["sm4"], writes=["sm4"])
                for h in range(4):
                    cx.op("dve", lambda e, h=h: e.tensor_scalar(out=row[2][:, h * 256:(h + 1) * 256], in0=row[2][:, h * 256:(h + 1) * 256],
                                                                scalar1=sm4[:, 48 + 2 * h:49 + 2 * h], scalar2=sm4[:, 56 + h:57 + h],
                                                                op0=ALU.subtract, op1=ALU.mult), reads=["sm4"] + SR, writes=SR)
                cx.op("act", lambda e: e.activation(out=row[3][:], in_=proj[:, 4352:5376], func=AF.Silu), reads=["proj"], writes=SR)
                cx.op("dve", lambda e: e.tensor_tensor(out=row[2][:], in0=row[2][:], in1=row[3][:], op=ALU.mult), reads=SR, writes=SR)
                s_transpose(row[2], 8, 10)
                w_ao_r = w_att_out.rearrange("(k p) n -> p k n", p=128)
                w_ro_r2 = w_ret_out.rearrange("(k p) n -> p k n", p=128)
                w_o_r2 = w_o.rearrange("(k p) n -> p k n", p=128)
                cx.op("act", lambda e: e.activation(out=row[3][:], in_=proj[:, 5376:6400], func=AF.Sigmoid), reads=["proj"], writes=SR)
                cx.op("act", lambda e: e.activation(out=row[4][:], in_=proj[:, 6400:7424], func=AF.Sigmoid), reads=["proj"], writes=SR)
                for hh in range(2):
                    hs = slice(hh * 512, (hh + 1) * 512)
                    pp, pk_ = s_linear(2, w_ao_r, hh * 512, 512, xoff=8)
                    cx.op("dve", lambda e, pp=pp, hs=hs: e.tensor_tensor(out=row[3][:, hs], in0=pp, in1=row[3][:, hs], op=ALU.mult),
                          reads=[pk_] + SR, writes=SR)
                    pp, pk_ = s_linear(8, WRO_bf, hh * 512, 512, xoff=10, wkey="ro")
                    cx.op("dve", lambda e, pp=pp, hs=hs: e.tensor_tensor(out=row[4][:, hs], in0=pp, in1=row[4][:, hs], op=ALU.mult),
                          reads=[pk_] + SR, writes=SR)
                cx.op("dve", lambda e: e.tensor_tensor(out=row[3][:], in0=row[3][:], in1=row[4][:], op=ALU.add), reads=SR, writes=SR)
                s_transpose(row[3], 8, 0)
                for hh in range(2):
                    hs = slice(hh * 512, (hh + 1) * 512)
                    pp, pk_ = s_linear(8, WO_bf, hh * 512, 512, xoff=0, wkey="o")
                    cx.op("dve", lambda e, pp=pp, hs=hs: e.tensor_tensor(out=row[4][:, hs], in0=pp, in1=modtm[0:4, 2 * D + hh * 512:2 * D + (hh + 1) * 512],
                                                                         op=ALU.mult), reads=[pk_, "modtm"] + SR, writes=SR)
                cx.op("dve", lambda e: e.scalar_tensor_tensor(out=row[4][:], in0=xs_sb[:], scalar=float(ALPHA), in1=row[4][:],
                                                              op0=ALU.mult, op1=ALU.add), reads=["xs_sb"] + SR, writes=SR)
                s_ln(row[4][:], row[1][:], 0)
                cx.op("dve", lambda e: e.scalar_tensor_tensor(out=row[0][:], in0=modtm[0:4, 4 * D:5 * D], scalar=1.0, in1=row[1][:],
                                                              op0=ALU.add, op1=ALU.mult), reads=["modtm"] + SR, writes=SR)
                cx.op("dve", lambda e: e.tensor_tensor(out=row[0][:], in0=row[0][:], in1=modtm[0:4, 3 * D:4 * D], op=ALU.add),
                      reads=["modtm"] + SR, writes=SR)
                s_transpose(row[0], 8, 0)
                w_fi_r2 = w_ffn_in.rearrange("(k p) n -> p k n", p=128)
                w_fo_r2 = w_ffn_out.rearrange("(k p) n -> p k n", p=128)
                for c0 in range(0, 2 * DFF, 512):
                    pp, pk_ = s_linear(8, WFI_bf, c0, 512, xoff=0, wkey="fi")
                    ng = max(0, min(512, DFF - c0))
                    if ng > 0:
                        cx.op("act", lambda e, pp=pp, c0=c0, ng=ng: e.activation(out=big[:, c0:c0 + ng], in_=pp[:, 0:ng], func=AF.Silu),
                              reads=[pk_], writes=["big"])
                    if ng < 512:
                        cx.op("act", lambda e, pp=pp, c0=c0, ng=ng: e.activation(out=big[:, c0 + ng:c0 + 512], in_=pp[:, ng:512],
                                                                                 func=AF.Copy), reads=[pk_], writes=["big"])
                cx.op("dve", lambda e: e.tensor_tensor(out=big[:, 0:DFF], in0=big[:, 0:DFF], in1=big[:, DFF:2 * DFF], op=ALU.mult),
                      reads=["big"], writes=["big"])
                for i0 in range(0, 22, 8):
                    n_ = min(8, 22 - i0)
                    for i in range(n_):
                        cx.op("pe", lambda e, i=i, i0=i0: e.transpose(PT[:, i * 4:(i + 1) * 4], big[:, (i0 + i) * 128:(i0 + i + 1) * 128],
                                                                     ident[0:4, 0:4]), reads=["big", "ident"], writes=["PT"], sig=(i == n_ - 1))
                    cx.op("act", lambda e, i0=i0, n_=n_: e.activation(out=xT_s[:, i0:i0 + n_, :].rearrange("p a b -> p (a b)"),
                                                                      in_=PT[:, 0:4 * n_], func=AF.Copy), reads=["PT"], writes=["xT_s"])
                for hh in range(2):
                    hs = slice(hh * 512, (hh + 1) * 512)
                    pp, pk_ = s_linear(22, WFO_bf, hh * 512, 512, xoff=0, wkey="fo")
                    cx.op("dve", lambda e, pp=pp, hs=hs, hh=hh: e.tensor_tensor(
                        out=row[4][:, hs], in0=pp, in1=modtm[0:4, 5 * D + hh * 512:5 * D + (hh + 1) * 512], op=ALU.mult),
                        reads=[pk_, "modtm"] + SR, writes=SR)
                cx.op("dve", lambda e: e.scalar_tensor_tensor(out=row[4][:], in0=row[1][:], scalar=float(ALPHA), in1=row[4][:],
                                                              op0=ALU.mult, op1=ALU.add), reads=SR, writes=SR)
                s_ln(row[4][:], row[2][:], 2)
                cx.dma("sp", y_s, row[2][:], reads=SR, skey="s_yout")
            cx.barrier()

        with contextlib.ExitStack() as s2:
          if STAGE >= 3:
            NW = 5
            wsl = [sb("wsl%d" % i, [128, 8, 512], BF16, s2) for i in range(NW)]
            wao = sb("wao", [128, 2, D], BF16, s2)
            LG1 = sb("LG1", [128, D], F32, s2)
            LB1 = sb("LB1", [128, D], F32, s2)
            xt4s = [sb("xt4b%d" % i, [128, 4, D], F32, s2) for i in range(2)]
            hT = sb("hT", [128, 8, 512], BF16, s2)
            oTbs = [sb("oTb%d" % i, [128, 2, 512], BF16, s2) for i in range(2)]
            rqk = sb("rqk", [128, 4, 2, 512], BF16, s2)
            rvS = sb("rvS", [128, 4, D], BF16, s2)
            rgS = sb("rgS", [128, 4, D], BF16, s2)
            ropeTs = [sb("ropeT%d" % i, [128, 4, 256], F32, s2) for i in range(2)]
            tmp = [sb("tmp%d" % i, [128, D], F32, s2) for i in range(4)]
            x1o = [sb("x1o%d" % i, [128, D], F32, s2) for i in range(2)]
            qkT = [sb("qkT%d" % i, [128, 8, 128], BF16, s2) for i in range(2)]
            sm = [sb("sm%d" % i, [128, 4, 128], BF16, s2) for i in range(2)]
            cmask = sb("cmask", [128, 128], BF16, s2)
            R = sb("R", [128, 4, 256], F32, s2)
            Rb = sb("Rb", [128, 4, 256], BF16, s2)
            stt = sb("stt", [128, 24], F32, s2)
            mv = sb("mv", [128, 8], F32, s2)
            rstd = sb("rstd", [128, 4], F32, s2)
            nmr = sb("nmr", [128, 4], F32, s2)
            stt2 = sb("stt2", [128, 24], F32, s2)
            mv2 = sb("mv2", [128, 8], F32, s2)
            rstd2 = sb("rstd2", [128, 4], F32, s2)
            gtok = [sb("gtok%d" % i, [128, D], BF16, s2) for i in range(2)]
            gT = sb("gT", [128, 8, 512], BF16, s2)
            mT = sb("mT", [128, 8, 512], BF16, s2)
            sg = [sb("sg%d" % i, [128, 512], F32, s2) for i in range(2)]
            tt = [sb("tt%d" % i, [128, 512], F32, s2) for i in range(2)]
            h2t = [sb("h2t%d" % i, [128, 8, 128], BF16, s2) for i in range(2)]
            A = [ps("A%d" % i, [128, 512], F32, s2) for i in range(2)]
            S = ps("S", [128, 512], F32, s2)
            O = ps("O", [128, 1024], F32, s2)
            Dl = ps("Dl", [128, 1024], F32, s2)
            H = ps("H", [128, 1024], BF16, s2)
            cx.psum_keys.update(["O0", "O1", "D0", "D1"])
            AK = ["A0", "A1"]
            PB = [A[0][:, :], A[1][:, :], O[:, 0:512], O[:, 512:1024], Dl[:, 0:512], Dl[:, 512:1024]]
            PBK = ["A0", "A1", "O0", "O1", "D0", "D1"]
            r_a6, r_tp = Rot(6), Rot(2)
            with cx.group("p2c"):
                cx.dma("pool", cmask[:], cin["cmask"], writes=["cmask"])
                cx.dma("pool", wao[:], w_att_out.rearrange("(a p) n -> p a n", p=128), writes=["wao"])
                cx.dma("sp", LG1[:], lnrep["lng1"], writes=["LG1"])
                cx.dma("sp", LB1[:], lnrep["lnb1"], writes=["LB1"])
            cx.op("dve", lambda e: e.memset(R[:], 0.0), writes=["R"])
            cx.op("dve", lambda e: e.memset(Rb[:], 0.0), writes=["Rb"])
            r_w, r_a, r_q, r_s, r_g, r_sg, r_x1, r_h2 = Rot(NW), Rot(2), Rot(2), Rot(2), Rot(2), Rot(2), Rot(2), Rot(2)

            def load_slab(src_ap, wkey):
                wi = r_w.next()
                cx.dma("sp", wsl[wi][:], src_ap, reads=[("WS", wkey)], writes=["wsl%d" % wi], skey="wsl%d" % wi)
                return wi

            def w_in_slab(c0):
                return load_slab(WIN_bf[:, :, c0:c0 + 512], "in")

            w_ro_r = w_ret_out.rearrange("(k p) n -> p k n", p=128)
            w_o_r = w_o.rearrange("(k p) n -> p k n", p=128)
            NBLK = 8 if STAGE >= 3.9 else 1
            for b in range(NBLK):
                tok0 = b * 512
                pb = b % 2
                xt4, oTb, ropeT = xt4s[pb], oTbs[pb], ropeTs[pb]
                xk = ["xt4b%d_%d" % (pb, t) for t in range(4)]
                with cx.group("p2in%d" % pb):
                    for t in range(4):
                        cx.dma("sp", xt4[:, t, :], x[tok0 + t * 128:tok0 + (t + 1) * 128, :], writes=[xk[t]])
                        cx.dma("sp", ropeT[:, t, :], cin["ropeR"][tok0 + t * 128:tok0 + (t + 1) * 128, :],
                               writes=[("ropeT", pb, t)])
                    cx.dma("sp", oTb[:], oT_scr[:, :, tok0:tok0 + 512], reads=[("oT_scr", b)], writes=["oTb%d" % pb])
                cx.dma("sp", hT[:, :, :], hT_scr[:, :, tok0:tok0 + 512], reads=[("hT_scr", b)], writes=["hT"], skey="hTld_p2")
                slabs = [("q", 2304), ("k", 2816), ("v0", 3328), ("v1", 3840), ("g0", 4352), ("g1", 4864)]
                for (kind, c0) in slabs:
                    wi = w_in_slab(c0)
                    wk = "wsl%d" % wi
                    for t in range(4):
                        ai = r_a6.next()
                        tp_ = (r_tp.next()) * 2
                        for k in range(8):
                            cx.op("pe", lambda e, k=k, t=t, ai=ai, wi=wi: e.matmul(
                                PB[ai], lhsT=hT[:, k, t * 128:(t + 1) * 128], rhs=wsl[wi][:, k, :],
                                start=(k == 0), stop=(k == 7)),
                                reads=["hT", wk], writes=[PBK[ai]], sig=(k == 7))
                        if kind in ("q", "k"):
                            a = 0 if kind == "q" else 1
                            pv4 = PB[ai].rearrange("p (h c) -> p h c", h=4)
                            tA = tmp[tp_][:, 0:512].rearrange("p (h c) -> p h c", h=4)
                            tB = tmp[tp_ + 1][:, 0:512].rearrange("p (h c) -> p h c", h=4)
                            cx.op("dve", lambda e, pv4=pv4, tA=tA, t=t, tp_=tp_: e.tensor_tensor(
                                out=tA, in0=pv4, in1=ropeT[:, t:t + 1, 0:128].to_broadcast([128, 4, 128]), op=ALU.mult),
                                reads=[PBK[ai], ("ropeT", pb, t)], writes=["tmp%d" % tp_])
                            for hf in range(2):
                                cx.op("dve", lambda e, pv4=pv4, tB=tB, t=t, hf=hf, tp_=tp_: e.tensor_tensor(
                                    out=tB[:, :, 64 * hf:64 * hf + 64], in0=pv4[:, :, 64 - 64 * hf:128 - 64 * hf],
                                    in1=ropeT[:, t:t + 1, 128 + 64 * hf:192 + 64 * hf].to_broadcast([128, 4, 64]), op=ALU.mult),
                                    reads=[PBK[ai], ("ropeT", pb, t)], writes=["tmp%d" % (tp_ + 1)])
                            cx.op("dve", lambda e, tp_=tp_: e.tensor_tensor(out=tmp[tp_][:, 0:512], in0=tmp[tp_][:, 0:512],
                                                                    in1=tmp[tp_ + 1][:, 0:512], op=ALU.add),
                                  reads=["tmp%d" % tp_, "tmp%d" % (tp_ + 1)], writes=["tmp%d" % tp_])
                            for h in range(4):
                                cx.op("act", lambda e, h=h, a=a, t=t, tp_=tp_: e.activation(
                                    out=rqk[:, t, a, h * 128:(h + 1) * 128], in_=tmp[tp_][:, h * 128:(h + 1) * 128],
                                    func=AF.Copy, scale=dec[:, a * 4 + h:a * 4 + h + 1]),
                                    reads=["tmp%d" % tp_, "dec"], writes=[("rqk", t, a)])
                        elif kind[0] == "v":
                            j = int(kind[1])
                            cx.op("act", lambda e, t=t, j=j, ai=ai: e.activation(
                                out=rvS[:, t, j * 512:(j + 1) * 512], in_=PB[ai], func=AF.Copy),
                                reads=[PBK[ai]], writes=[("rvS", t, j)])
                        else:
                            j = int(kind[1])
                            cx.op("act", lambda e, t=t, j=j, ai=ai: e.activation(
                                out=rgS[:, t, j * 512:(j + 1) * 512], in_=PB[ai], func=AF.Silu),
                                reads=[PBK[ai]], writes=[("rgS", t, j)])
                def stage_A1(t):
                    for a in range(2):
                        for h in range(4):
                            cx.op("pe", lambda e, a=a, h=h, t=t: e.transpose(
                                H[:, (a * 4 + h) * 128:(a * 4 + h + 1) * 128], rqk[:, t, a, h * 128:(h + 1) * 128], ident_b[:]),
                                reads=[("rqk", t, a), "ident_b"], writes=["H"], sig=(a == 1 and h == 3))
                    qi = r_q.next()
                    qkk = "qkT%d" % qi
                    cx.op("act", lambda e, qi=qi: e.activation(out=qkT[qi][:].rearrange("p a b -> p (a b)"), in_=H[:, :], func=AF.Copy),
                          reads=["H"], writes=[qkk])
                    for h in range(4):
                        cx.op("pe", lambda e, h=h, qi=qi: e.matmul(
                            S[:, h * 128:(h + 1) * 128], lhsT=qkT[qi][:, 4 + h, :], rhs=qkT[qi][:, h, :], start=True, stop=True),
                            reads=[qkk], writes=["S"], sig=(h == 3))
                    si = r_s.next()
                    cx.op("dve", lambda e, si=si: e.tensor_tensor(
                        out=sm[si][:], in0=S[:, :].rearrange("p (h c) -> p h c", h=4),
                        in1=cmask[:, None, :].to_broadcast([128, 4, 128]), op=ALU.mult),
                        reads=["S", "cmask"], writes=["sm%d" % si])
                    return (qi, si)

                def stage_A2(t, st):
                    gt = 4 * b + t
                    qi, si = st
                    qkk = "qkT%d" % qi
                    for h in range(4):
                        ok = "O%d" % (h // 2)
                        cx.op("pe", lambda e, h=h, si=si, t=t: e.matmul(
                            O[:, h * 256:(h + 1) * 256], lhsT=sm[si][:, h, :], rhs=rvS[:, t, h * 256:(h + 1) * 256],
                            start=True, stop=(gt == 0)),
                            reads=["sm%d" % si, ("rvS", t, 0), ("rvS", t, 1)], writes=[ok], sig=(gt == 0 and h % 2 == 1))
                        if gt > 0:
                            cx.op("pe", lambda e, h=h, qi=qi: e.matmul(
                                O[:, h * 256:(h + 1) * 256], lhsT=qkT[qi][:, h, :], rhs=Rb[:, h, :], start=False, stop=True),
                                reads=[qkk, "Rb"], writes=[ok], sig=(h % 2 == 1))
                    for h in range(4):
                        dk_ = "D%d" % (h // 2)
                        cx.op("pe", lambda e, h=h, t=t: e.matmul(
                            Dl[:, h * 256:(h + 1) * 256], lhsT=rqk[:, t, 1, h * 128:(h + 1) * 128],
                            rhs=rvS[:, t, h * 256:(h + 1) * 256], start=True, stop=True),
                            reads=[("rqk", t, 1), ("rvS", t, 0), ("rvS", t, 1)], writes=[dk_], sig=(h % 2 == 1))
                    ob = 1 + (t % 3)
                    cx.op("act", lambda e, ob=ob: e.activation(out=tmp[ob][:], in_=O[:, :], func=AF.Copy),
                          reads=["O0", "O1"], writes=["tmp%d" % ob])
                    cx.op("dve", lambda e: e.tensor_tensor(out=R[:].rearrange("p h c -> p (h c)"),
                                                           in0=R[:].rearrange("p h c -> p (h c)"), in1=Dl[:, :], op=ALU.add),
                          reads=["D0", "D1", "R"], writes=["R"])
                    for h in range(4):
                        cx.op("dve", lambda e, h=h: e.tensor_scalar(out=R[:, h, :], in0=R[:, h, :], scalar1=float(GC[h]),
                                                                    scalar2=None, op0=ALU.mult),
                              reads=["R"], writes=["R"])
                    cx.op("act", lambda e: e.activation(out=Rb[:].rearrange("p h c -> p (h c)"),
                                                        in_=R[:].rearrange("p h c -> p (h c)"), func=AF.Copy),
                          reads=["R"], writes=["Rb"])

                def stage_B(t):
                    ob = 1 + (t % 3)
                    okk = "tmp%d" % ob
                    for h in range(4):
                        cx.op("dve", lambda e, h=h, ob=ob: e.bn_stats(out=stt[:, h * 6:(h + 1) * 6], in_=tmp[ob][:, h * 256:(h + 1) * 256]),
                              reads=[okk], writes=["stt"])
                    for h in range(4):
                        cx.op("dve", lambda e, h=h: e.bn_aggr(out=mv[:, 2 * h:2 * h + 2], in_=stt[:, h * 6:(h + 1) * 6]),
                              reads=["stt"], writes=["mv"])
                    cx.op("dve", lambda e: e.tensor_scalar(out=rstd[:, :], in0=mv[:, :].rearrange("p (h c) -> p h c", c=2)[:, :, 1],
                                                           scalar1=GN_EPS, scalar2=None, op0=ALU.add),
                          reads=["mv"], writes=["rstd"])
                    cx.op("act", lambda e: e.activation(out=rstd[:, :], in_=rstd[:, :], func=AF.Sqrt), reads=["rstd"], writes=["rstd"])
                    cx.op("dve", lambda e: e.reciprocal(out=rstd[:, :], in_=rstd[:, :]), reads=["rstd"], writes=["rstd"])
                    for h in range(4):
                        cx.op("dve", lambda e, h=h, ob=ob: e.tensor_scalar(
                            out=tmp[ob][:, h * 256:(h + 1) * 256], in0=tmp[ob][:, h * 256:(h + 1) * 256],
                            scalar1=mv[:, 2 * h:2 * h + 1], scalar2=rstd[:, h:h + 1], op0=ALU.subtract, op1=ALU.mult),
                            reads=[okk, "mv", "rstd"], writes=[okk])
                    gi = r_g.next()
                    cx.op("dve", lambda e, gi=gi, t=t, ob=ob: e.tensor_tensor(out=gtok[gi][:], in0=tmp[ob][:], in1=rgS[:, t, :], op=ALU.mult),
                          reads=[okk, ("rgS", t, 0), ("rgS", t, 1)], writes=["gtok%d" % gi])
                    for c in range(8):
                        cx.op("pe", lambda e, c=c, gi=gi: e.transpose(
                            H[:, c * 128:(c + 1) * 128], gtok[gi][:, c * 128:(c + 1) * 128], ident_b[:]),
                            reads=["gtok%d" % gi, "ident_b"], writes=["H"], sig=(c == 7))
                    cx.op("act", lambda e, t=t: e.activation(
                        out=gT[:, :, t * 128:(t + 1) * 128], in_=H[:, :].rearrange("p (a b) -> p a b", a=8), func=AF.Copy),
                        reads=["H"], writes=[("gT", t)])

                st_ = stage_A1(0)
                for t in range(4):
                    nxt = stage_A1(t + 1) if t < 3 else None
                    stage_A2(t, st_)
                    if t > 1:
                        stage_B(t - 2)
                    st_ = nxt
                stage_B(2)
                stage_B(3)
                if b == NBLK - 1:
                    cx.dma("pool", p_state.rearrange("h p c -> p h c"), R[:], reads=["R"], skey="Rout")
                gTk = [("gT", t) for t in range(4)]
                for half in range(2):
                    wga = w_in_slab(5376 + half * 512)
                    wgb = w_in_slab(6400 + half * 512)
                    wro = load_slab(WRO_bf[:, :, half * 512:(half + 1) * 512], "ro")
                    for ncc in range(4):
                        n = half * 4 + ncc
                        cs_ = slice(ncc * 128, (ncc + 1) * 128)
                        for k in range(8):
                            cx.op("pe", lambda e, k=k, cs_=cs_, wga=wga: e.matmul(
                                A[1][:, :], lhsT=wsl[wga][:, k, cs_], rhs=hT[:, k, :], start=(k == 0), stop=(k == 7)),
                                reads=["wsl%d" % wga, "hT"], writes=["A1"], sig=(k == 7))
                        for k in range(8):
                            cx.op("pe", lambda e, k=k, cs_=cs_, wgb=wgb: e.matmul(
                                O[:, 512:1024], lhsT=wsl[wgb][:, k, cs_], rhs=hT[:, k, :], start=(k == 0), stop=(k == 7)),
                                reads=["wsl%d" % wgb, "hT"], writes=["O1"], sig=(k == 7))
                        for hp in range(2):
                            cx.op("pe", lambda e, hp=hp, n=n: e.matmul(
                                A[0][:, :], lhsT=wao[:, hp, n * 128:(n + 1) * 128], rhs=oTb[:, hp, :],
                                start=(hp == 0), stop=(hp == 1)),
                                reads=["wao", "oTb%d" % pb], writes=["A0"], sig=(hp == 1))
                        for k in range(8):
                            cx.op("pe", lambda e, k=k, cs_=cs_, wro=wro: e.matmul(
                                O[:, 0:512], lhsT=wsl[wro][:, k, cs_], rhs=gT[:, k, :], start=(k == 0), stop=(k == 7)),
                                reads=["wsl%d" % wro] + gTk, writes=["O0"], sig=(k == 7))
                        s0i, s1i = r_sg.next(), r_sg.next()
                        cx.op("act", lambda e, s0i=s0i: e.activation(out=sg[s0i][:], in_=A[1][:, :], func=AF.Sigmoid),
                              reads=["A1"], writes=["sg%d" % s0i])
                        cx.op("act", lambda e, s1i=s1i: e.activation(out=sg[s1i][:], in_=O[:, 512:1024], func=AF.Sigmoid),
                              reads=["O1"], writes=["sg%d" % s1i])
                        cx.op("dve", lambda e, s0i=s0i: e.tensor_tensor(out=tt[0][:], in0=A[0][:, :], in1=sg[s0i][:], op=ALU.mult),
                              reads=["A0", "sg%d" % s0i], writes=["tt0"])
                        cx.op("dve", lambda e, s1i=s1i: e.tensor_tensor(out=tt[1][:], in0=O[:, 0:512], in1=sg[s1i][:], op=ALU.mult),
                              reads=["O0", "sg%d" % s1i], writes=["tt1"])
                        cx.op("dve", lambda e, n=n: e.tensor_tensor(out=mT[:, n, :], in0=tt[0][:], in1=tt[1][:], op=ALU.add),
                              reads=["tt0", "tt1"], writes=[("mT", n)])
                mTk = [("mT", n) for n in range(8)]
                wo = [load_slab(WO_bf[:, :, hh * 512:(hh + 1) * 512], "o") for hh in range(2)]
                def o_mm(t):
                    for hh in range(2):
                        for c in range(8):
                            cx.op("pe", lambda e, c=c, hh=hh, t=t: e.matmul(
                                Dl[:, hh * 512:(hh + 1) * 512], lhsT=mT[:, c, t * 128:(t + 1) * 128], rhs=wsl[wo[hh]][:, c, :],
                                start=(c == 0), stop=(c == 7)),
                                reads=mTk + ["wsl%d" % wo[hh]], writes=["D%d" % hh], sig=(c == 7))

                o_mm(0)
                for t in range(4):
                    tok = tok0 + t * 128
                    ia, ic = (t % 2) * 2, (t % 2) * 2 + 1
                    ka, kc = "tmp%d" % ia, "tmp%d" % ic
                    cx.op("dve", lambda e, ia=ia: e.tensor_tensor(out=tmp[ia][:], in0=Dl[:, :], in1=G1[:], op=ALU.mult),
                          reads=["D0", "D1"] + GK[0], writes=[ka])
                    if t < 3:
                        o_mm(t + 1)
                    cx.op("dve", lambda e, t=t, ia=ia: e.scalar_tensor_tensor(out=tmp[ia][:], in0=xt4[:, t, :], scalar=float(ALPHA),
                                                                              in1=tmp[ia][:], op0=ALU.mult, op1=ALU.add),
                          reads=[xk[t], ka], writes=[ka])
                    ln_stats(tmp[ia], stt2, mv2, rstd2, ka, LN_EPS)
                    cx.op("dve", lambda e, ia=ia, ic=ic: e.tensor_scalar(out=tmp[ic][:], in0=tmp[ia][:], scalar1=mv2[:, 0:1],
                                                                         scalar2=rstd2[:, 0:1], op0=ALU.subtract, op1=ALU.mult),
                          reads=[ka, "lnmv", "lnrs"], writes=[kc])
                    xi = r_x1.next()
                    cx.op("dve", lambda e, xi=xi, ic=ic: e.tensor_tensor(out=x1o[xi][:], in0=tmp[ic][:], in1=LG1[:], op=ALU.mult),
                          reads=[kc, "LG1"], writes=["x1o%d" % xi])
                    cx.op("dve", lambda e, xi=xi: e.tensor_tensor(out=x1o[xi][:], in0=x1o[xi][:], in1=LB1[:], op=ALU.add),
                          reads=["x1o%d" % xi, "LB1"], writes=["x1o%d" % xi])
                    cx.dma("pool", x1_scr[tok:tok + 128, :], x1o[xi][:], reads=["x1o%d" % xi], writes=[("x1_scr", tok // 128)],
                           skey="x1o%d" % xi)
                    hi_ = r_h2.next()
                    for k in range(8):
                        ai = k // 4
                        cx.op("pe", lambda e, k=k, ai=ai, ic=ic: e.transpose(
                            A[ai][:, (k % 4) * 128:(k % 4 + 1) * 128], tmp[ic][:, k * 128:(k + 1) * 128], ident[:]),
                            reads=[kc, "ident"], writes=[AK[ai]], sig=(k % 4 == 3))
                    for k in range(8):
                        ai = k // 4
                        cx.op("act", lambda e, k=k, ai=ai, hi_=hi_: e.activation(
                            out=h2t[hi_][:, k, :], in_=A[ai][:, (k % 4) * 128:(k % 4 + 1) * 128], func=AF.Identity,
                            scale=a2T[:, k:k + 1], bias=b2T[:, k:k + 1]),
                            reads=[AK[ai], "a2T", "b2T"], writes=["h2t%d" % hi_])
                    cx.dma("act", h2T_scr[:, :, tok:tok + 128], h2t[hi_][:], reads=["h2t%d" % hi_],
                           writes=[("h2T_scr", tok // 512)], skey="h2t%d" % hi_)
                    if debug and b == 0 and t == 0:
                        cx.dma("sp", dbg["gT"], gT[:], reads=gTk, skey="dbgA")
                        cx.dma("sp", dbg["mT"], mT[:], reads=mTk, skey="dbgB")
            cx.barrier()

        with contextlib.ExitStack() as s3:
          if STAGE >= 4:
            NW = 6
            wsl = [sb("wslf%d" % i, [128, 8, 512], BF16, s3) for i in range(NW)]
            wfo = [sb("wfo%d" % i, [128, 11, 512], BF16, s3) for i in range(2)]
            LG2 = sb("LG2", [128, D], F32, s3)
            LB2 = sb("LB2", [128, D], F32, s3)
            h2Ts = [sb("h2T%d" % i, [128, 8, 512], BF16, s3) for i in range(2)]
            aT = sb("aT", [128, 22, 512], BF16, s3)
            sgt = [sb("sgt%d" % i, [128, 512], F32, s3) for i in range(2)]
            x1bs = [sb("x1b%d" % i, [128, 4, D], F32, s3) for i in range(2)]
            t2 = sb("t2", [128, 4, D], F32, s3)
            tmpf = sb("tmpf", [128, 512], F32, s3)
            xh = sb("xh", [128, D], F32, s3)
            yo = [sb("yo%d" % i, [128, D], F32, s3) for i in range(2)]
            stt = sb("stt3", [128, 24], F32, s3)
            mv = sb("mv3", [128, 8], F32, s3)
            rstd = sb("rstd3", [128, 4], F32, s3)
            FA = [ps("FA%d" % i, [128, 512], F32, s3) for i in range(4)]
            ACC = [ps("ACC%d" % i, [128, 512], F32, s3) for i in range(4)]
            with cx.group("p3c"):
                cx.dma("sp", LG2[:], lnrep["lng2"], writes=["LG2"])
                cx.dma("sp", LB2[:], lnrep["lnb2"], writes=["LB2"])
            r_w, r_f, r_fo, r_sg, r_y = Rot(NW), Rot(2), Rot(2), Rot(2), Rot(2)
            w_fi_r = w_ffn_in.rearrange("(k p) n -> p k n", p=128)
            w_fo_r = w_ffn_out.rearrange("(j p) n -> p j n", p=128)
            NBLK = 8 if STAGE >= 4.9 else 1
            for b in range(NBLK):
                tok0 = b * 512
                pb = b % 2
                h2T, x1b = h2Ts[pb], x1bs[pb]
                hk = "h2T%d" % pb
                cx.dma("sp", h2T[:], h2T_scr[:, :, tok0:tok0 + 512], reads=[("h2T_scr", b)], writes=[hk], skey=hk)
                with cx.group("x1b%d" % pb):
                    for t in range(4):
                        cx.dma("sp", x1b[:, t, :], x1_scr[tok0 + t * 128:tok0 + (t + 1) * 128, :],
                               reads=[("x1_scr", b * 4 + t)], writes=[("x1b", pb, t)])
                for jg in range(6):
                    ncol = min(512, DFF - jg * 512)
                    wis = []
                    for gu in range(2):
                        wi = r_w.next()
                        cx.dma("sp", wsl[wi][:, :, 0:ncol], WFI_bf[:, :, gu * DFF + jg * 512:gu * DFF + jg * 512 + ncol],
                               reads=[("WS", "fi")], writes=["wslf%d" % wi], skey="wslf%d" % wi)
                        wis.append(wi)
                    for jj in range(ncol // 128):
                        j = jg * 4 + jj
                        fi = r_f.next()
                        for gu in range(2):
                            wi = wis[gu]
                            for k in range(8):
                                cx.op("pe", lambda e, k=k, gu=gu, jj=jj, fi=fi, wi=wi: e.matmul(
                                    FA[fi * 2 + gu][:, :], lhsT=wsl[wi][:, k, jj * 128:(jj + 1) * 128],
                                    rhs=h2T[:, k, :], start=(k == 0), stop=(k == 7)),
                                    reads=["wslf%d" % wi, hk], writes=["FA%d" % (fi * 2 + gu)], sig=(k == 7))
                        si = r_sg.next()
                        cx.op("act", lambda e, si=si, fi=fi: e.activation(out=sgt[si][:], in_=FA[fi * 2][:, :], func=AF.Silu),
                              reads=["FA%d" % (fi * 2)], writes=["sgt%d" % si])
                        cx.op("dve", lambda e, si=si, fi=fi, j=j: e.tensor_tensor(
                            out=aT[:, j, :], in0=FA[fi * 2 + 1][:, :], in1=sgt[si][:], op=ALU.mult),
                            reads=["FA%d" % (fi * 2 + 1), "sgt%d" % si], writes=[("aT", j)])
                for half in range(2):
                    for s_ in range(2):
                        oi = r_fo.next()
                        ok = "wfo%d" % oi
                        cx.dma("sp", wfo[oi][:], WFO_bf[:, s_ * 11:(s_ + 1) * 11, half * 512:(half + 1) * 512],
                               reads=[("WS", "fo")], writes=[ok], skey=ok)
                        for t in range(4):
                            for jj in range(11):
                                j = s_ * 11 + jj
                                cx.op("pe", lambda e, t=t, jj=jj, j=j, oi=oi: e.matmul(
                                    ACC[t][:, :], lhsT=aT[:, j, t * 128:(t + 1) * 128], rhs=wfo[oi][:, jj, :],
                                    start=(j == 0), stop=(j == 21)),
                                    reads=[("aT", j), ok], writes=["ACC%d" % t], sig=(jj == 10))
                    for t in range(4):
                        hs = slice(half * 512, (half + 1) * 512)
                        cx.op("dve", lambda e, t=t, hs=hs: e.tensor_tensor(out=tmpf[:], in0=ACC[t][:, :], in1=G2[:, hs], op=ALU.mult),
                              reads=["ACC%d" % t] + GK[1], writes=["tmpf"])
                        cx.op("dve", lambda e, t=t, hs=hs: e.scalar_tensor_tensor(
                            out=t2[:, t, hs], in0=x1b[:, t, hs], scalar=float(ALPHA), in1=tmpf[:], op0=ALU.mult, op1=ALU.add),
                            reads=[("x1b", pb, t), "tmpf"], writes=[("t2", t, half)])
                for t in range(4):
                    tok = tok0 + t * 128
                    t2k = ("t2", t, 9)
                    cx.op("dve", lambda e, t=t: e.bn_stats(out=stt[:, 0:6], in_=t2[:, t, 0:512]), reads=[("t2", t, 0)], writes=["lnst"])
                    cx.op("dve", lambda e, t=t: e.bn_stats(out=stt[:, 6:12], in_=t2[:, t, 512:1024]), reads=[("t2", t, 1)], writes=["lnst"])
                    cx.op("dve", lambda e: e.bn_aggr(out=mv[:, 0:2], in_=stt[:, 0:12]), reads=["lnst"], writes=["lnmv"])
                    cx.op("dve", lambda e: e.tensor_scalar(out=rstd[:, 0:1], in0=mv[:, 1:2], scalar1=LN_EPS, scalar2=None, op0=ALU.add),
                          reads=["lnmv"], writes=["lnrs"])
                    cx.op("act", lambda e: e.activation(out=rstd[:, 0:1], in_=rstd[:, 0:1], func=AF.Sqrt), reads=["lnrs"], writes=["lnrs"])
                    cx.op("dve", lambda e: e.reciprocal(out=rstd[:, 0:1], in_=rstd[:, 0:1]), reads=["lnrs"], writes=["lnrs"])
                    cx.op("dve", lambda e, t=t: e.tensor_scalar(out=xh[:], in0=t2[:, t, :], scalar1=mv[:, 0:1], scalar2=rstd[:, 0:1],
                                                                op0=ALU.subtract, op1=ALU.mult),
                          reads=[("t2", t, 0), ("t2", t, 1), "lnmv", "lnrs"], writes=["xh"])
                    yi = r_y.next()
                    cx.op("dve", lambda e: e.tensor_tensor(out=xh[:], in0=xh[:], in1=LG2[:], op=ALU.mult),
                          reads=["xh", "LG2"], writes=["xh"])
                    cx.op("dve", lambda e, yi=yi: e.tensor_tensor(out=yo[yi][:], in0=xh[:], in1=LB2[:], op=ALU.add),
                          reads=["xh", "LB2"], writes=["yo%d" % yi])
                    cx.dma("pool", y_p[tok:tok + 128, :], yo[yi][:], reads=["yo%d" % yi], skey="yo%d" % yi)
            cx.barrier()

        cx.finish()
        print("instructions:", cx.nins, "sems:", len(cx.semh))
    return nc


def _prep_inputs(inp, b):
    f = lambda a: np.ascontiguousarray(a, dtype=np.float32)
    m = {}
    m["x"] = f(inp["x_prompt"][b])
    m["xs"] = f(inp["x_sample"][4 * b:4 * b + 4, 0])
    c5 = np.concatenate([inp["c_sample"][4 * b:4 * b + 4], inp["c_prompt"][b:b + 1]], axis=0)
    m["cT"] = f(c5.T.reshape(8, 128, 5).transpose(1, 0, 2))
    ba = np.asarray(inp["b_ada"][0])
    m["bT5"] = f(np.repeat(ba.reshape(48, 128).T[:, :, None], 5, axis=2))
    m["b5"] = f(np.repeat(ba[None, :], 5, axis=0))
    for n in ("w_ada", "w_in", "w_att_out", "w_ret_out", "w_o", "w_ffn_in", "w_ffn_out"):
        m[n] = f(inp[n][0])
    m["lng1"] = f(np.repeat(np.asarray(inp["ln1_g"][0])[None, :], 128, axis=0))
    m["lnb1"] = f(np.repeat(np.asarray(inp["ln1_b"][0])[None, :], 128, axis=0))
    m["lng2"] = f(np.repeat(np.asarray(inp["ln2_g"][0])[None, :], 128, axis=0))
    m["lnb2"] = f(np.repeat(np.asarray(inp["ln2_b"][0])[None, :], 128, axis=0))
    caches = {128: (inp["cache_k_w128"], inp["cache_v_w128"]), 512: (inp["cache_k_w512"], inp["cache_v_w512"]),
              2048: (inp["cache_k_w2048"], inp["cache_v_w2048"])}
    for w in WINS:
        m["ck%d" % w] = f(caches[w][0][0, 4 * b:4 * b + 4].reshape(4, w, 256))
        m["cv%d" % w] = f(caches[w][1][0, 4 * b:4 * b + 4].reshape(4, w, 256))
    m["state"] = f(inp["state_retention"][0, 4 * b:4 * b + 4])
    m["lng1T"] = f(np.asarray(inp["ln1_g"][0]).reshape(8, 128).T)
    m["lnb1T"] = f(np.asarray(inp["ln1_b"][0]).reshape(8, 128).T)
    return m


def kernel(**inp):
    inp = {k: np.asarray(v) for k, v in inp.items()}
    nc = build(DEBUG)
    cs = _consts()
    in_maps = []
    for b in range(8):
        m = _prep_inputs(inp, b)
        for n, v in cs.items():
            m["c_" + n] = v
        in_maps.append(m)
    res = run_bass_kernel_spmd(nc, in_maps, core_ids=list(range(8)))
    R = res.results
    g = lambda n: np.stack([np.asarray(R[b][n], dtype=np.float32) for b in range(8)], axis=0)
    outs = [g("y_p"), g("y_s").reshape(32, 1, D)]
    for w in WINS:
        outs.append(g("pk%d" % w).reshape(1, 8, w, 4, 64))
        outs.append(g("pv%d" % w).reshape(1, 8, w, 4, 64))
    outs.append(g("p_state").reshape(1, 8, 4, 128, 256))
    for w in WINS:
        outs.append(g("sk%d" % w).reshape(1, 32, w, 4, 64))
        outs.append(g("sv%d" % w).reshape(1, 32, w, 4, 64))
    outs.append(g("s_state").reshape(1, 32, 4, 128, 256))
    return tuple(outs)
```

```python
import contextlib
import math
import numpy as np
import concourse.bass as bass
import concourse.mybir as mybir
from concourse.bass_utils import run_bass_kernel_spmd

F32 = mybir.dt.float32
BF16 = mybir.dt.bfloat16
AF = mybir.ActivationFunctionType
ALU = mybir.AluOpType

D = 1024
T = 4096
NS = 4
DFF = 2816
PAST = 16384
DILS = (1, 4, 16)
WINS = (128, 512, 2048)
ALPHA = 2.0 ** 0.25
STRICT = True

DEBUG = False
STAGE = 99
GSEL = (0, 1, 2)
EPI = 9
ATT = 9


class Ctx:
    ENG = ("pe", "act", "dve", "pool", "sp")

    def __init__(self, nc, es):
        self.nc, self.es = nc, es
        self.e = dict(pe=nc.tensor, act=nc.scalar, dve=nc.vector, pool=nc.gpsimd, sp=nc.sync)
        self.semh = {}
        self.cnt = {}
        for k in self.ENG:
            self.semh["E:" + k] = es.enter_context(nc.semaphore("sem_" + k))
            self.cnt["E:" + k] = 0
        self.seen = {k: {} for k in self.ENG}
        self.lastw = {}
        self.readers = {}
        self.psum_keys = set()
        self.nins = 0

    def _wait(self, e, dep):
        sk, v = dep
        if sk == "E:" + e:
            if e == "pe" or e == "sp" or not STRICT:
                return
        if self.seen[e].get(sk, 0) >= v:
            return
        assert v <= self.cnt[sk], ("dependency on unsignaled instruction", e, dep)
        self.e[e].wait_ge(self.semh[sk], v)
        self.seen[e][sk] = v

    def _deps(self, e, reads, writes):
        deps = set()
        for k in reads:
            if k in self.lastw:
                deps.add(self.lastw[k])
            if k in self.psum_keys:
                for sk, v in self.readers.get(k, {}).items():
                    if sk != "E:" + e:
                        deps.add((sk, v))
        for k in writes:
            if k in self.lastw:
                deps.add(self.lastw[k])
            for sk, v in self.readers.get(k, {}).items():
                deps.add((sk, v))
        for d in sorted(deps, key=lambda t: (t[0], t[1])):
            self._wait(e, d)

    def _record(self, tag, reads, writes):
        sk, v = tag
        for k in reads:
            r = self.readers.setdefault(k, {})
            if r.get(sk, 0) < v:
                r[sk] = v
        for k in writes:
            self.lastw[k] = tag
            self.readers[k] = {}

    def op(self, e, fn, reads=(), writes=(), sig=True):
        self._deps(e, reads, writes)
        ins = fn(self.e[e])
        self.nins += 1
        sk = "E:" + e
        if sig:
            self.cnt[sk] += 1
            ins.then_inc(self.semh[sk], 1)
            tag = (sk, self.cnt[sk])
        else:
            tag = (sk, self.cnt[sk] + 1)
        self._record(tag, reads, writes)
        return ins

    def dma(self, q, out, in_, reads=(), writes=(), skey=None):
        self._deps(q, reads, writes)
        grp = getattr(self, "_grp", None)
        if grp is not None:
            skey = grp[0] + "_" + q
        sk = "D:" + skey
        if sk not in self.semh:
            self.semh[sk] = self.es.enter_context(self.nc.semaphore("dsem_%d" % len(self.semh)))
            self.cnt[sk] = 0
        self.cnt[sk] += 16
        self.e[q].dma_start(out=out, in_=in_).then_inc(self.semh[sk], 16)
        self.nins += 1
        if grp is not None:
            grp[1].append((sk, tuple(reads), tuple(writes)))
        else:
            self._record((sk, self.cnt[sk]), reads, writes)

    @contextlib.contextmanager
    def group(self, skey):
        self._grp = (skey, [])
        try:
            yield
        finally:
            g = self._grp
            self._grp = None
            for sk, reads, writes in g[1]:
                self._record((sk, self.cnt[sk]), reads, writes)

    def barrier(self):
        for e in self.ENG:
            for sk, h in self.semh.items():
                v = self.cnt[sk]
                if v > 0 and sk != "E:" + e and self.seen[e].get(sk, 0) < v:
                    self.e[e].wait_ge(h, v)
                    self.seen[e][sk] = v

    def finish(self):
        for sk, h in self.semh.items():
            if sk.startswith("D:") and self.cnt[sk] > 0:
                self.e["sp"].wait_ge(h, self.cnt[sk])
        for k in ("pe", "act", "dve", "pool"):
            sk = "E:" + k
            if self.cnt[sk] > 0:
                self.e["sp"].wait_ge(self.semh[sk], self.cnt[sk])


class Rot:
    def __init__(self, n):
        self.n, self.i = n, -1

    def next(self):
        self.i = (self.i + 1) % self.n
        return self.i


def _consts():
    c = {}
    c["ident"] = np.eye(128, dtype=np.float32)
    k = np.arange(128)[:, None]
    q = np.arange(128)[None, :]
    prev = (q <= k).astype(np.float32)
    same = (q >= k).astype(np.float32)
    am = np.zeros((128, 2, 2, 128), np.float32)
    am[:, :, 0, :] = prev[:, None, :]
    am[:, :, 1, :] = same[:, None, :]
    c["amask"] = am.reshape(128, 512)
    c["cmask"] = (k <= q).astype(np.float32)
    def _inv32(theta, half, rot):
        t = (np.float32(-math.log(theta)) * np.arange(half, dtype=np.float32)).astype(np.float32)
        return np.exp((t * np.float32(2.0 / rot)).astype(np.float32)).astype(np.float32)

    def _ang32(pos, inv):
        return (np.asarray(pos, dtype=np.float32)[:, None] * inv[None, :]).astype(np.float32).astype(np.float64)

    inv = _inv32(500000.0, 8, 16)
    ra = np.zeros((3, 128, 32, 32), np.float32)
    for g, dil in enumerate(DILS):
        nb = 32 // dil
        for r in range(dil):
            for cb in range(nb):
                sidx = r * nb + cb
                pos = (128 * cb + np.arange(128)) * dil + r
                ang = _ang32(pos, inv)
                cs, sn = np.cos(ang), np.sin(ang)
                ra[g, :, sidx, 0:8] = cs
                ra[g, :, sidx, 8:16] = cs
                ra[g, :, sidx, 16:24] = -sn
                ra[g, :, sidx, 24:32] = sn
    c["ropeA"] = ra
    inv64 = _inv32(10000.0, 64, 128)
    pos = np.arange(T, dtype=np.float64)
    ang = _ang32(pos, inv64)
    rr = np.zeros((T, 256), np.float32)
    rr[:, 0:64] = np.cos(ang)
    rr[:, 64:128] = np.cos(ang)
    rr[:, 128:192] = -np.sin(ang)
    rr[:, 192:256] = np.sin(ang)
    c["ropeR"] = rr
    gam = 1.0 - 2.0 ** (-5.0 - np.arange(4, dtype=np.float64))
    i = np.arange(128, dtype=np.float64)[:, None]
    dec = np.zeros((128, 12), np.float64)
    dec[:, 0:4] = gam[None, :] ** (i + 1.0)
    dec[:, 4:8] = gam[None, :] ** (-(i + 1.0)) * (128.0 ** -0.5)
    dec[:, 8:12] = gam[None, :] ** 128.0
    c["dec"] = dec.astype(np.float32)
    sel = np.zeros((5, 128), np.float32)
    sel[4, :] = 1.0
    c["sel5"] = sel
    a = _ang32([PAST], inv)[0]
    ta = np.concatenate([np.cos(a), np.cos(a), -np.sin(a), np.sin(a)]).astype(np.float32)
    c["tabAs"] = np.repeat(ta[None, :], 4, axis=0)
    a = _ang32([PAST], inv64)[0]
    tr = np.concatenate([np.cos(a), np.cos(a), -np.sin(a), np.sin(a)]).astype(np.float32)
    c["tabRs"] = np.repeat(tr[None, :], 4, axis=0)
    oh = np.zeros((128, 4, 4, 4), np.float32)
    for s_ in range(4):
        oh[:, s_, :, s_] = 1.0
    c["oh"] = oh
    ohE = np.zeros((12, 4, 4), np.float32)
    for r_ in range(12):
        ohE[r_, :, r_ % 4] = 1.0
    c["ohE"] = ohE
    selS = np.zeros((4, 4, 128), np.float32)
    for s_ in range(4):
        selS[s_, s_, :] = 1.0
    c["selS"] = selS
    c["oh4"] = np.eye(4, dtype=np.float32)
    return c


SCONST_SHAPES = dict(tabAs=[4, 32], tabRs=[4, 256], oh=[128, 4, 4, 4], ohE=[12, 4, 4], selS=[4, 4, 128], oh4=[4, 4])


CONST_SHAPES = dict(ident=[128, 128], amask=[128, 512], cmask=[128, 128], ropeA=[3, 128, 32, 32],
                    ropeR=[T, 256], dec=[128, 12], sel5=[5, 128])


def build(debug=False):
    nc = bass.Bass("TRN2", target_bir_lowering=False)

    def din(name, shape, dt=F32):
        return nc.dram_tensor(name, list(shape), dt, kind="ExternalInput").ap()

    def dout(name, shape, dt=F32):
        return nc.dram_tensor(name, list(shape), dt, kind="ExternalOutput").ap()

    x = din("x", [T, D])
    xs = din("xs", [NS, D])
    cT = din("cT", [128, 8, 5])
    bT5 = din("bT5", [128, 48, 5])
    b5 = din("b5", [5, 6 * D])
    w_ada = din("w_ada", [D, 6 * D])
    w_in = din("w_in", [D, 7424])
    w_att_out = din("w_att_out", [256, D])
    w_ret_out = din("w_ret_out", [D, D])
    w_o = din("w_o", [D, D])
    w_ffn_in = din("w_ffn_in", [D, 2 * DFF])
    w_ffn_out = din("w_ffn_out", [DFF, D])
    lnrep = {n: din(n, [128, D]) for n in ("lng1", "lnb1", "lng2", "lnb2")}
    lng1T = din("lng1T", [128, 8])
    lnb1T = din("lnb1T", [128, 8])
    cin = {n: din("c_" + n, s) for n, s in CONST_SHAPES.items()}

    cin_s = {}
    for w in WINS:
        cin_s["ck%d" % w] = din("ck%d" % w, [NS, w, 256])
        cin_s["cv%d" % w] = din("cv%d" % w, [NS, w, 256])
    cin_s["state"] = din("state", [NS, 4, 128, 256])
    for n, shp in SCONST_SHAPES.items():
        cin_s[n] = din("c_" + n, shp)
    y_p = dout("y_p", [T, D])
    y_s = dout("y_s", [NS, D])
    sk = [dout("sk%d" % w, [NS, w, 256]) for w in WINS]
    sv = [dout("sv%d" % w, [NS, w, 256]) for w in WINS]
    s_state = dout("s_state", [NS, 4, 128, 256])
    pk = [dout("pk%d" % w, [w, 256]) for w in WINS]
    pv = [dout("pv%d" % w, [w, 256]) for w in WINS]
    p_state = dout("p_state", [4, 128, 256])
    dbg = {}
    if debug:
        dbg["oT"] = dout("dbg_oT", [128, 2, T], BF16)
        dbg["modT"] = dout("dbg_modT", [128, 48, 5])
        dbg["G"] = dout("dbg_G", [128, 2, D])
        dbg["gT"] = dout("dbg_gT", [128, 8, 512], BF16)
        dbg["mT"] = dout("dbg_mT", [128, 8, 512], BF16)

    x1_scr = nc.dram_tensor("x1_scr", [T, D], F32, kind="Internal").ap()
    h2T_scr = nc.dram_tensor("h2T_scr", [128, 8, T], BF16, kind="Internal").ap()
    oT_scr = nc.dram_tensor("oT_scr", [128, 2, T], BF16, kind="Internal").ap()
    modtm_scr = nc.dram_tensor("modtm_scr", [5, 6 * D], F32, kind="Internal").ap()
    hT_scr = nc.dram_tensor("hT_scr", [128, 8, T], BF16, kind="Internal").ap()
    WIN_bf = nc.dram_tensor("WIN_bf", [128, 8, 7424], BF16, kind="Internal").ap()
    WRO_bf = nc.dram_tensor("WRO_bf", [128, 8, D], BF16, kind="Internal").ap()
    WO_bf = nc.dram_tensor("WO_bf", [128, 8, D], BF16, kind="Internal").ap()
    WFI_bf = nc.dram_tensor("WFI_bf", [128, 8, 2 * DFF], BF16, kind="Internal").ap()
    WFO_bf = nc.dram_tensor("WFO_bf", [128, 22, D], BF16, kind="Internal").ap()

    w_in_r = w_in.rearrange("(k p) n -> p k n", p=128)
    w_ada_r = w_ada.rearrange("(k p) n -> p k n", p=128)

    es = contextlib.ExitStack()
    with es:
        cx = Ctx(nc, es)

        def sb(name, shape, dt=F32, st=None):
            return (st or es).enter_context(nc.sbuf_tensor(name, list(shape), dt))

        def ps(name, shape, dt=F32, st=None):
            cx.psum_keys.add(name)
            return (st or es).enter_context(nc.psum_tensor(name, list(shape), dt))

        ident = sb("ident", [128, 128])
        ident_b = sb("ident_b", [128, 128], BF16)
        ones_b = sb("ones_b", [128, 64], BF16)
        dec = sb("dec", [128, 12])
        modT = sb("modT", [128, 48, 5])
        ops1T = sb("ops1T", [128, 8, 5])
        a2T = sb("a2T", [128, 8])
        b2T = sb("b2T", [128, 8])
        G1 = sb("G1", [128, D])
        G2 = sb("G2", [128, D])
        with cx.group("consts"):
            cx.dma("sp", ident[:], cin["ident"], writes=["ident"])
            cx.dma("pool", ident_b[:], cin["ident"], writes=["ident_b"])
            cx.dma("sp", dec[:], cin["dec"], writes=["dec"])
        cx.op("dve", lambda e: e.memset(ones_b[:], 1.0), writes=["ones_b"])

        GAM = [1.0 - 2.0 ** (-5.0 - h) for h in range(4)]
        GC = [g_ ** 128.0 for g_ in GAM]
        LN_EPS = 1e-5
        GN_EPS = 1e-6

        with contextlib.ExitStack() as s0:
            cT_sb = sb("cT_sb", [128, 8, 5], F32, s0)
            scT = sb("scT", [128, 8, 5], F32, s0)
            bT5_sb = sb("bT5_sb", [128, 48, 5], F32, s0)
            modtm = sb("modtm", [5, 6 * D], F32, s0)
            sel5 = sb("sel5", [5, 128], F32, s0)
            g1T = sb("g1T", [128, 8], F32, s0)
            wsl = [sb("wsl0_%d" % i, [128, 8, 512], F32, s0) for i in range(3)]
            pm = ps("pm", [128, 512], F32, s0)
            pg = [ps("pg%d" % i, [128, 512], F32, s0) for i in range(2)]
            with cx.group("p0c"):
                cx.dma("sp", cT_sb[:], cT, writes=["cT_sb"])
                cx.dma("sp", bT5_sb[:], bT5, writes=["bT5"])
                cx.dma("sp", modtm[:], b5, writes=["modtm"])
                cx.dma("sp", sel5[:], cin["sel5"], writes=["sel5"])
                cx.dma("sp", g1T[:], lng1T, writes=["g1T"])
                cx.dma("sp", b2T[:], lnb1T, writes=["b2T"])
            cx.op("act", lambda e: e.activation(out=scT[:], in_=cT_sb[:], func=AF.Silu),
                  reads=["cT_sb"], writes=["scT"])
            rg = Rot(2)
            for s in range(12):
                w = wsl[s % 3]
                wk = "wsl0_%d" % (s % 3)
                cx.dma("sp", w[:], w_ada_r[:, :, s * 512:(s + 1) * 512], writes=[wk], skey=wk)
                if True:
                    bi = rg.next()
                    for k in range(8):
                        cx.op("pe", lambda e, k=k, w=w, bi=bi: e.matmul(
                            pg[bi][0:5, :], lhsT=scT[:, k, :], rhs=w[:, k, :], start=(k == 0), stop=(k == 7)),
                            reads=[wk, "scT"], writes=["pg%d" % bi], sig=(k == 7))
                    cx.op("dve", lambda e, bi=bi, s=s: e.tensor_tensor(
                        out=modtm[:, s * 512:(s + 1) * 512], in0=pg[bi][0:5, :],
                        in1=modtm[:, s * 512:(s + 1) * 512], op=ALU.add),
                        reads=["pg%d" % bi, "modtm"], writes=["modtm"])
            for c in range(48):
                cx.op("pe", lambda e, c=c: e.transpose(pm[:, c * 5:(c + 1) * 5], modtm[0:5, c * 128:(c + 1) * 128], ident[0:5, 0:5]),
                      reads=["modtm", "ident"], writes=["pm"], sig=(c == 47))
            cx.op("dve", lambda e: e.tensor_copy(out=modT[:], in_=pm[:, 0:240].rearrange("p (a b) -> p a b", b=5)),
                  reads=["pm"], writes=["modT"])
            cx.op("dve", lambda e: e.tensor_scalar(out=ops1T[:], in0=modT[:, 8:16, :], scalar1=1.0, scalar2=None,
                                                   op0=ALU.add), reads=["modT"], writes=["ops1T"])
            tmp8 = sb("tmp8", [128, 8], F32, s0)
            cx.op("dve", lambda e: e.tensor_scalar(out=tmp8[:], in0=modT[:, 32:40, 4], scalar1=1.0, scalar2=None,
                                                   op0=ALU.add), reads=["modT"], writes=["tmp8"])
            cx.op("dve", lambda e: e.tensor_tensor(out=a2T[:], in0=g1T[:], in1=tmp8[:], op=ALU.mult),
                  reads=["tmp8", "g1T"], writes=["a2T"])
            cx.op("dve", lambda e: e.tensor_tensor(out=b2T[:], in0=b2T[:], in1=tmp8[:], op=ALU.mult),
                  reads=["tmp8", "b2T"], writes=["b2T"])
            cx.op("dve", lambda e: e.tensor_tensor(out=b2T[:], in0=b2T[:], in1=modT[:, 24:32, 4], op=ALU.add),
                  reads=["modT", "b2T"], writes=["b2T"])
            for gi, Gt in enumerate((G1, G2)):
                for half in range(2):
                    bi = rg.next()
                    cx.op("pe", lambda e, gi=gi, half=half, bi=bi: e.matmul(
                        pg[bi][:, :], lhsT=sel5[:], rhs=modtm[:, (2 + 3 * gi) * D + half * 512:(2 + 3 * gi) * D + (half + 1) * 512],
                        start=True, stop=True),
                        reads=["sel5", "modtm"], writes=["pg%d" % bi])
                    cx.op("act", lambda e, Gt=Gt, half=half, bi=bi: e.activation(
                        out=Gt[:, half * 512:(half + 1) * 512], in_=pg[bi][:, :], func=AF.Copy),
                        reads=["pg%d" % bi], writes=[("G", gi, half)])
            if debug:
                cx.dma("sp", dbg["modT"], modT[:], reads=["modT"], skey="dbg0")
                cx.dma("sp", dbg["G"][:, 0, :], G1[:], reads=[("G", 0, 0), ("G", 0, 1)], skey="dbg1")
                cx.dma("sp", dbg["G"][:, 1, :], G2[:], reads=[("G", 1, 0), ("G", 1, 1)], skey="dbg2")
            cx.dma("sp", modtm_scr, modtm[:], reads=["modtm"], writes=["modtm_scr"], skey="modtm_out")
            cx.barrier()
        GK = [[("G", gi, h) for h in range(2)] for gi in range(2)]

        def make_hT(xt, xkeys, nt, hT, hkey, banks, bankkeys, rot):
            for k in range(8):
                bi = rot.next()
                for t in range(nt):
                    cx.op("pe", lambda e, t=t, k=k, bi=bi: e.transpose(
                        banks[bi][:, t * 128:(t + 1) * 128], xt[:, t, k * 128:(k + 1) * 128], ident[:]),
                        reads=[xkeys[t], "ident"], writes=[bankkeys[bi]], sig=(t == nt - 1))
                if bi == 0:
                    cx.op("act", lambda e, k=k, bi=bi: e.activation(
                        out=hT[:, k, :], in_=banks[bi][:, 0:nt * 128], func=AF.Identity,
                        scale=ops1T[:, k, 4:5], bias=modT[:, k, 4:5]),
                        reads=[bankkeys[bi], "ops1T", "modT"], writes=[hkey])
                else:
                    cx.op("dve", lambda e, k=k, bi=bi: e.tensor_scalar(
                        out=hT[:, k, :], in0=banks[bi][:, 0:nt * 128], scalar1=ops1T[:, k, 4:5],
                        scalar2=modT[:, k, 4:5], op0=ALU.mult, op1=ALU.add),
                        reads=[bankkeys[bi], "ops1T", "modT"], writes=[hkey])

        with contextlib.ExitStack() as s1:
          if STAGE >= 1:
            wA = sb("wA", [128, 8, 9, 256], BF16, s1)
            hTB = sb("hTB", [128, 8, 2048], BF16, s1)
            qT = sb("qT", [128, 3, 2048], BF16, s1)
            kT = sb("kT", [128, 3, T], BF16, s1)
            vS = sb("vS", [128, 3, 32, 128], BF16, s1)
            ropeA = sb("ropeA", [128, 3, 32, 32], F32, s1)
            amask = sb("amask", [128, 512], BF16, s1)
            xt4 = [sb("xt4_%d" % i, [128, 4, D], F32, s1) for i in range(2)]
            qkv = [sb("qkv_%d" % i, [128, 384], F32, s1) for i in range(4)]
            rA = [sb("rA_%d" % i, [128, 4, 16], F32, s1) for i in range(2)]
            rB = [sb("rB_%d" % i, [128, 4, 16], F32, s1) for i in range(2)]
            pT = [sb("pT_%d" % i, [128, 512], BF16, s1) for i in range(4)]
            rLL = sb("rLL", [128, 512], F32, s1)
            Uc = sb("Uc", [128, 512], F32, s1)
            Lc = sb("Lc", [128, 512], F32, s1)
            pending = [None]
            oTw = [sb("oTw%d" % i, [128, 512], BF16, s1) for i in range(2)]
            r_ow = Rot(2)
            PA = ps("PA", [128, 1024], F32, s1)
            PB = ps("PB", [128, 1024], F32, s1)
            PC = ps("PC", [128, 1024], F32, s1)
            pq = [PA[:, 0:512], PA[:, 512:1024], PC[:, 0:512], PC[:, 512:1024]]
            pqk = ["pq0", "pq1", "PC0", "PC1"]
            pt = [PB[:, 0:512], PB[:, 512:1024]]
            pss = [PB, PC, PA]
            psk = [["pt0", "pt1"], ["PC0", "PC1"], ["pq0", "pq1"]]
            cx.psum_keys.update(["pq0", "pq1", "pt0", "pt1", "PC0", "PC1"])
            U = ps("U", [128, 512], F32, s1)
            LL = ps("LL", [128, 512], F32, s1)
            with cx.group("p1c"):
                for g in range(3):
                    cx.dma("sp", ropeA[:, g, :, :], cin["ropeA"][g], writes=["ropeA"])
                cx.dma("pool", amask[:], cin["amask"], writes=["amask"])
            with cx.group("s_roll"):
                for g, Lg in enumerate(WINS):
                    for dst_, src_ in ((sk[g], cin_s["ck%d" % Lg]), (sv[g], cin_s["cv%d" % Lg])):
                        for s_ in range(NS):
                            cx.dma("act",
                                   dst_[s_, 0:Lg - 1, :].rearrange("l d -> (l d)").rearrange("(c n) -> c n", c=16),
                                   src_[s_, 1:Lg, :].rearrange("l d -> (l d)").rearrange("(c n) -> c n", c=16))
            r_pq, r_pt, r_ps, r_x, r_qk, r_v, r_r, r_p = Rot(4), Rot(2), Rot(3), Rot(2), Rot(4), Rot(2), Rot(2), Rot(4)
            for g in range(3):
                for t in range(3):
                    c0 = t * 768 + g * 256
                    cx.dma("pool", wA[:, :, g * 3 + t, :], w_in_r[:, :, c0:c0 + 256], writes=["wA"], skey="wA")
            for hp in range(2):
                if hp == 0:
                    with cx.group("wconv"):
                        for k in range(8):
                            cx.dma("pool", WIN_bf[:, k, :], w_in[k * 128:(k + 1) * 128, :], writes=[("WS", "in")])
                        for k in range(8):
                            cx.dma("pool", WRO_bf[:, k, :], w_ret_out[k * 128:(k + 1) * 128, :], writes=[("WS", "ro")])
                            cx.dma("pool", WO_bf[:, k, :], w_o[k * 128:(k + 1) * 128, :], writes=[("WS", "o")])
                        for k in range(8):
                            cx.dma("pool", WFI_bf[:, k, :], w_ffn_in[k * 128:(k + 1) * 128, :], writes=[("WS", "fi")])
                        for k in range(22):
                            cx.dma("pool", WFO_bf[:, k, :], w_ffn_out[k * 128:(k + 1) * 128, :], writes=[("WS", "fo")])
                for B in range(2):
                    for tg in range(4):
                        tk0 = B * 2048 + tg * 512
                        if hp == 0:
                            xi = r_x.next()
                            xk = ["xt4_%d_%d" % (xi, t) for t in range(4)]
                            for t in range(4):
                                tok = tk0 + t * 128
                                cx.dma("sp", xt4[xi][:, t, :], x[tok:tok + 128, :], writes=[xk[t]], skey=xk[t])
                            make_hT(xt4[xi], xk, 4, hTB[:, :, tg * 512:(tg + 1) * 512], ("hTB", tg),
                                    pt, ["pt0", "pt1"], r_pt)
                            cx.dma("act", hT_scr[:, :, tk0:tk0 + 512], hTB[:, :, tg * 512:(tg + 1) * 512], reads=[("hTB", tg)],
                                   writes=[("hT_scr", B * 4 + tg)], skey="hTst%d" % tg)
                        else:
                            cx.dma("sp", hTB[:, :, tg * 512:(tg + 1) * 512], hT_scr[:, :, tk0:tk0 + 512],
                                   reads=[("hT_scr", B * 4 + tg)], writes=[("hTB", tg)], skey="hTld%d" % tg)
                    hkeys = [("hTB", tg) for tg in range(4)]
                    for g, dil in (enumerate(DILS) if STAGE >= 1.5 else []):
                        nbl = 16 // dil
                        nb = 32 // dil
                        for r in range(dil):
                            for cl in range(nbl):
                                cb = B * nbl + cl
                                sidx = r * nb + cb
                                qidx = r * nbl + cl
                                st = r + 128 * cl * dil
                                if g not in GSEL:
                                    continue
                                bi = r_pq.next()
                                for k in range(8):
                                    cx.op("pe", lambda e, k=k, bi=bi, st=st, dil=dil, g=g, hp=hp: e.matmul(
                                        pq[bi][:, 0:384], lhsT=hTB[:, k, st:st + 127 * dil + 1:dil],
                                        rhs=wA[:, k, g * 3:(g + 1) * 3, hp * 128:(hp + 1) * 128],
                                        start=(k == 0), stop=(k == 7)),
                                        reads=hkeys + ["wA"], writes=[pqk[bi]], sig=(k == 7))
                                ri = r_r.next()
                                qi = r_qk.next()
                                qk = ("qkv", qi)
                                cx.op("act", lambda e, bi=bi, qi=qi: e.activation(
                                    out=qkv[qi][:, :], in_=pq[bi][:, 0:384], func=AF.Copy),
                                    reads=[pqk[bi]], writes=[qk])
                                pqv = qkv[qi][:, 0:256].rearrange("p (a c) -> p a c", a=4)
                                tabC = ropeA[:, g, sidx:sidx + 1, 0:16].to_broadcast([128, 4, 16])
                                cx.op("dve", lambda e, ri=ri, pqv=pqv, tabC=tabC: e.tensor_tensor(
                                    out=rA[ri][:], in0=pqv[:, :, 0:16], in1=tabC, op=ALU.mult),
                                    reads=[qk, "ropeA"], writes=["rA%d" % ri])
                                for hf in range(2):
                                    tabS = ropeA[:, g, sidx:sidx + 1, 16 + 8 * hf:24 + 8 * hf].to_broadcast([128, 4, 8])
                                    cx.op("dve", lambda e, ri=ri, pqv=pqv, tabS=tabS, hf=hf: e.tensor_tensor(
                                        out=rB[ri][:, :, 8 * hf:8 * hf + 8],
                                        in0=pqv[:, :, 8 - 8 * hf:16 - 8 * hf], in1=tabS, op=ALU.mult),
                                        reads=[qk, "ropeA"], writes=["rB%d" % ri])
                                cx.op("dve", lambda e, ri=ri, pqv=pqv: e.tensor_tensor(
                                    out=pqv[:, :, 0:16], in0=rA[ri][:], in1=rB[ri][:], op=ALU.add),
                                    reads=["rA%d" % ri, "rB%d" % ri], writes=[qk])
                                cx.op("act", lambda e, qi=qi, g=g, sidx=sidx: e.activation(
                                    out=vS[:, g, sidx, :], in_=qkv[qi][:, 256:384], func=AF.Copy),
                                    reads=[qk], writes=[("vS", g, sidx)])
                                inwin = (128 * cb * dil + r) >= T - WINS[g]
                                if inwin:
                                    row0 = 128 * cb * dil + r - (T - WINS[g])
                                    cx.dma("sp", pv[g][row0:row0 + 127 * dil + 1:dil, hp * 128:(hp + 1) * 128],
                                           qkv[qi][:, 256:384], reads=[qk], skey="qkv_%d" % qi)
                                    cx.dma("sp", pk[g][row0:row0 + 127 * dil + 1:dil, hp * 128:(hp + 1) * 128],
                                           qkv[qi][:, 128:256], reads=[qk], skey="qkv_%d" % qi)
                                ti = r_pt.next()
                                for a in range(2):
                                    cx.op("pe", lambda e, a=a, ti=ti, qi=qi: e.transpose(
                                        pt[ti][:, a * 128:(a + 1) * 128], qkv[qi][:, a * 128:(a + 1) * 128], ident[:]),
                                        reads=[qk, "ident"], writes=["pt%d" % ti], sig=(a == 1))
                                if ti == 0:
                                    cx.op("act", lambda e, ti=ti, g=g, qidx=qidx: e.activation(
                                        out=qT[:, g, qidx * 128:(qidx + 1) * 128], in_=pt[ti][:, 0:128], func=AF.Copy),
                                        reads=["pt%d" % ti], writes=[("qT", g, qidx)])
                                    cx.op("act", lambda e, ti=ti, g=g, sidx=sidx: e.activation(
                                        out=kT[:, g, sidx * 128:(sidx + 1) * 128], in_=pt[ti][:, 128:256], func=AF.Copy),
                                        reads=["pt%d" % ti], writes=[("kT", g, sidx)])
                                else:
                                    cx.op("dve", lambda e, ti=ti, g=g, qidx=qidx: e.tensor_copy(
                                        out=qT[:, g, qidx * 128:(qidx + 1) * 128], in_=pt[ti][:, 0:128]),
                                        reads=["pt%d" % ti], writes=[("qT", g, qidx)])
                                    cx.op("dve", lambda e, ti=ti, g=g, sidx=sidx: e.tensor_copy(
                                        out=kT[:, g, sidx * 128:(sidx + 1) * 128], in_=pt[ti][:, 128:256]),
                                        reads=["pt%d" % ti], writes=[("kT", g, sidx)])
                    for Wl in range(4 if STAGE >= 2 else 0):
                        W = 4 * B + Wl
                        cx.op("dve", lambda e: e.memset(U[:], 0.0), writes=["U"])
                        cx.op("dve", lambda e: e.memset(LL[:], 0.0), writes=["LL"])
                        units = []
                        for j in range(4 * Wl, 4 * Wl + 4):
                            units.append((0, 0, j, 0, 128, (j % 4) * 128, 1))
                        for r in range(4):
                            units.append((1, r, Wl, 0, 128, r, 4))
                        for r in range(16):
                            units.append((2, r, 0, Wl * 32, 32, r, 16))
                        def unit_front(u):
                            (g, r, cl, q0, nq, col0, cstep) = u
                            dil = DILS[g]
                            nbl = 16 // dil
                            nb = 32 // dil
                            cb = B * nbl + cl
                            qidx = r * nbl + cl
                            kbs = [(0, cb - 1), (1, cb)] if cb >= 1 else [(1, cb)]
                            si = r_ps.next()
                            last = (kbs[-1][0], 1)
                            for (kbi, kb) in kbs:
                                sidx = r * nb + kb
                                for e2 in range(2):
                                    cx.op("pe", lambda e, g=g, sidx=sidx, e2=e2, qidx=qidx, q0=q0, nq=nq, kbi=kbi, si=si:
                                          e.matmul(pss[si][:, e2 * 512 + kbi * nq:e2 * 512 + (kbi + 1) * nq],
                                                   lhsT=kT[64 * e2:64 * e2 + 64, g, sidx * 128:(sidx + 1) * 128],
                                                   rhs=qT[64 * e2:64 * e2 + 64, g, qidx * 128 + q0:qidx * 128 + q0 + nq],
                                                   start=True, stop=True),
                                          reads=[("kT", g, sidx), ("qT", g, qidx)], writes=psk[si],
                                          sig=((kbi, e2) == last))
                            lo = 0 if len(kbs) == 2 else nq
                            hi = 2 * nq
                            pi = r_p.next()
                            pTv = pT[pi][:, 0:4 * nq].rearrange("p (a b) -> p a b", a=2)
                            cx.op("act", lambda e, si=si, pTv=pTv, lo=lo, hi=hi: e.activation(
                                out=pTv[:, :, lo:hi], in_=pss[si][:, :].rearrange("p (a b) -> p a b", a=2)[:, :, lo:hi],
                                func=AF.Exp, scale=0.125),
                                reads=psk[si], writes=["pT%d" % pi])
                            mk = amask[:, :].rearrange("p (a k b) -> p a k b", a=2, k=2)[:, :, lo // nq:2, q0:q0 + nq]
                            pT4 = pT[pi][:, 0:4 * nq].rearrange("p (a k b) -> p a k b", a=2, k=2)[:, :, lo // nq:2, :]
                            cx.op("dve", lambda e, pT4=pT4, mk=mk: e.tensor_tensor(out=pT4, in0=pT4, in1=mk, op=ALU.mult),
                                  reads=["pT%d" % pi, "amask"], writes=["pT%d" % pi])
                            return (g, r, nb, nq, col0, cstep, kbs, pi)

                        def unit_back(st):
                            (g, r, nb, nq, col0, cstep, kbs, pi) = st
                            cols = slice(col0, col0 + (nq - 1) * cstep + 1, cstep)
                            nmm = len(kbs) * 2
                            cntm = 0
                            for (kbi, kb) in kbs:
                                sidx = r * nb + kb
                                for e2 in range(2):
                                    slot = e2 * 2 + kbi
                                    cntm += 1
                                    cx.op("pe", lambda e, g=g, sidx=sidx, e2=e2, slot=slot, pi=pi, nq=nq, cols=cols: e.matmul(
                                        U[64 * e2:64 * e2 + 64, cols], lhsT=vS[:, g, sidx, 64 * e2:64 * e2 + 64],
                                        rhs=pT[pi][:, slot * nq:(slot + 1) * nq], start=False, stop=False,
                                        skip_group_check=True),
                                        reads=[("vS", g, sidx), "pT%d" % pi], writes=["U"], sig=False)
                                    cx.op("pe", lambda e, e2=e2, slot=slot, pi=pi, nq=nq, cols=cols: e.matmul(
                                        LL[64 * e2:64 * e2 + 64, cols], lhsT=ones_b[:, 0:64],
                                        rhs=pT[pi][:, slot * nq:(slot + 1) * nq], start=False, stop=False,
                                        skip_group_check=True),
                                        reads=["ones_b", "pT%d" % pi], writes=["LL"], sig=(cntm == nmm))

                        inflight = []
                        for ui, u in enumerate(units):
                            inflight.append(unit_front(u))
                            if len(inflight) > 2:
                                unit_back(inflight.pop(0))
                            if ui == 2 and pending[0] is not None:
                                pending[0]()
                                pending[0] = None
                        while inflight:
                            unit_back(inflight.pop(0))
                        cx.op("act", lambda e: e.activation(out=Uc[:], in_=U[:], func=AF.Copy), reads=["U"], writes=["Uc"])
                        cx.op("act", lambda e: e.activation(out=Lc[:], in_=LL[:], func=AF.Copy), reads=["LL"], writes=["Lc"])

                        def finalize(hp=hp, W=W):
                            cx.op("dve", lambda e: e.reciprocal(out=rLL[:], in_=Lc[:]), reads=["Lc"], writes=["rLL"])
                            oi = r_ow.next()
                            cx.op("dve", lambda e, oi=oi: e.tensor_tensor(
                                out=oTw[oi][:], in0=Uc[:], in1=rLL[:], op=ALU.mult),
                                reads=["Uc", "rLL"], writes=["oTw%d" % oi])
                            cx.dma("sp", oT_scr[:, hp, W * 512:(W + 1) * 512], oTw[oi][:], reads=["oTw%d" % oi],
                                   writes=[("oT_scr", W)], skey="oTw%d" % oi)
                            if debug:
                                cx.dma("sp", dbg["oT"][:, hp, W * 512:(W + 1) * 512], oTw[oi][:], reads=["oTw%d" % oi],
                                       skey="oTwd%d" % oi)
                        pending[0] = finalize
                    if pending[0] is not None:
                        pending[0]()
                        pending[0] = None
            cx.barrier()

        def ln_stats(src, stt, mv, rstd, key, eps):
            for hh in range(2):
                cx.op("dve", lambda e, hh=hh: e.bn_stats(out=stt[:, hh * 6:(hh + 1) * 6], in_=src[:, hh * 512:(hh + 1) * 512]),
                      reads=[key], writes=["lnst"])
            cx.op("dve", lambda e: e.bn_aggr(out=mv[:, 0:2], in_=stt[:, 0:12]), reads=["lnst"], writes=["lnmv"])
            cx.op("dve", lambda e: e.tensor_scalar(out=rstd[:, 0:1], in0=mv[:, 1:2], scalar1=eps, scalar2=None, op0=ALU.add),
                  reads=["lnmv"], writes=["lnrs"])
            cx.op("act", lambda e: e.activation(out=rstd[:, 0:1], in_=rstd[:, 0:1], func=AF.Sqrt), reads=["lnrs"], writes=["lnrs"])
            cx.op("dve", lambda e: e.reciprocal(out=rstd[:, 0:1], in_=rstd[:, 0:1]), reads=["lnrs"], writes=["lnrs"])

        with contextlib.ExitStack() as s0:
            if STAGE >= 0.5:
                modtm = sb("modtm_s", [5, 6 * D], F32, s0)
                wsl = [sb("wslS_%d" % i, [128, 8, 512], BF16, s0) for i in range(4)]
                pg = [ps("pgS%d" % i, [128, 512], F32, s0) for i in range(2)]
                rg = Rot(2)
                cx.dma("sp", modtm[:], modtm_scr, reads=["modtm_scr"], writes=["modtm"], skey="s_modtm")
                ck = [cin_s["ck%d" % w] for w in WINS]
                cv = [cin_s["cv%d" % w] for w in WINS]
                wslS = wsl
                r_ws = Rot(4)
                proj = sb("proj", [4, 7424], F32, s0)
                xs_sb = sb("xs_sb", [4, D], F32, s0)
                row = [sb("row%d" % i, [4, D], F32, s0) for i in range(6)]
                big = sb("big", [4, 2 * DFF], F32, s0)
                xT_s = sb("xT_s", [128, 22, 4], BF16, s0)
                tabAs = sb("tabAs", [4, 32], F32, s0)
                tabRs = sb("tabRs", [4, 256], F32, s0)
                oh = sb("oh", [128, 4, 4, 4], F32, s0)
                ohE = sb("ohE", [12, 4, 4], F32, s0)
                selS = sb("selS", [4, 4, 128], F32, s0)
                oh4 = sb("oh4", [4, 4], F32, s0)
                lnr = sb("lnr", [4, 4, D], F32, s0)
                Rst = sb("Rst", [128, 16, 256], F32, s0)
                Ksel = [sb("Ksel%d" % i, [128, 256], F32, s0) for i in range(4)]
                Vsel = [sb("Vsel%d" % i, [128, 256], F32, s0) for i in range(4)]
                ones_f = sb("ones_f", [128, 1], F32, s0)
                KE = sb("KE", [12, 256], F32, s0)
                VE = sb("VE", [12, 4, 65], F32, s0)
                qE = sb("qE", [12, 256], F32, s0)
                prod = [sb("prod%d" % i, [128, 256], F32, s0) for i in range(4)]
                scs = [sb("scs%d" % i, [128, 4], F32, s0) for i in range(4)]
                Pm = [sb("Pm%d" % i, [128, 4, 4], F32, s0) for i in range(4)]
                qTm = sb("qTm", [128, 4, 4, 4], F32, s0)
                qTr = sb("qTr", [128, 4, 4], F32, s0)
                kM = sb("kM", [4, 4, 512], F32, s0)
                sm4 = sb("sm4", [4, 64], F32, s0)
                PT = ps("PT", [128, 512], F32, s0)
                QB = ps("QB", [128, 512], F32, s0)
                UL = ps("UL", [128, 512], F32, s0)
                QR = ps("QR", [128, 1024], F32, s0)
                cx.psum_keys.update(["QR0", "QR1"])
                with cx.group("s_consts"):
                    cx.dma("sp", xs_sb[:], xs, writes=["xs_sb"])
                    cx.dma("sp", tabAs[:], cin_s["tabAs"], writes=["tabAs"])
                    cx.dma("sp", tabRs[:], cin_s["tabRs"], writes=["tabRs"])
                    cx.dma("sp", oh[:], cin_s["oh"], writes=["oh"])
                    cx.dma("sp", ohE[:], cin_s["ohE"], writes=["ohE"])
                    cx.dma("sp", selS[:], cin_s["selS"], writes=["selS"])
                    cx.dma("sp", oh4[:], cin_s["oh4"], writes=["oh4"])
                    for i, nme in enumerate(("lng1", "lnb1", "lng2", "lnb2")):
                        cx.dma("sp", lnr[:, i, :], lnrep[nme][0:4, :], writes=["lnr"])
                    cx.dma("sp", Rst[:], cin_s["state"].rearrange("s h p c -> p (s h) c"), writes=["Rst"])

                def s_transpose(src, n, dst_off):
                    for i in range(n):
                        cx.op("pe", lambda e, i=i: e.transpose(PT[:, i * 4:(i + 1) * 4], src[:, i * 128:(i + 1) * 128], ident[0:4, 0:4]),
                              reads=["srow", "ident"], writes=["PT"], sig=(i == n - 1))
                    cx.op("act", lambda e: e.activation(out=xT_s[:, dst_off:dst_off + n, :].rearrange("p a b -> p (a b)"),
                                                        in_=PT[:, 0:4 * n], func=AF.Copy), reads=["PT"], writes=["xT_s"])

                def s_linear(nk, wr, c0, ncols, xoff=0, wkey=None):
                    bi = rg.next()
                    for k0 in range(0, nk, 8):
                        kk = min(8, nk - k0)
                        wi = r_ws.next()
                        wk = "wslS_%d" % wi
                        if wkey is None:
                            cx.dma("pool", wslS[wi][:, 0:kk, 0:ncols], wr[:, k0:k0 + kk, c0:c0 + ncols], writes=[wk], skey=wk + "p")
                        else:
                            cx.dma("sp", wslS[wi][:, 0:kk, 0:ncols], wr[:, k0:k0 + kk, c0:c0 + ncols], reads=[("WS", wkey)],
                                   writes=[wk], skey=wk)
                        for k in range(kk):
                            cx.op("pe", lambda e, k=k, k0=k0, wi=wi, bi=bi: e.matmul(
                                pg[bi][0:4, 0:ncols], lhsT=xT_s[:, xoff + k0 + k, :], rhs=wslS[wi][:, k, 0:ncols],
                                start=(k0 + k == 0), stop=(k0 + k == nk - 1)),
                                reads=[wk, "xT_s"], writes=["pgS%d" % bi], sig=(k0 + k == nk - 1))
                    return pg[bi][0:4, 0:ncols], "pgS%d" % bi

                def s_ln(src, dst, gi):
                    for hh in range(2):
                        cx.op("dve", lambda e, hh=hh: e.bn_stats(out=sm4[:, hh * 6:(hh + 1) * 6], in_=src[:, hh * 512:(hh + 1) * 512]),
                              reads=["srow"], writes=["sm4"])
                    cx.op("dve", lambda e: e.bn_aggr(out=sm4[:, 12:14], in_=sm4[:, 0:12]), reads=["sm4"], writes=["sm4"])
                    cx.op("dve", lambda e: e.tensor_scalar(out=sm4[:, 14:15], in0=sm4[:, 13:14], scalar1=LN_EPS, scalar2=None, op0=ALU.add),
                          reads=["sm4"], writes=["sm4"])
                    cx.op("act", lambda e: e.activation(out=sm4[:, 14:15], in_=sm4[:, 14:15], func=AF.Sqrt), reads=["sm4"], writes=["sm4"])
                    cx.op("dve", lambda e: e.reciprocal(out=sm4[:, 14:15], in_=sm4[:, 14:15]), reads=["sm4"], writes=["sm4"])
                    cx.op("dve", lambda e: e.tensor_scalar(out=row[5][:], in0=src, scalar1=sm4[:, 12:13], scalar2=sm4[:, 14:15],
                                                           op0=ALU.subtract, op1=ALU.mult), reads=["srow", "sm4"], writes=["srow"])
                    cx.op("dve", lambda e: e.tensor_tensor(out=dst, in0=row[5][:], in1=lnr[:, gi, :], op=ALU.mult),
                          reads=["srow", "lnr"], writes=["srow"])
                    cx.op("dve", lambda e: e.tensor_tensor(out=dst, in0=dst, in1=lnr[:, gi + 1, :], op=ALU.add),
                          reads=["srow", "lnr"], writes=["srow"])

                SR = ["srow"]
                cx.op("dve", lambda e: e.scalar_tensor_tensor(out=row[0][:], in0=modtm[0:4, D:2 * D], scalar=1.0, in1=xs_sb[:],
                                                              op0=ALU.add, op1=ALU.mult), reads=["modtm", "xs_sb"], writes=SR)
                cx.op("dve", lambda e: e.tensor_tensor(out=row[0][:], in0=row[0][:], in1=modtm[0:4, 0:D], op=ALU.add),
                      reads=["modtm"] + SR, writes=SR)
                s_transpose(row[0], 8, 0)
                for c0 in range(0, 7424, 512):
                    ncols = min(512, 7424 - c0)
                    pp, pk_ = s_linear(8, WIN_bf, c0, ncols, wkey="in")
                    cx.op("act", lambda e, pp=pp, c0=c0, ncols=ncols: e.activation(out=proj[:, c0:c0 + ncols], in_=pp, func=AF.Copy),
                          reads=[pk_], writes=["proj"])
                qk24 = proj[:, 0:1536].rearrange("p (h c) -> p h c", c=64)
                rA_s = big[:, 0:384].rearrange("p (h c) -> p h c", c=16)
                rB_s = big[:, 384:768].rearrange("p (h c) -> p h c", c=16)
                cx.op("dve", lambda e: e.tensor_tensor(out=rA_s, in0=qk24[:, :, 0:16], in1=tabAs[:, None, 0:16].to_broadcast([4, 24, 16]),
                                                       op=ALU.mult), reads=["proj", "tabAs"], writes=["big"])
                for hf in range(2):
                    cx.op("dve", lambda e, hf=hf: e.tensor_tensor(
                        out=rB_s[:, :, 8 * hf:8 * hf + 8], in0=qk24[:, :, 8 - 8 * hf:16 - 8 * hf],
                        in1=tabAs[:, None, 16 + 8 * hf:24 + 8 * hf].to_broadcast([4, 24, 8]), op=ALU.mult),
                        reads=["proj", "tabAs"], writes=["big"])
                cx.op("dve", lambda e: e.tensor_tensor(out=qk24[:, :, 0:16], in0=rA_s, in1=rB_s, op=ALU.add), reads=["big"], writes=["proj"])
                rqk8 = proj[:, 2304:3328].rearrange("p (h c) -> p h c", c=128)
                rA_r = big[:, 0:1024].rearrange("p (h c) -> p h c", c=128)
                rB_r = big[:, 1024:2048].rearrange("p (h c) -> p h c", c=128)
                cx.op("dve", lambda e: e.tensor_tensor(out=rA_r, in0=rqk8, in1=tabRs[:, None, 0:128].to_broadcast([4, 8, 128]), op=ALU.mult),
                      reads=["proj", "tabRs"], writes=["big"])
                for hf in range(2):
                    cx.op("dve", lambda e, hf=hf: e.tensor_tensor(
                        out=rB_r[:, :, 64 * hf:64 * hf + 64], in0=rqk8[:, :, 64 - 64 * hf:128 - 64 * hf],
                        in1=tabRs[:, None, 128 + 64 * hf:192 + 64 * hf].to_broadcast([4, 8, 64]), op=ALU.mult),
                        reads=["proj", "tabRs"], writes=["big"])
                cx.op("dve", lambda e: e.tensor_tensor(out=rqk8, in0=rA_r, in1=rB_r, op=ALU.add), reads=["big"], writes=["proj"])
                cx.op("dve", lambda e: e.tensor_scalar(out=proj[:, 2816:3328], in0=proj[:, 2816:3328], scalar1=float(128.0 ** -0.5),
                                                       scalar2=None, op0=ALU.mult), reads=["proj"], writes=["proj"])
                with cx.group("s_new"):
                    for g, Lg in enumerate(WINS):
                        cx.dma("sp", sk[g][:, Lg - 1, :], proj[:, 768 + g * 256:768 + (g + 1) * 256], reads=["proj"])
                        cx.dma("sp", sv[g][:, Lg - 1, :], proj[:, 1536 + g * 256:1536 + (g + 1) * 256], reads=["proj"])
                cx.op("dve", lambda e: e.memset(UL[:], 0.0), writes=["UL"])
                cx.op("dve", lambda e: e.memset(ones_f[:], 1.0), writes=["ones_f"])
                cx.op("dve", lambda e: e.memset(VE[:], 1.0), writes=["VE"])
                r_kv = Rot(4)
                for g, Lg in enumerate(WINS):
                    dil = DILS[g]
                    cx.dma("sp", KE[g * 4:(g + 1) * 4, :], proj[:, 768 + g * 256:768 + (g + 1) * 256], reads=["proj"], writes=["KE"],
                           skey="s_KE")
                    cx.dma("sp", VE[g * 4:(g + 1) * 4, :, 0:64],
                           proj[:, 1536 + g * 256:1536 + (g + 1) * 256].rearrange("s (h d) -> s h d", d=64), reads=["VE", "proj"],
                           writes=["VE"], skey="s_VE")
                    cx.dma("sp", qE[g * 4:(g + 1) * 4, :], proj[:, g * 256:(g + 1) * 256], reads=["proj"], writes=["qE"], skey="s_qE")
                def s_front(s_, g):
                    Lg, dil = WINS[g], DILS[g]
                    ki = r_kv.next()
                    kk_, vk_ = "Ksel%d" % ki, "Vsel%d" % ki
                    cx.dma("sp", Ksel[ki][:, :], ck[g][s_, 0:Lg - dil + 1:dil, :], writes=[kk_], skey=kk_)
                    cx.dma("act", Vsel[ki][:, :], cv[g][s_, 0:Lg - dil + 1:dil, :], writes=[vk_], skey=vk_)
                    cx.op("pe", lambda e, s_=s_, g=g: e.matmul(QB[:, 0:256], lhsT=selS[:, s_, :], rhs=proj[:, g * 256:(g + 1) * 256],
                                                               start=True, stop=True), reads=["selS", "proj"], writes=["QB"])
                    cx.op("dve", lambda e, ki=ki: e.tensor_tensor(out=prod[ki][:], in0=QB[:, 0:256], in1=Ksel[ki][:], op=ALU.mult),
                          reads=["QB", kk_], writes=["prod%d" % ki])
                    cx.op("dve", lambda e, ki=ki: e.tensor_reduce(out=scs[ki][:], in_=prod[ki][:].rearrange("p (h d) -> p h d", d=64),
                                                                  axis=mybir.AxisListType.X, op=ALU.add),
                          reads=["prod%d" % ki], writes=["scs%d" % ki])
                    cx.op("act", lambda e, ki=ki: e.activation(out=scs[ki][:], in_=scs[ki][:], func=AF.Exp, scale=0.125),
                          reads=["scs%d" % ki], writes=["scs%d" % ki])
                    cx.op("dve", lambda e, ki=ki, s_=s_: e.tensor_tensor(
                        out=Pm[ki][:], in0=scs[ki][:, :, None].to_broadcast([128, 4, 4]), in1=oh[:, s_, :, :], op=ALU.mult),
                        reads=["scs%d" % ki, "oh"], writes=["Pm%d" % ki])
                    return ki

                def s_back(ki):
                    for h in range(4):
                        cx.op("pe", lambda e, h=h, ki=ki: e.matmul(
                            UL[0:4, h * 65:h * 65 + 64], lhsT=Pm[ki][:, h, :], rhs=Vsel[ki][:, h * 64:(h + 1) * 64], start=False,
                            stop=False, skip_group_check=True), reads=["Pm%d" % ki, "Vsel%d" % ki], writes=["UL"], sig=False)
                        cx.op("pe", lambda e, h=h, ki=ki: e.matmul(
                            UL[0:4, h * 65 + 64:h * 65 + 65], lhsT=Pm[ki][:, h, :], rhs=ones_f[:, 0:1], start=False,
                            stop=False, skip_group_check=True), reads=["Pm%d" % ki, "ones_f"], writes=["UL"], sig=(h == 3))

                prev_ = None
                for s_ in range(4):
                    for g in range(3):
                        cur_ = s_front(s_, g)
                        if prev_ is not None:
                            s_back(prev_)
                        prev_ = cur_
                s_back(prev_)
                cx.op("dve", lambda e: e.tensor_tensor(out=prod[0][0:12, :], in0=qE[:], in1=KE[:], op=ALU.mult),
                      reads=["qE", "KE"], writes=["prod0"])
                cx.op("dve", lambda e: e.tensor_reduce(out=scs[0][0:12, :], in_=prod[0][0:12, :].rearrange("p (h d) -> p h d", d=64),
                                                       axis=mybir.AxisListType.X, op=ALU.add), reads=["prod0"], writes=["scs0"])
                cx.op("act", lambda e: e.activation(out=scs[0][0:12, :], in_=scs[0][0:12, :], func=AF.Exp, scale=0.125),
                      reads=["scs0"], writes=["scs0"])
                cx.op("dve", lambda e: e.tensor_tensor(out=Pm[0][0:12, :, :], in0=scs[0][0:12, :, None].to_broadcast([12, 4, 4]),
                                                       in1=ohE[:], op=ALU.mult), reads=["scs0", "ohE"], writes=["Pm0"])
                for h in range(4):
                    cx.op("pe", lambda e, h=h: e.matmul(UL[0:4, h * 65:(h + 1) * 65], lhsT=Pm[0][0:12, h, :], rhs=VE[:, h, :],
                                                        start=False, stop=False, skip_group_check=True),
                          reads=["Pm0", "VE"], writes=["UL"], sig=(h == 3))
                ULv = UL[0:4, 0:260].rearrange("p (h c) -> p h c", c=65)
                cx.op("dve", lambda e: e.reciprocal(out=sm4[:, 16:20], in_=ULv[:, :, 64]), reads=["UL"], writes=["sm4"])
                cx.op("dve", lambda e: e.tensor_tensor(out=row[1][:, 0:256].rearrange("p (h d) -> p h d", d=64), in0=ULv[:, :, 0:64],
                                                       in1=sm4[:, 16:20, None].to_broadcast([4, 4, 64]), op=ALU.mult),
                      reads=["UL", "sm4"], writes=SR)
                s_transpose(row[1], 2, 8)
                rqv = proj[:, 2304:2816]
                rkv = proj[:, 2816:3328]
                for i in range(4):
                    cx.op("pe", lambda e, i=i: e.transpose(PT[:, 64 + i * 4:64 + (i + 1) * 4], rqv[:, i * 128:(i + 1) * 128], ident[0:4, 0:4]),
                          reads=["proj", "ident"], writes=["PT"], sig=(i == 3))
                cx.op("act", lambda e: e.activation(out=qTr[:].rearrange("p a b -> p (a b)"), in_=PT[:, 64:80], func=AF.Copy),
                      reads=["PT"], writes=["qTr"])
                cx.op("dve", lambda e: e.tensor_tensor(out=qTm[:], in0=qTr[:, :, None, :].to_broadcast([128, 4, 4, 4]),
                                                       in1=oh[:].rearrange("p s h c -> p h s c"), op=ALU.mult),
                      reads=["qTr", "oh"], writes=["qTm"])
                cx.op("dve", lambda e: e.memset(QR[:], 0.0), writes=["QR0", "QR1"])
                for h in range(4):
                    for s_ in range(4):
                        cx.op("pe", lambda e, h=h, s_=s_: e.matmul(
                            QR[0:4, h * 256:(h + 1) * 256], lhsT=qTm[:, h, s_, :], rhs=Rst[:, s_ * 4 + h, :], start=False, stop=False,
                            skip_group_check=True), reads=["qTm", "Rst"], writes=["QR%d" % (h // 2)], sig=(s_ == 3))
                cx.op("dve", lambda e: e.tensor_tensor(out=big[:, 0:512], in0=rqv, in1=rkv, op=ALU.mult), reads=["proj"], writes=["big"])
                cx.op("dve", lambda e: e.tensor_reduce(out=sm4[:, 20:24], in_=big[:, 0:512].rearrange("p (h d) -> p h d", d=128),
                                                       axis=mybir.AxisListType.X, op=ALU.add), reads=["big"], writes=["sm4"])
                rvv = proj[:, 3328:4352]
                for h in range(4):
                    cx.op("dve", lambda e, h=h: e.tensor_scalar(out=row[2][:, h * 256:(h + 1) * 256], in0=rvv[:, h * 256:(h + 1) * 256],
                                                                scalar1=sm4[:, 20 + h:21 + h], scalar2=None, op0=ALU.mult),
                          reads=["proj", "sm4"], writes=SR)
                    cx.op("dve", lambda e, h=h: e.scalar_tensor_tensor(
                        out=row[2][:, h * 256:(h + 1) * 256], in0=QR[0:4, h * 256:(h + 1) * 256], scalar=float(GAM[h]),
                        in1=row[2][:, h * 256:(h + 1) * 256], op0=ALU.mult, op1=ALU.add),
                        reads=["QR%d" % (h // 2)] + SR, writes=SR)
                for s_ in range(4):
                    cx.op("dve", lambda e, s_=s_: e.tensor_scalar(out=kM[:, s_, :], in0=rkv, scalar1=oh4[:, s_:s_ + 1], scalar2=None,
                                                                  op0=ALU.mult), reads=["proj", "oh4"], writes=["kM"])
                for s_ in range(4):
                    for h in range(4):
                        bk_ = (s_ * 4 + h) % 3
                        pb_, pk__ = ((QB[:, 256:512], "QB"), (pg[0][:, 0:256], "pgS0"), (pg[1][:, 0:256], "pgS1"))[bk_]
                        rk_ = ("Rst", s_ * 4 + h)
                        cx.op("pe", lambda e, s_=s_, h=h, pb_=pb_: e.matmul(pb_, lhsT=kM[:, s_, h * 128:(h + 1) * 128],
                                                                          rhs=rvv[:, h * 256:(h + 1) * 256], start=True, stop=True),
                              reads=["kM", "proj"], writes=[pk__])
                        cx.op("dve", lambda e, s_=s_, h=h, pb_=pb_: e.scalar_tensor_tensor(
                            out=Rst[:, s_ * 4 + h, :], in0=Rst[:, s_ * 4 + h, :], scalar=float(GAM[h]), in1=pb_,
                            op0=ALU.mult, op1=ALU.add), reads=[pk__, "Rst", rk_], writes=[rk_])
                cx.dma("sp", s_state.rearrange("s h p c -> p (s h) c"), Rst[:], reads=["Rst"] + [("Rst", i) for i in range(16)], skey="s_Rout")
                for h in range(4):
                    cx.op("dve", lambda e, h=h: e.bn_stats(out=sm4[:, 24 + h * 6:30 + h * 6], in_=row[2][:, h * 256:(h + 1) * 256]),
                          reads=SR, writes=["sm4"])
                    cx.op("dve", lambda e, h=h: e.bn_aggr(out=sm4[:, 48 + 2 * h:50 + 2 * h], in_=sm4[:, 24 + h * 6:30 + h * 6]),
                          reads=["sm4"], writes=["sm4"])
                    cx.op("dve", lambda e, h=h: e.tensor_scalar(out=sm4[:, 56 + h:57 + h], in0=sm4[:, 49 + 2 * h:50 + 2 * h], scalar1=GN_EPS,
                                                                scalar2=None, op0=ALU.add), reads=["sm4"], writes=["sm4"])
                cx.op("act", lambda e: e.activation(out=sm4[:, 56:60], in_=sm4[:, 56:60], func=AF.Sqrt), reads=["sm4"], writes=["sm4"])
                cx.op("dve", lambda e: e.reciprocal(out=sm4[:, 56:60], in_=sm4[:, 56:60]), reads=["sm4"], writes=["sm4"])
                for h in range(4):
                    cx.op("dve", lambda e, h=h: e.tensor_scalar(out=row[2][:, h * 256:(h + 1) * 256], in0=row[2][:, h * 256:(h + 1) * 256],
                                                                scalar1=sm4[:, 48 + 2 * h:49 + 2 * h], scalar2=sm4[:, 56 + h:57 + h],
                                                                op0=ALU.subtract, op1=ALU.mult), reads=["sm4"] + SR, writes=SR)
                cx.op("act", lambda e: e.activation(out=row[3][:], in_=proj[:, 4352:5376], func=AF.Silu), reads=["proj"], writes=SR)
                cx.op("dve", lambda e: e.tensor_tensor(out=row[2][:], in0=row[2][:], in1=row[3][:], op=ALU.mult), reads=SR, writes=SR)
                s_transpose(row[2], 8, 10)
                w_ao_r = w_att_out.rearrange("(k p) n -> p k n", p=128)
                w_ro_r2 = w_ret_out.rearrange("(k p) n -> p k n", p=128)
                w_o_r2 = w_o.rearrange("(k p) n -> p k n", p=128)
                cx.op("act", lambda e: e.activation(out=row[3][:], in_=proj[:, 5376:6400], func=AF.Sigmoid), reads=["proj"], writes=SR)
                cx.op("act", lambda e: e.activation(out=row[4][:], in_=proj[:, 6400:7424], func=AF.Sigmoid), reads=["proj"], writes=SR)
                for hh in range(2):
                    hs = slice(hh * 512, (hh + 1) * 512)
                    pp, pk_ = s_linear(2, w_ao_r, hh * 512, 512, xoff=8)
                    cx.op("dve", lambda e, pp=pp, hs=hs: e.tensor_tensor(out=row[3][:, hs], in0=pp, in1=row[3][:, hs], op=ALU.mult),
                          reads=[pk_] + SR, writes=SR)
                    pp, pk_ = s_linear(8, WRO_bf, hh * 512, 512, xoff=10, wkey="ro")
                    cx.op("dve", lambda e, pp=pp, hs=hs: e.tensor_tensor(out=row[4][:, hs], in0=pp, in1=row[4][:, hs], op=ALU.mult),
                          reads=[pk_] + SR, writes=SR)
                cx.op("dve", lambda e: e.tensor_tensor(out=row[3][:], in0=row[3][:], in1=row[4][:], op=ALU.add), reads=SR, writes=SR)
                s_transpose(row[3], 8, 0)
                for hh in range(2):
                    hs = slice(hh * 512, (hh + 1) * 512)
                    pp, pk_ = s_linear(8, WO_bf, hh * 512, 512, xoff=0, wkey="o")
                    cx.op("dve", lambda e, pp=pp, hs=hs: e.tensor_tensor(out=row[4][:, hs], in0=pp, in1=modtm[0:4, 2 * D + hh * 512:2 * D + (hh + 1) * 512],
                                                                         op=ALU.mult), reads=[pk_, "modtm"] + SR, writes=SR)
                cx.op("dve", lambda e: e.scalar_tensor_tensor(out=row[4][:], in0=xs_sb[:], scalar=float(ALPHA), in1=row[4][:],
                                                              op0=ALU.mult, op1=ALU.add), reads=["xs_sb"] + SR, writes=SR)
                s_ln(row[4][:], row[1][:], 0)
                cx.op("dve", lambda e: e.scalar_tensor_tensor(out=row[0][:], in0=modtm[0:4, 4 * D:5 * D], scalar=1.0, in1=row[1][:],
                                                              op0=ALU.add, op1=ALU.mult), reads=["modtm"] + SR, writes=SR)
                cx.op("dve", lambda e: e.tensor_tensor(out=row[0][:], in0=row[0][:], in1=modtm[0:4, 3 * D:4 * D], op=ALU.add),
                      reads=["modtm"] + SR, writes=SR)
                s_transpose(row[0], 8, 0)
                w_fi_r2 = w_ffn_in.rearrange("(k p) n -> p k n", p=128)
                w_fo_r2 = w_ffn_out.rearrange("(k p) n -> p k n", p=128)
                for c0 in range(0, 2 * DFF, 512):
                    pp, pk_ = s_linear(8, WFI_bf, c0, 512, xoff=0, wkey="fi")
                    ng = max(0, min(512, DFF - c0))
                    if ng > 0:
                        cx.op("act", lambda e, pp=pp, c0=c0, ng=ng: e.activation(out=big[:, c0:c0 + ng], in_=pp[:, 0:ng], func=AF.Silu),
                              reads=[pk_], writes=["big"])
                    if ng < 512:
                        cx.op("act", lambda e, pp=pp, c0=c0, ng=ng: e.activation(out=big[:, c0 + ng:c0 + 512], in_=pp[:, ng:512],
                                                                                 func=AF.Copy), reads=[pk_], writes=["big"])
                cx.op("dve", lambda e: e.tensor_tensor(out=big[:, 0:DFF], in0=big[:, 0:DFF], in1=big[:, DFF:2 * DFF], op=ALU.mult),
                      reads=["big"], writes=["big"])
                for i0 in range(0, 22, 8):
                    n_ = min(8, 22 - i0)
                    for i in range(n_):
                        cx.op("pe", lambda e, i=i, i0=i0: e.transpose(PT[:, i * 4:(i + 1) * 4], big[:, (i0 + i) * 128:(i0 + i + 1) * 128],
                                                                     ident[0:4, 0:4]), reads=["big", "ident"], writes=["PT"], sig=(i == n_ - 1))
                    cx.op("act", lambda e, i0=i0, n_=n_: e.activation(out=xT_s[:, i0:i0 + n_, :].rearrange("p a b -> p (a b)"),
                                                                      in_=PT[:, 0:4 * n_], func=AF.Copy), reads=["PT"], writes=["xT_s"])
                for hh in range(2):
                    hs = slice(hh * 512, (hh + 1) * 512)
                    pp, pk_ = s_linear(22, WFO_bf, hh * 512, 512, xoff=0, wkey="fo")
                    cx.op("dve", lambda e, pp=pp, hs=hs, hh=hh: e.tensor_tensor(
                        out=row[4][:, hs], in0=pp, in1=modtm[0:4, 5 * D + hh * 512:5 * D + (hh + 1) * 512], op=ALU.mult),
                        reads=[pk_, "modtm"] + SR, writes=SR)
                cx.op("dve", lambda e: e.scalar_tensor_tensor(out=row[4][:], in0=row[1][:], scalar=float(ALPHA), in1=row[4][:],
                                                              op0=ALU.mult, op1=ALU.add), reads=SR, writes=SR)
                s_ln(row[4][:], row[2][:], 2)
                cx.dma("sp", y_s, row[2][:], reads=SR, skey="s_yout")
            cx.barrier()

        with contextlib.ExitStack() as s2:
          if STAGE >= 3:
            NW = 5
            wsl = [sb("wsl%d" % i, [128, 8, 512], BF16, s2) for i in range(NW)]
            wao = sb("wao", [128, 2, D], BF16, s2)
            LG1 = sb("LG1", [128, D], F32, s2)
            LB1 = sb("LB1", [128, D], F32, s2)
            xt4s = [sb("xt4b%d" % i, [128, 4, D], F32, s2) for i in range(2)]
            hT = sb("hT", [128, 8, 512], BF16, s2)
            oTbs = [sb("oTb%d" % i, [128, 2, 512], BF16, s2) for i in range(2)]
            rqk = sb("rqk", [128, 4, 2, 512], BF16, s2)
            rvS = sb("rvS", [128, 4, D], BF16, s2)
            rgS = sb("rgS", [128, 4, D], BF16, s2)
            ropeTs = [sb("ropeT%d" % i, [128, 4, 256], F32, s2) for i in range(2)]
            tmp = [sb("tmp%d" % i, [128, D], F32, s2) for i in range(4)]
            x1o = [sb("x1o%d" % i, [128, D], F32, s2) for i in range(2)]
            qkT = [sb("qkT%d" % i, [128, 8, 128], BF16, s2) for i in range(2)]
            sm = [sb("sm%d" % i, [128, 4, 128], BF16, s2) for i in range(2)]
            cmask = sb("cmask", [128, 128], BF16, s2)
            R = sb("R", [128, 4, 256], F32, s2)
            Rb = sb("Rb", [128, 4, 256], BF16, s2)
            stt = sb("stt", [128, 24], F32, s2)
            mv = sb("mv", [128, 8], F32, s2)
            rstd = sb("rstd", [128, 4], F32, s2)
            nmr = sb("nmr", [128, 4], F32, s2)
            stt2 = sb("stt2", [128, 24], F32, s2)
            mv2 = sb("mv2", [128, 8], F32, s2)
            rstd2 = sb("rstd2", [128, 4], F32, s2)
            gtok = [sb("gtok%d" % i, [128, D], BF16, s2) for i in range(2)]
            gT = sb("gT", [128, 8, 512], BF16, s2)
            mT = sb("mT", [128, 8, 512], BF16, s2)
            sg = [sb("sg%d" % i, [128, 512], F32, s2) for i in range(2)]
            tt = [sb("tt%d" % i, [128, 512], F32, s2) for i in range(2)]
            h2t = [sb("h2t%d" % i, [128, 8, 128], BF16, s2) for i in range(2)]
            A = [ps("A%d" % i, [128, 512], F32, s2) for i in range(2)]
            S = ps("S", [128, 512], F32, s2)
            O = ps("O", [128, 1024], F32, s2)
            Dl = ps("Dl", [128, 1024], F32, s2)
            H = ps("H", [128, 1024], BF16, s2)
            cx.psum_keys.update(["O0", "O1", "D0", "D1"])
            AK = ["A0", "A1"]
            PB = [A[0][:, :], A[1][:, :], O[:, 0:512], O[:, 512:1024], Dl[:, 0:512], Dl[:, 512:1024]]
            PBK = ["A0", "A1", "O0", "O1", "D0", "D1"]
            r_a6, r_tp = Rot(6), Rot(2)
            with cx.group("p2c"):
                cx.dma("pool", cmask[:], cin["cmask"], writes=["cmask"])
                cx.dma("pool", wao[:], w_att_out.rearrange("(a p) n -> p a n", p=128), writes=["wao"])
                cx.dma("sp", LG1[:], lnrep["lng1"], writes=["LG1"])
                cx.dma("sp", LB1[:], lnrep["lnb1"], writes=["LB1"])
            cx.op("dve", lambda e: e.memset(R[:], 0.0), writes=["R"])
            cx.op("dve", lambda e: e.memset(Rb[:], 0.0), writes=["Rb"])
            r_w, r_a, r_q, r_s, r_g, r_sg, r_x1, r_h2 = Rot(NW), Rot(2), Rot(2), Rot(2), Rot(2), Rot(2), Rot(2), Rot(2)

            def load_slab(src_ap, wkey):
                wi = r_w.next()
                cx.dma("sp", wsl[wi][:], src_ap, reads=[("WS", wkey)], writes=["wsl%d" % wi], skey="wsl%d" % wi)
                return wi

            def w_in_slab(c0):
                return load_slab(WIN_bf[:, :, c0:c0 + 512], "in")

            w_ro_r = w_ret_out.rearrange("(k p) n -> p k n", p=128)
            w_o_r = w_o.rearrange("(k p) n -> p k n", p=128)
            NBLK = 8 if STAGE >= 3.9 else 1
            for b in range(NBLK):
                tok0 = b * 512
                pb = b % 2
                xt4, oTb, ropeT = xt4s[pb], oTbs[pb], ropeTs[pb]
                xk = ["xt4b%d_%d" % (pb, t) for t in range(4)]
                with cx.group("p2in%d" % pb):
                    for t in range(4):
                        cx.dma("sp", xt4[:, t, :], x[tok0 + t * 128:tok0 + (t + 1) * 128, :], writes=[xk[t]])
                        cx.dma("sp", ropeT[:, t, :], cin["ropeR"][tok0 + t * 128:tok0 + (t + 1) * 128, :],
                               writes=[("ropeT", pb, t)])
                    cx.dma("sp", oTb[:], oT_scr[:, :, tok0:tok0 + 512], reads=[("oT_scr", b)], writes=["oTb%d" % pb])
                cx.dma("sp", hT[:, :, :], hT_scr[:, :, tok0:tok0 + 512], reads=[("hT_scr", b)], writes=["hT"], skey="hTld_p2")
                slabs = [("q", 2304), ("k", 2816), ("v0", 3328), ("v1", 3840), ("g0", 4352), ("g1", 4864)]
                for (kind, c0) in slabs:
                    wi = w_in_slab(c0)
                    wk = "wsl%d" % wi
                    for t in range(4):
                        ai = r_a6.next()
                        tp_ = (r_tp.next()) * 2
                        for k in range(8):
                            cx.op("pe", lambda e, k=k, t=t, ai=ai, wi=wi: e.matmul(
                                PB[ai], lhsT=hT[:, k, t * 128:(t + 1) * 128], rhs=wsl[wi][:, k, :],
                                start=(k == 0), stop=(k == 7)),
                                reads=["hT", wk], writes=[PBK[ai]], sig=(k == 7))
                        if kind in ("q", "k"):
                            a = 0 if kind == "q" else 1
                            pv4 = PB[ai].rearrange("p (h c) -> p h c", h=4)
                            tA = tmp[tp_][:, 0:512].rearrange("p (h c) -> p h c", h=4)
                            tB = tmp[tp_ + 1][:, 0:512].rearrange("p (h c) -> p h c", h=4)
                            cx.op("dve", lambda e, pv4=pv4, tA=tA, t=t, tp_=tp_: e.tensor_tensor(
                                out=tA, in0=pv4, in1=ropeT[:, t:t + 1, 0:128].to_broadcast([128, 4, 128]), op=ALU.mult),
                                reads=[PBK[ai], ("ropeT", pb, t)], writes=["tmp%d" % tp_])
                            for hf in range(2):
                                cx.op("dve", lambda e, pv4=pv4, tB=tB, t=t, hf=hf, tp_=tp_: e.tensor_tensor(
                                    out=tB[:, :, 64 * hf:64 * hf + 64], in0=pv4[:, :, 64 - 64 * hf:128 - 64 * hf],
                                    in1=ropeT[:, t:t + 1, 128 + 64 * hf:192 + 64 * hf].to_broadcast([128, 4, 64]), op=ALU.mult),
                                    reads=[PBK[ai], ("ropeT", pb, t)], writes=["tmp%d" % (tp_ + 1)])
                            cx.op("dve", lambda e, tp_=tp_: e.tensor_tensor(out=tmp[tp_][:, 0:512], in0=tmp[tp_][:, 0:512],
                                                                    in1=tmp[tp_ + 1][:, 0:512], op=ALU.add),
                                  reads=["tmp%d" % tp_, "tmp%d" % (tp_ + 1)], writes=["tmp%d" % tp_])
                            for h in range(4):
                                cx.op("act", lambda e, h=h, a=a, t=t, tp_=tp_: e.activation(
                                    out=rqk[:, t, a, h * 128:(h + 1) * 128], in_=tmp[tp_][:, h * 128:(h + 1) * 128],
                                    func=AF.Copy, scale=dec[:, a * 4 + h:a * 4 + h + 1]),
                                    reads=["tmp%d" % tp_, "dec"], writes=[("rqk", t, a)])
                        elif kind[0] == "v":
                            j = int(kind[1])
                            cx.op("act", lambda e, t=t, j=j, ai=ai: e.activation(
                                out=rvS[:, t, j * 512:(j + 1) * 512], in_=PB[ai], func=AF.Copy),
                                reads=[PBK[ai]], writes=[("rvS", t, j)])
                        else:
                            j = int(kind[1])
                            cx.op("act", lambda e, t=t, j=j, ai=ai: e.activation(
                                out=rgS[:, t, j * 512:(j + 1) * 512], in_=PB[ai], func=AF.Silu),
                                reads=[PBK[ai]], writes=[("rgS", t, j)])
                def stage_A1(t):
                    for a in range(2):
                        for h in range(4):
                            cx.op("pe", lambda e, a=a, h=h, t=t: e.transpose(
                                H[:, (a * 4 + h) * 128:(a * 4 + h + 1) * 128], rqk[:, t, a, h * 128:(h + 1) * 128], ident_b[:]),
                                reads=[("rqk", t, a), "ident_b"], writes=["H"], sig=(a == 1 and h == 3))
                    qi = r_q.next()
                    qkk = "qkT%d" % qi
                    cx.op("act", lambda e, qi=qi: e.activation(out=qkT[qi][:].rearrange("p a b -> p (a b)"), in_=H[:, :], func=AF.Copy),
                          reads=["H"], writes=[qkk])
                    for h in range(4):
                        cx.op("pe", lambda e, h=h, qi=qi: e.matmul(
                            S[:, h * 128:(h + 1) * 128], lhsT=qkT[qi][:, 4 + h, :], rhs=qkT[qi][:, h, :], start=True, stop=True),
                            reads=[qkk], writes=["S"], sig=(h == 3))
                    si = r_s.next()
                    cx.op("dve", lambda e, si=si: e.tensor_tensor(
                        out=sm[si][:], in0=S[:, :].rearrange("p (h c) -> p h c", h=4),
                        in1=cmask[:, None, :].to_broadcast([128, 4, 128]), op=ALU.mult),
                        reads=["S", "cmask"], writes=["sm%d" % si])
                    return (qi, si)

                def stage_A2(t, st):
                    gt = 4 * b + t
                    qi, si = st
                    qkk = "qkT%d" % qi
                    for h in range(4):
                        ok = "O%d" % (h // 2)
                        cx.op("pe", lambda e, h=h, si=si, t=t: e.matmul(
                            O[:, h * 256:(h + 1) * 256], lhsT=sm[si][:, h, :], rhs=rvS[:, t, h * 256:(h + 1) * 256],
                            start=True, stop=(gt == 0)),
                            reads=["sm%d" % si, ("rvS", t, 0), ("rvS", t, 1)], writes=[ok], sig=(gt == 0 and h % 2 == 1))
                        if gt > 0:
                            cx.op("pe", lambda e, h=h, qi=qi: e.matmul(
                                O[:, h * 256:(h + 1) * 256], lhsT=qkT[qi][:, h, :], rhs=Rb[:, h, :], start=False, stop=True),
                                reads=[qkk, "Rb"], writes=[ok], sig=(h % 2 == 1))
                    for h in range(4):
                        dk_ = "D%d" % (h // 2)
                        cx.op("pe", lambda e, h=h, t=t: e.matmul(
                            Dl[:, h * 256:(h + 1) * 256], lhsT=rqk[:, t, 1, h * 128:(h + 1) * 128],
                            rhs=rvS[:, t, h * 256:(h + 1) * 256], start=True, stop=True),
                            reads=[("rqk", t, 1), ("rvS", t, 0), ("rvS", t, 1)], writes=[dk_], sig=(h % 2 == 1))
                    ob = 1 + (t % 3)
                    cx.op("act", lambda e, ob=ob: e.activation(out=tmp[ob][:], in_=O[:, :], func=AF.Copy),
                          reads=["O0", "O1"], writes=["tmp%d" % ob])
                    cx.op("dve", lambda e: e.tensor_tensor(out=R[:].rearrange("p h c -> p (h c)"),
                                                           in0=R[:].rearrange("p h c -> p (h c)"), in1=Dl[:, :], op=ALU.add),
                          reads=["D0", "D1", "R"], writes=["R"])
                    for h in range(4):
                        cx.op("dve", lambda e, h=h: e.tensor_scalar(out=R[:, h, :], in0=R[:, h, :], scalar1=float(GC[h]),
                                                                    scalar2=None, op0=ALU.mult),
                              reads=["R"], writes=["R"])
                    cx.op("act", lambda e: e.activation(out=Rb[:].rearrange("p h c -> p (h c)"),
                                                        in_=R[:].rearrange("p h c -> p (h c)"), func=AF.Copy),
                          reads=["R"], writes=["Rb"])

                def stage_B(t):
                    ob = 1 + (t % 3)
                    okk = "tmp%d" % ob
                    for h in range(4):
                        cx.op("dve", lambda e, h=h, ob=ob: e.bn_stats(out=stt[:, h * 6:(h + 1) * 6], in_=tmp[ob][:, h * 256:(h + 1) * 256]),
                              reads=[okk], writes=["stt"])
                    for h in range(4):
                        cx.op("dve", lambda e, h=h: e.bn_aggr(out=mv[:, 2 * h:2 * h + 2], in_=stt[:, h * 6:(h + 1) * 6]),
                              reads=["stt"], writes=["mv"])
                    cx.op("dve", lambda e: e.tensor_scalar(out=rstd[:, :], in0=mv[:, :].rearrange("p (h c) -> p h c", c=2)[:, :, 1],
                                                           scalar1=GN_EPS, scalar2=None, op0=ALU.add),
                          reads=["mv"], writes=["rstd"])
                    cx.op("act", lambda e: e.activation(out=rstd[:, :], in_=rstd[:, :], func=AF.Sqrt), reads=["rstd"], writes=["rstd"])
                    cx.op("dve", lambda e: e.reciprocal(out=rstd[:, :], in_=rstd[:, :]), reads=["rstd"], writes=["rstd"])
                    for h in range(4):
                        cx.op("dve", lambda e, h=h, ob=ob: e.tensor_scalar(
                            out=tmp[ob][:, h * 256:(h + 1) * 256], in0=tmp[ob][:, h * 256:(h + 1) * 256],
                            scalar1=mv[:, 2 * h:2 * h + 1], scalar2=rstd[:, h:h + 1], op0=ALU.subtract, op1=ALU.mult),
                            reads=[okk, "mv", "rstd"], writes=[okk])
                    gi = r_g.next()
                    cx.op("dve", lambda e, gi=gi, t=t, ob=ob: e.tensor_tensor(out=gtok[gi][:], in0=tmp[ob][:], in1=rgS[:, t, :], op=ALU.mult),
                          reads=[okk, ("rgS", t, 0), ("rgS", t, 1)], writes=["gtok%d" % gi])
                    for c in range(8):
                        cx.op("pe", lambda e, c=c, gi=gi: e.transpose(
                            H[:, c * 128:(c + 1) * 128], gtok[gi][:, c * 128:(c + 1) * 128], ident_b[:]),
                            reads=["gtok%d" % gi, "ident_b"], writes=["H"], sig=(c == 7))
                    cx.op("act", lambda e, t=t: e.activation(
                        out=gT[:, :, t * 128:(t + 1) * 128], in_=H[:, :].rearrange("p (a b) -> p a b", a=8), func=AF.Copy),
                        reads=["H"], writes=[("gT", t)])

                st_ = stage_A1(0)
                for t in range(4):
                    nxt = stage_A1(t + 1) if t < 3 else None
                    stage_A2(t, st_)
                    if t > 1:
                        stage_B(t - 2)
                    st_ = nxt
                stage_B(2)
                stage_B(3)
                if b == NBLK - 1:
                    cx.dma("pool", p_state.rearrange("h p c -> p h c"), R[:], reads=["R"], skey="Rout")
                gTk = [("gT", t) for t in range(4)]
                for half in range(2):
                    wga = w_in_slab(5376 + half * 512)
                    wgb = w_in_slab(6400 + half * 512)
                    wro = load_slab(WRO_bf[:, :, half * 512:(half + 1) * 512], "ro")
                    for ncc in range(4):
                        n = half * 4 + ncc
                        cs_ = slice(ncc * 128, (ncc + 1) * 128)
                        for k in range(8):
                            cx.op("pe", lambda e, k=k, cs_=cs_, wga=wga: e.matmul(
                                A[1][:, :], lhsT=wsl[wga][:, k, cs_], rhs=hT[:, k, :], start=(k == 0), stop=(k == 7)),
                                reads=["wsl%d" % wga, "hT"], writes=["A1"], sig=(k == 7))
                        for k in range(8):
                            cx.op("pe", lambda e, k=k, cs_=cs_, wgb=wgb: e.matmul(
                                O[:, 512:1024], lhsT=wsl[wgb][:, k, cs_], rhs=hT[:, k, :], start=(k == 0), stop=(k == 7)),
                                reads=["wsl%d" % wgb, "hT"], writes=["O1"], sig=(k == 7))
                        for hp in range(2):
                            cx.op("pe", lambda e, hp=hp, n=n: e.matmul(
                                A[0][:, :], lhsT=wao[:, hp, n * 128:(n + 1) * 128], rhs=oTb[:, hp, :],
                                start=(hp == 0), stop=(hp == 1)),
                                reads=["wao", "oTb%d" % pb], writes=["A0"], sig=(hp == 1))
                        for k in range(8):
                            cx.op("pe", lambda e, k=k, cs_=cs_, wro=wro: e.matmul(
                                O[:, 0:512], lhsT=wsl[wro][:, k, cs_], rhs=gT[:, k, :], start=(k == 0), stop=(k == 7)),
                                reads=["wsl%d" % wro] + gTk, writes=["O0"], sig=(k == 7))
                        s0i, s1i = r_sg.next(), r_sg.next()
                        cx.op("act", lambda e, s0i=s0i: e.activation(out=sg[s0i][:], in_=A[1][:, :], func=AF.Sigmoid),
                              reads=["A1"], writes=["sg%d" % s0i])
                        cx.op("act", lambda e, s1i=s1i: e.activation(out=sg[s1i][:], in_=O[:, 512:1024], func=AF.Sigmoid),
                              reads=["O1"], writes=["sg%d" % s1i])
                        cx.op("dve", lambda e, s0i=s0i: e.tensor_tensor(out=tt[0][:], in0=A[0][:, :], in1=sg[s0i][:], op=ALU.mult),
                              reads=["A0", "sg%d" % s0i], writes=["tt0"])
                        cx.op("dve", lambda e, s1i=s1i: e.tensor_tensor(out=tt[1][:], in0=O[:, 0:512], in1=sg[s1i][:], op=ALU.mult),
                              reads=["O0", "sg%d" % s1i], writes=["tt1"])
                        cx.op("dve", lambda e, n=n: e.tensor_tensor(out=mT[:, n, :], in0=tt[0][:], in1=tt[1][:], op=ALU.add),
                              reads=["tt0", "tt1"], writes=[("mT", n)])
                mTk = [("mT", n) for n in range(8)]
                wo = [load_slab(WO_bf[:, :, hh * 512:(hh + 1) * 512], "o") for hh in range(2)]
                def o_mm(t):
                    for hh in range(2):
                        for c in range(8):
                            cx.op("pe", lambda e, c=c, hh=hh, t=t: e.matmul(
                                Dl[:, hh * 512:(hh + 1) * 512], lhsT=mT[:, c, t * 128:(t + 1) * 128], rhs=wsl[wo[hh]][:, c, :],
                                start=(c == 0), stop=(c == 7)),
                                reads=mTk + ["wsl%d" % wo[hh]], writes=["D%d" % hh], sig=(c == 7))

                o_mm(0)
                for t in range(4):
                    tok = tok0 + t * 128
                    ia, ic = (t % 2) * 2, (t % 2) * 2 + 1
                    ka, kc = "tmp%d" % ia, "tmp%d" % ic
                    cx.op("dve", lambda e, ia=ia: e.tensor_tensor(out=tmp[ia][:], in0=Dl[:, :], in1=G1[:], op=ALU.mult),
                          reads=["D0", "D1"] + GK[0], writes=[ka])
                    if t < 3:
                        o_mm(t + 1)
                    cx.op("dve", lambda e, t=t, ia=ia: e.scalar_tensor_tensor(out=tmp[ia][:], in0=xt4[:, t, :], scalar=float(ALPHA),
                                                                              in1=tmp[ia][:], op0=ALU.mult, op1=ALU.add),
                          reads=[xk[t], ka], writes=[ka])
                    ln_stats(tmp[ia], stt2, mv2, rstd2, ka, LN_EPS)
                    cx.op("dve", lambda e, ia=ia, ic=ic: e.tensor_scalar(out=tmp[ic][:], in0=tmp[ia][:], scalar1=mv2[:, 0:1],
                                                                         scalar2=rstd2[:, 0:1], op0=ALU.subtract, op1=ALU.mult),
                          reads=[ka, "lnmv", "lnrs"], writes=[kc])
                    xi = r_x1.next()
                    cx.op("dve", lambda e, xi=xi, ic=ic: e.tensor_tensor(out=x1o[xi][:], in0=tmp[ic][:], in1=LG1[:], op=ALU.mult),
                          reads=[kc, "LG1"], writes=["x1o%d" % xi])
                    cx.op("dve", lambda e, xi=xi: e.tensor_tensor(out=x1o[xi][:], in0=x1o[xi][:], in1=LB1[:], op=ALU.add),
                          reads=["x1o%d" % xi, "LB1"], writes=["x1o%d" % xi])
                    cx.dma("pool", x1_scr[tok:tok + 128, :], x1o[xi][:], reads=["x1o%d" % xi], writes=[("x1_scr", tok // 128)],
                           skey="x1o%d" % xi)
                    hi_ = r_h2.next()
                    for k in range(8):
                        ai = k // 4
                        cx.op("pe", lambda e, k=k, ai=ai, ic=ic: e.transpose(
                            A[ai][:, (k % 4) * 128:(k % 4 + 1) * 128], tmp[ic][:, k * 128:(k + 1) * 128], ident[:]),
                            reads=[kc, "ident"], writes=[AK[ai]], sig=(k % 4 == 3))
                    for k in range(8):
                        ai = k // 4
                        cx.op("act", lambda e, k=k, ai=ai, hi_=hi_: e.activation(
                            out=h2t[hi_][:, k, :], in_=A[ai][:, (k % 4) * 128:(k % 4 + 1) * 128], func=AF.Identity,
                            scale=a2T[:, k:k + 1], bias=b2T[:, k:k + 1]),
                            reads=[AK[ai], "a2T", "b2T"], writes=["h2t%d" % hi_])
                    cx.dma("act", h2T_scr[:, :, tok:tok + 128], h2t[hi_][:], reads=["h2t%d" % hi_],
                           writes=[("h2T_scr", tok // 512)], skey="h2t%d" % hi_)
                    if debug and b == 0 and t == 0:
                        cx.dma("sp", dbg["gT"], gT[:], reads=gTk, skey="dbgA")
                        cx.dma("sp", dbg["mT"], mT[:], reads=mTk, skey="dbgB")
            cx.barrier()

        with contextlib.ExitStack() as s3:
          if STAGE >= 4:
            NW = 6
            wsl = [sb("wslf%d" % i, [128, 8, 512], BF16, s3) for i in range(NW)]
            wfo = [sb("wfo%d" % i, [128, 11, 512], BF16, s3) for i in range(2)]
            LG2 = sb("LG2", [128, D], F32, s3)
            LB2 = sb("LB2", [128, D], F32, s3)
            h2Ts = [sb("h2T%d" % i, [128, 8, 512], BF16, s3) for i in range(2)]
            aT = sb("aT", [128, 22, 512], BF16, s3)
            sgt = [sb("sgt%d" % i, [128, 512], F32, s3) for i in range(2)]
            x1bs = [sb("x1b%d" % i, [128, 4, D], F32, s3) for i in range(2)]
            t2 = sb("t2", [128, 4, D], F32, s3)
            tmpf = sb("tmpf", [128, 512], F32, s3)
            xh = sb("xh", [128, D], F32, s3)
            yo = [sb("yo%d" % i, [128, D], F32, s3) for i in range(2)]
            stt = sb("stt3", [128, 24], F32, s3)
            mv = sb("mv3", [128, 8], F32, s3)
            rstd = sb("rstd3", [128, 4], F32, s3)
            FA = [ps("FA%d" % i, [128, 512], F32, s3) for i in range(4)]
            ACC = [ps("ACC%d" % i, [128, 512], F32, s3) for i in range(4)]
            with cx.group("p3c"):
                cx.dma("sp", LG2[:], lnrep["lng2"], writes=["LG2"])
                cx.dma("sp", LB2[:], lnrep["lnb2"], writes=["LB2"])
            r_w, r_f, r_fo, r_sg, r_y = Rot(NW), Rot(2), Rot(2), Rot(2), Rot(2)
            w_fi_r = w_ffn_in.rearrange("(k p) n -> p k n", p=128)
            w_fo_r = w_ffn_out.rearrange("(j p) n -> p j n", p=128)
            NBLK = 8 if STAGE >= 4.9 else 1
            for b in range(NBLK):
                tok0 = b * 512
                pb = b % 2
                h2T, x1b = h2Ts[pb], x1bs[pb]
                hk = "h2T%d" % pb
                cx.dma("sp", h2T[:], h2T_scr[:, :, tok0:tok0 + 512], reads=[("h2T_scr", b)], writes=[hk], skey=hk)
                with cx.group("x1b%d" % pb):
                    for t in range(4):
                        cx.dma("sp", x1b[:, t, :], x1_scr[tok0 + t * 128:tok0 + (t + 1) * 128, :],
                               reads=[("x1_scr", b * 4 + t)], writes=[("x1b", pb, t)])
                for jg in range(6):
                    ncol = min(512, DFF - jg * 512)
                    wis = []
                    for gu in range(2):
                        wi = r_w.next()
                        cx.dma("sp", wsl[wi][:, :, 0:ncol], WFI_bf[:, :, gu * DFF + jg * 512:gu * DFF + jg * 512 + ncol],
                               reads=[("WS", "fi")], writes=["wslf%d" % wi], skey="wslf%d" % wi)
                        wis.append(wi)
                    for jj in range(ncol // 128):
                        j = jg * 4 + jj
                        fi = r_f.next()
                        for gu in range(2):
                            wi = wis[gu]
                            for k in range(8):
                                cx.op("pe", lambda e, k=k, gu=gu, jj=jj, fi=fi, wi=wi: e.matmul(
                                    FA[fi * 2 + gu][:, :], lhsT=wsl[wi][:, k, jj * 128:(jj + 1) * 128],
                                    rhs=h2T[:, k, :], start=(k == 0), stop=(k == 7)),
                                    reads=["wslf%d" % wi, hk], writes=["FA%d" % (fi * 2 + gu)], sig=(k == 7))
                        si = r_sg.next()
                        cx.op("act", lambda e, si=si, fi=fi: e.activation(out=sgt[si][:], in_=FA[fi * 2][:, :], func=AF.Silu),
                              reads=["FA%d" % (fi * 2)], writes=["sgt%d" % si])
                        cx.op("dve", lambda e, si=si, fi=fi, j=j: e.tensor_tensor(
                            out=aT[:, j, :], in0=FA[fi * 2 + 1][:, :], in1=sgt[si][:], op=ALU.mult),
                            reads=["FA%d" % (fi * 2 + 1), "sgt%d" % si], writes=[("aT", j)])
                for half in range(2):
                    for s_ in range(2):
                        oi = r_fo.next()
                        ok = "wfo%d" % oi
                        cx.dma("sp", wfo[oi][:], WFO_bf[:, s_ * 11:(s_ + 1) * 11, half * 512:(half + 1) * 512],
                               reads=[("WS", "fo")], writes=[ok], skey=ok)
                        for t in range(4):
                            for jj in range(11):
                                j = s_ * 11 + jj
                                cx.op("pe", lambda e, t=t, jj=jj, j=j, oi=oi: e.matmul(
                                    ACC[t][:, :], lhsT=aT[:, j, t * 128:(t + 1) * 128], rhs=wfo[oi][:, jj, :],
                                    start=(j == 0), stop=(j == 21)),
                                    reads=[("aT", j), ok], writes=["ACC%d" % t], sig=(jj == 10))
                    for t in range(4):
                        hs = slice(half * 512, (half + 1) * 512)
                        cx.op("dve", lambda e, t=t, hs=hs: e.tensor_tensor(out=tmpf[:], in0=ACC[t][:, :], in1=G2[:, hs], op=ALU.mult),
                              reads=["ACC%d" % t] + GK[1], writes=["tmpf"])
                        cx.op("dve", lambda e, t=t, hs=hs: e.scalar_tensor_tensor(
                            out=t2[:, t, hs], in0=x1b[:, t, hs], scalar=float(ALPHA), in1=tmpf[:], op0=ALU.mult, op1=ALU.add),
                            reads=[("x1b", pb, t), "tmpf"], writes=[("t2", t, half)])
                for t in range(4):
                    tok = tok0 + t * 128
                    t2k = ("t2", t, 9)
                    cx.op("dve", lambda e, t=t: e.bn_stats(out=stt[:, 0:6], in_=t2[:, t, 0:512]), reads=[("t2", t, 0)], writes=["lnst"])
                    cx.op("dve", lambda e, t=t: e.bn_stats(out=stt[:, 6:12], in_=t2[:, t, 512:1024]), reads=[("t2", t, 1)], writes=["lnst"])
                    cx.op("dve", lambda e: e.bn_aggr(out=mv[:, 0:2], in_=stt[:, 0:12]), reads=["lnst"], writes=["lnmv"])
                    cx.op("dve", lambda e: e.tensor_scalar(out=rstd[:, 0:1], in0=mv[:, 1:2], scalar1=LN_EPS, scalar2=None, op0=ALU.add),
                          reads=["lnmv"], writes=["lnrs"])
                    cx.op("act", lambda e: e.activation(out=rstd[:, 0:1], in_=rstd[:, 0:1], func=AF.Sqrt), reads=["lnrs"], writes=["lnrs"])
                    cx.op("dve", lambda e: e.reciprocal(out=rstd[:, 0:1], in_=rstd[:, 0:1]), reads=["lnrs"], writes=["lnrs"])
                    cx.op("dve", lambda e, t=t: e.tensor_scalar(out=xh[:], in0=t2[:, t, :], scalar1=mv[:, 0:1], scalar2=rstd[:, 0:1],
                                                                op0=ALU.subtract, op1=ALU.mult),
                          reads=[("t2", t, 0), ("t2", t, 1), "lnmv", "lnrs"], writes=["xh"])
                    yi = r_y.next()
                    cx.op("dve", lambda e: e.tensor_tensor(out=xh[:], in0=xh[:], in1=LG2[:], op=ALU.mult),
                          reads=["xh", "LG2"], writes=["xh"])
                    cx.op("dve", lambda e, yi=yi: e.tensor_tensor(out=yo[yi][:], in0=xh[:], in1=LB2[:], op=ALU.add),
                          reads=["xh", "LB2"], writes=["yo%d" % yi])
                    cx.dma("pool", y_p[tok:tok + 128, :], yo[yi][:], reads=["yo%d" % yi], skey="yo%d" % yi)
            cx.barrier()

        cx.finish()
        print("instructions:", cx.nins, "sems:", len(cx.semh))
    return nc


def _prep_inputs(inp, b):
    f = lambda a: np.ascontiguousarray(a, dtype=np.float32)
    m = {}
    m["x"] = f(inp["x_prompt"][b])
    m["xs"] = f(inp["x_sample"][4 * b:4 * b + 4, 0])
    c5 = np.concatenate([inp["c_sample"][4 * b:4 * b + 4], inp["c_prompt"][b:b + 1]], axis=0)
    m["cT"] = f(c5.T.reshape(8, 128, 5).transpose(1, 0, 2))
    ba = np.asarray(inp["b_ada"][0])
    m["bT5"] = f(np.repeat(ba.reshape(48, 128).T[:, :, None], 5, axis=2))
    m["b5"] = f(np.repeat(ba[None, :], 5, axis=0))
    for n in ("w_ada", "w_in", "w_att_out", "w_ret_out", "w_o", "w_ffn_in", "w_ffn_out"):
        m[n] = f(inp[n][0])
    m["lng1"] = f(np.repeat(np.asarray(inp["ln1_g"][0])[None, :], 128, axis=0))
    m["lnb1"] = f(np.repeat(np.asarray(inp["ln1_b"][0])[None, :], 128, axis=0))
    m["lng2"] = f(np.repeat(np.asarray(inp["ln2_g"][0])[None, :], 128, axis=0))
    m["lnb2"] = f(np.repeat(np.asarray(inp["ln2_b"][0])[None, :], 128, axis=0))
    caches = {128: (inp["cache_k_w128"], inp["cache_v_w128"]), 512: (inp["cache_k_w512"], inp["cache_v_w512"]),
              2048: (inp["cache_k_w2048"], inp["cache_v_w2048"])}
    for w in WINS:
        m["ck%d" % w] = f(caches[w][0][0, 4 * b:4 * b + 4].reshape(4, w, 256))
        m["cv%d" % w] = f(caches[w][1][0, 4 * b:4 * b + 4].reshape(4, w, 256))
    m["state"] = f(inp["state_retention"][0, 4 * b:4 * b + 4])
    m["lng1T"] = f(np.asarray(inp["ln1_g"][0]).reshape(8, 128).T)
    m["lnb1T"] = f(np.asarray(inp["ln1_b"][0]).reshape(8, 128).T)
    return m


def kernel(**inp):
    inp = {k: np.asarray(v) for k, v in inp.items()}
    nc = build(DEBUG)
    cs = _consts()
    in_maps = []
    for b in range(8):
        m = _prep_inputs(inp, b)
        for n, v in cs.items():
            m["c_" + n] = v
        in_maps.append(m)
    res = run_bass_kernel_spmd(nc, in_maps, core_ids=list(range(8)))
    R = res.results
    g = lambda n: np.stack([np.asarray(R[b][n], dtype=np.float32) for b in range(8)], axis=0)
    outs = [g("y_p"), g("y_s").reshape(32, 1, D)]
    for w in WINS:
        outs.append(g("pk%d" % w).reshape(1, 8, w, 4, 64))
        outs.append(g("pv%d" % w).reshape(1, 8, w, 4, 64))
    outs.append(g("p_state").reshape(1, 8, 4, 128, 256))
    for w in WINS:
        outs.append(g("sk%d" % w).reshape(1, 32, w, 4, 64))
        outs.append(g("sv%d" % w).reshape(1, 32, w, 4, 64))
    outs.append(g("s_state").reshape(1, 32, 4, 128, 256))
    return tuple(outs)
```
